# Optimizing a Trainium2 kernel written in Bass

```python
import jax, jax.numpy as jnp
from jax import lax
import numpy as np

D_MODEL = 1024
BATCH = 8
SEQ = 2048
DEPTH = 4
DEC_BATCH = 32
DEC_SEQ = 8
PAST_LEN = 8192
PAGE_SIZE = 128

N_EVEN = (DEPTH + 1) // 2
N_ODD = DEPTH // 2
D_FF = 4 * D_MODEL
A_WIDTH = D_MODEL // 2
A_GROUPS = 4
A_CG = A_WIDTH // A_GROUPS
CHUNK = 128
B_WIDTH = D_MODEL // 2
B_HEAD_DIM = 64
B_HEADS = B_WIDTH // B_HEAD_DIM
B_LORA_W = 32
B_LORA_A = 32
B_LORA_G = 64
B_COLS = 3 * B_WIDTH + B_LORA_W + B_LORA_A + B_LORA_G
EVEN_IN = 2 * A_WIDTH + B_COLS
C_HEAD_DIM = 64
C_HEADS = D_MODEL // C_HEAD_DIM
ROT_DIM = C_HEAD_DIM // 4
ROPE_THETA = 500000.0
DILATED = ((128, 1), (512, 4), (2048, 16))
WIN_MAX = 2048
BAND = 128
NORM_EPS = 1e-6
LN_EPS = 1e-5
GN_EPS = 64e-5
NEG = -1e30

kernel_name = 'hybrid_gmlp_rwkv7_dilated_attn_decoder_step'


def rmsnorm(x, g):
    xf = x.astype(jnp.float32)
    return xf * lax.rsqrt(jnp.mean(xf * xf, axis=-1, keepdims=True) + NORM_EPS) * g


def ada_params(c, w, b):
    m = jax.nn.silu(c.astype(jnp.float32)) @ w + b
    return jnp.split(m[:, None, :], 6, axis=-1)


def modulate(x, g, shift, scale):
    return rmsnorm(x, g) * (1.0 + scale) + shift


def sq_relu_ffn(h, w1, w2):
    return jnp.square(jax.nn.relu(h @ w1)) @ w2


def gmlp_spatial_gate(pa, ln_g, ln_b, ws, bs):
    n, t_len, _ = pa.shape
    u = jax.nn.gelu(pa[..., :A_WIDTH], approximate=False)
    v = jax.nn.gelu(pa[..., A_WIDTH:], approximate=False).astype(jnp.float32)
    mu = jnp.mean(v, axis=-1, keepdims=True)
    var = jnp.mean(jnp.square(v - mu), axis=-1, keepdims=True)
    v = (v - mu) * lax.rsqrt(var + LN_EPS) * ln_g + ln_b
    n_chunks = -(-t_len // CHUNK)
    t_pad = n_chunks * CHUNK
    vc = jnp.pad(v, ((0, 0), (0, t_pad - t_len), (0, 0))).reshape(n, n_chunks, CHUNK, A_GROUPS, A_CG)
    w_causal = ws * jnp.tril(jnp.ones((CHUNK, CHUNK), jnp.float32))
    z = jnp.einsum('gij,bnjgc->bnigc', w_causal, vc) + bs.T[None, None, :, :, None]
    z = z.reshape(n, t_pad, A_WIDTH)[:, :t_len]
    return u * z, v


def wkv7_scan(s0, r, w, k, v, a, b):
    def step(s, xs):
        r_t, w_t, k_t, v_t, a_t, b_t = xs
        sa = jnp.einsum('bhvk,bhk->bhv', s, a_t)
        s = s * w_t[:, :, None, :] + sa[..., :, None] * b_t[..., None, :] + v_t[..., :, None] * k_t[..., None, :]
        return s, jnp.einsum('bhvk,bhk->bhv', s, r_t)
    xs = tuple(jnp.swapaxes(t, 0, 1) for t in (r, w, k, v, a, b))
    s, o = lax.scan(step, s0, xs)
    return s, jnp.swapaxes(o, 0, 1)


def rwkv7_time_mix(pb, prev, s0, mu, w0, w2, a0, a2, g2, k_k, k_a, r_k, lnx_g, lnx_b):
    f32 = jnp.float32
    n, t_len, _ = pb.shape
    pb = pb.astype(f32)
    shifted = jnp.concatenate([prev[:, None, :].astype(f32), pb[:, :-1]], axis=1)
    xm = pb + mu * (shifted - pb)
    cuts = [B_WIDTH, 2 * B_WIDTH, 3 * B_WIDTH, 3 * B_WIDTH + B_LORA_W, 3 * B_WIDTH + B_LORA_W + B_LORA_A]
    r, k, v, xw, xa, xg = jnp.split(xm, cuts, axis=-1)
    w_log = -jax.nn.softplus(-(w0 + jnp.tanh(xw) @ w2)) - 0.5
    decay = jnp.exp(-jnp.exp(w_log))
    a = jax.nn.sigmoid(a0 + xa @ a2)
    g = jax.nn.sigmoid(xg) @ g2
    heads = lambda t: t.reshape(n, t_len, B_HEADS, B_HEAD_DIM)
    kk = heads(k * k_k)
    kk = kk * lax.rsqrt(jnp.sum(kk * kk, axis=-1, keepdims=True) + 1e-12)
    k = k * (1.0 + (a - 1.0) * k_a)
    rh, kh, vh, ah = heads(r), heads(k), heads(v), heads(a)
    s, o = wkv7_scan(s0.astype(f32), rh, heads(decay), kh, vh, -kk, kk * ah)
    mo = jnp.mean(o, axis=-1, keepdims=True)
    vo = jnp.mean(jnp.square(o - mo), axis=-1, keepdims=True)
    o = ((o - mo) * lax.rsqrt(vo + GN_EPS)).reshape(n, t_len, B_WIDTH) * lnx_g + lnx_b
    bonus = jnp.sum(rh * kh * r_k, axis=-1, keepdims=True) * vh
    return (o + bonus.reshape(n, t_len, B_WIDTH)) * g, s


def even_mixer(h, s0, prev, w_in, w_out, ln_g, ln_b, ws, bs, mu, w0, w2, a0, a2, g2, k_k, k_a, r_k, lnx_g, lnx_b):
    proj = h @ w_in
    out_a, v_rows = gmlp_spatial_gate(proj[..., :2 * A_WIDTH], ln_g, ln_b, ws, bs)
    pb = proj[..., 2 * A_WIDTH:]
    out_b, s = rwkv7_time_mix(pb, prev, s0, mu, w0, w2, a0, a2, g2, k_k, k_a, r_k, lnx_g, lnx_b)
    y = jnp.concatenate([out_a, out_b], axis=-1) @ w_out
    return y, s, pb[:, -1].astype(jnp.float32), v_rows


def rope_partial(t, pos):
    half = ROT_DIM // 2
    inv = ROPE_THETA ** (-jnp.arange(half, dtype=jnp.float32) * (2.0 / ROT_DIM))
    ang = pos.astype(jnp.float32)[:, None] * inv[None, :]
    cos = jnp.cos(ang)[None, :, None, :]
    sin = jnp.sin(ang)[None, :, None, :]
    t1, t2, rest = t[..., :half], t[..., half:ROT_DIM], t[..., ROT_DIM:]
    return jnp.concatenate([t1 * cos - t2 * sin, t2 * cos + t1 * sin, rest], axis=-1)


def qkv_heads(h, w_qkv, pos):
    n, t_len, _ = h.shape
    q, k, v = jnp.split((h @ w_qkv).astype(jnp.float32), 3, axis=-1)
    q, k, v = (t.reshape(n, t_len, C_HEADS, C_HEAD_DIM) for t in (q, k, v))
    return rope_partial(q, pos), rope_partial(k, pos), v


def dilated_band_attn(q, k, v, window, dil):
    n, s_len, n_h, e = q.shape
    sub_len = s_len // dil
    sub_w = window // dil
    nb = -(-sub_len // BAND)
    l_pad = nb * BAND
    scale = C_HEAD_DIM ** -0.5

    def to_sub(t):
        t = t.reshape(n, sub_len, dil, n_h, e).transpose(0, 2, 1, 3, 4)
        t = jnp.pad(t, ((0, 0), (0, 0), (0, l_pad - sub_len), (0, 0), (0, 0)))
        return t.reshape(n, dil, nb, BAND, n_h, e)

    def band(t):
        prev = jnp.pad(t, ((0, 0), (0, 0), (1, 0), (0, 0), (0, 0), (0, 0)))[:, :, :-1]
        return jnp.concatenate([prev, t], axis=3)

    qb, kb, vb = to_sub(q), to_sub(k), to_sub(v)
    kband, vband = band(kb), band(vb)
    s = jnp.einsum('brnqhe,brnkhe->brnhqk', qb, kband) * scale
    dist = (jnp.arange(BAND)[:, None] + BAND) - jnp.arange(2 * BAND)[None, :]
    key_m = jnp.arange(nb)[:, None] * BAND - BAND + jnp.arange(2 * BAND)[None, :]
    mask = ((dist >= 0) & (dist <= sub_w))[None, :, :] & (key_m >= 0)[:, None, :]
    s = jnp.where(mask[None, None, :, None], s, NEG)
    m = jnp.max(s, axis=-1, keepdims=True)
    p = jnp.exp(s - m)
    den = jnp.sum(p, axis=-1, keepdims=True)
    o = jnp.einsum('brnhqk,brnkhe->brnqhe', p, vband) / jnp.swapaxes(den, 3, 4)
    lse = jnp.swapaxes((m + jnp.log(den))[..., 0], 3, 4)

    def from_sub(t):
        t = t.reshape((n, dil, l_pad) + t.shape[4:])[:, :, :sub_len]
        t = jnp.swapaxes(t, 1, 2)
        return t.reshape((n, s_len) + t.shape[3:])

    return from_sub(o), from_sub(lse)


def dilated_gather_attn(q, k_all, v_all, window, dil):
    n, t_len, n_h, e = q.shape
    wb = k_all.shape[1] - t_len
    j = jnp.arange(window // dil + 1)
    t = jnp.arange(t_len)
    idx = wb + t[:, None] - j[None, :] * dil
    valid = (idx >= 0) & (PAST_LEN + t[:, None] - j[None, :] * dil >= 0)
    idx = jnp.maximum(idx, 0)
    kg = k_all[:, idx]
    vg = v_all[:, idx]
    s = jnp.einsum('bthe,btjhe->bthj', q, kg) * (C_HEAD_DIM ** -0.5)
    s = jnp.where(valid[None, :, None, :], s, NEG)
    m = jnp.max(s, axis=-1, keepdims=True)
    p = jnp.exp(s - m)
    den = jnp.sum(p, axis=-1, keepdims=True)
    o = jnp.einsum('bthj,btjhe->bthe', p, vg) / den
    return o, (m + jnp.log(den))[..., 0]


def mix_dilations(outs, lses):
    wts = jax.nn.softmax(jnp.stack(lses, axis=0), axis=0)
    return jnp.sum(jnp.stack(outs, axis=0) * wts[..., None], axis=0)


def attn_prompt(h, w_qkv, w_out):
    n, s_len, _ = h.shape
    q, k, v = qkv_heads(h, w_qkv, jnp.arange(s_len))
    res = [dilated_band_attn(q, k, v, w, d) for (w, d) in DILATED]
    o = mix_dilations([r[0] for r in res], [r[1] for r in res])
    keep = min(WIN_MAX, s_len)
    return o.reshape(n, s_len, D_MODEL) @ w_out, k[:, s_len - keep:], v[:, s_len - keep:]


def attn_sample(h, ck, cv, w_qkv, w_out):
    n, t_len, _ = h.shape
    q, k, v = qkv_heads(h, w_qkv, PAST_LEN + jnp.arange(t_len))
    k_all = jnp.concatenate([ck.astype(jnp.float32), k], axis=1)
    v_all = jnp.concatenate([cv.astype(jnp.float32), v], axis=1)
    res = [dilated_gather_attn(q, k_all, v_all, w, d) for (w, d) in DILATED]
    o = mix_dilations([r[0] for r in res], [r[1] for r in res])
    return o.reshape(n, t_len, D_MODEL) @ w_out, k, v


def setup_inputs(seed: int = 0) -> dict:
    key = jax.random.key(seed)
    keys = iter(jax.random.split(key, 48))
    f32 = jnp.float32
    D = D_MODEL
    wbuf = min(WIN_MAX, PAST_LEN)
    nrm = lambda shape, std: jax.random.normal(next(keys), shape, f32) * std
    uni = lambda shape, lo, hi: jax.random.uniform(next(keys), shape, f32, lo, hi)
    return {
        'x_prompt': nrm((BATCH, SEQ, D), 1.0),
        'x_sample': nrm((DEC_BATCH, DEC_SEQ, D), 1.0),
        'state_wkv': nrm((N_EVEN, DEC_BATCH, B_HEADS, B_HEAD_DIM, B_HEAD_DIM), 0.1),
        'state_shift': nrm((N_EVEN, DEC_BATCH, B_COLS), 1.0),
        'cache_k': nrm((N_ODD, DEC_BATCH, wbuf, C_HEADS, C_HEAD_DIM), 1.0),
        'cache_v': nrm((N_ODD, DEC_BATCH, wbuf, C_HEADS, C_HEAD_DIM), 1.0),
        'c_prompt': nrm((BATCH, D), 1.0),
        'c_sample': nrm((DEC_BATCH, D), 1.0),
        'ada_w': nrm((DEPTH, D, 6 * D), 0.5 * D ** -0.5),
        'ada_b': nrm((DEPTH, 6 * D), 0.02),
        'norm_mix_pre': 1.0 + nrm((DEPTH, D), 0.05),
        'norm_mix_post': 1.0 + nrm((DEPTH, D), 0.05),
        'norm_ffn_pre': 1.0 + nrm((DEPTH, D), 0.05),
        'norm_ffn_post': 1.0 + nrm((DEPTH, D), 0.05),
        'ffn_w1': nrm((DEPTH, D, D_FF), D ** -0.5),
        'ffn_w2': nrm((DEPTH, D_FF, D), D_FF ** -0.5),
        'ev_w_in': nrm((N_EVEN, D, EVEN_IN), D ** -0.5),
        'ev_w_out': nrm((N_EVEN, A_WIDTH + B_WIDTH, D), (A_WIDTH + B_WIDTH) ** -0.5),
        'gm_ln_g': 1.0 + nrm((N_EVEN, A_WIDTH), 0.05),
        'gm_ln_b': nrm((N_EVEN, A_WIDTH), 0.02),
        'gm_ws': nrm((N_EVEN, A_GROUPS, CHUNK, CHUNK), CHUNK ** -0.5),
        'gm_bs': 1.0 + nrm((N_EVEN, A_GROUPS, CHUNK), 0.05),
        'rw_mu': uni((N_EVEN, B_COLS), 0.0, 1.0),
        'rw_w0': uni((N_EVEN, B_WIDTH), -3.0, 1.0),
        'rw_w2': nrm((N_EVEN, B_LORA_W, B_WIDTH), B_LORA_W ** -0.5),
        'rw_a0': nrm((N_EVEN, B_WIDTH), 0.5),
        'rw_a2': nrm((N_EVEN, B_LORA_A, B_WIDTH), B_LORA_A ** -0.5),
        'rw_g2': nrm((N_EVEN, B_LORA_G, B_WIDTH), B_LORA_G ** -0.5),
        'rw_kk': 0.85 + nrm((N_EVEN, B_WIDTH), 0.05),
        'rw_ka': 1.0 + nrm((N_EVEN, B_WIDTH), 0.05),
        'rw_rk': nrm((N_EVEN, B_HEADS, B_HEAD_DIM), 0.1),
        'rw_lnx_g': 1.0 + nrm((N_EVEN, B_WIDTH), 0.05),
        'rw_lnx_b': nrm((N_EVEN, B_WIDTH), 0.02),
        'od_w_qkv': nrm((N_ODD, D, 3 * D), D ** -0.5),
        'od_w_out': nrm((N_ODD, D, D), D ** -0.5),
    }


def reference(x_prompt, x_sample, state_wkv, state_shift, cache_k, cache_v, c_prompt, c_sample,
              ada_w, ada_b, norm_mix_pre, norm_mix_post, norm_ffn_pre, norm_ffn_post, ffn_w1, ffn_w2,
              ev_w_in, ev_w_out, gm_ln_g, gm_ln_b, gm_ws, gm_bs, rw_mu, rw_w0, rw_w2, rw_a0, rw_a2,
              rw_g2, rw_kk, rw_ka, rw_rk, rw_lnx_g, rw_lnx_b, od_w_qkv, od_w_out):
    f32 = jnp.float32
    xp = x_prompt.astype(f32)
    xs = x_sample.astype(f32)
    n_p = x_prompt.shape[0]
    wkv_p, shift_p, k_p, v_p = [], [], [], []
    wkv_s, shift_s, k_s, v_s, gv_s = [], [], [], [], []
    for l in range(DEPTH):
        shm_p, scm_p, gtm_p, shf_p, scf_p, gtf_p = ada_params(c_prompt, ada_w[l], ada_b[l])
        shm_s, scm_s, gtm_s, shf_s, scf_s, gtf_s = ada_params(c_sample, ada_w[l], ada_b[l])
        hp = modulate(xp, norm_mix_pre[l], shm_p, scm_p)
        hs = modulate(xs, norm_mix_pre[l], shm_s, scm_s)
        if l % 2 == 0:
            e = l // 2
            ew = (ev_w_in[e], ev_w_out[e], gm_ln_g[e], gm_ln_b[e], gm_ws[e], gm_bs[e], rw_mu[e],
                  rw_w0[e], rw_w2[e], rw_a0[e], rw_a2[e], rw_g2[e], rw_kk[e], rw_ka[e], rw_rk[e],
                  rw_lnx_g[e], rw_lnx_b[e])
            s0 = jnp.zeros((n_p, B_HEADS, B_HEAD_DIM, B_HEAD_DIM), f32)
            prev0 = jnp.zeros((n_p, B_COLS), f32)
            yp, sp, lastp, _ = even_mixer(hp, s0, prev0, *ew)
            ys, ss, lasts, gvs = even_mixer(hs, state_wkv[e], state_shift[e], *ew)
            wkv_p.append(sp)
            shift_p.append(lastp)
            wkv_s.append(ss)
            shift_s.append(lasts)
            gv_s.append(gvs)
        else:
            o = l // 2
            yp, kp, vp = attn_prompt(hp, od_w_qkv[o], od_w_out[o])
            ys, kss, vss = attn_sample(hs, cache_k[o], cache_v[o], od_w_qkv[o], od_w_out[o])
            k_p.append(kp)
            v_p.append(vp)
            k_s.append(kss)
            v_s.append(vss)
        xp = xp + gtm_p * rmsnorm(yp, norm_mix_post[l])
        xs = xs + gtm_s * rmsnorm(ys, norm_mix_post[l])
        fp = modulate(xp, norm_ffn_pre[l], shf_p, scf_p)
        fs = modulate(xs, norm_ffn_pre[l], shf_s, scf_s)
        xp = xp + gtf_p * rmsnorm(sq_relu_ffn(fp, ffn_w1[l], ffn_w2[l]), norm_ffn_post[l])
        xs = xs + gtf_s * rmsnorm(sq_relu_ffn(fs, ffn_w1[l], ffn_w2[l]), norm_ffn_post[l])
    return (xp, xs, jnp.stack(wkv_p), jnp.stack(shift_p), jnp.stack(k_p), jnp.stack(v_p),
            jnp.stack(wkv_s), jnp.stack(shift_s), jnp.stack(k_s), jnp.stack(v_s), jnp.stack(gv_s))
```

```python
import numpy as np
import contextlib, os
import concourse.bass as bass
import concourse.mybir as mybir

F32 = mybir.dt.float32
BF16 = mybir.dt.bfloat16
I32 = mybir.dt.int32
AF = mybir.ActivationFunctionType
ALU = mybir.AluOpType
AX = mybir.AxisListType

ENGS = ("pe", "act", "dve", "pool", "sp")
NDMA_SEM = 12


class Op:
    __slots__ = ("eng", "fn", "deps", "is_dma", "sig", "token", "idx")

    def __init__(self, eng, fn, is_dma):
        self.eng = eng
        self.fn = fn
        self.deps = []
        self.is_dma = is_dma
        self.sig = False
        self.token = None
        self.idx = -1


class Prog:
    def __init__(self, nc):
        self.nc = nc
        self.ops = {e: [] for e in ENGS}
        self.res = {}
        self.nops = 0

    def op(self, eng, fn, reads=(), writes=(), dma=False):
        o = Op(eng, fn, dma)
        o.idx = len(self.ops[eng])
        deps = {}
        for r in reads:
            st = self.res.get(r)
            if st is not None and st[0] is not None:
                deps[id(st[0])] = st[0]
        for w in writes:
            st = self.res.get(w)
            if st is not None:
                if st[0] is not None:
                    deps[id(st[0])] = st[0]
                for rd in st[1]:
                    deps[id(rd)] = rd
        for d in deps.values():
            if d is o:
                continue
            if (not d.is_dma) and d.eng == "pe" and eng == "pe" and not dma:
                continue
            d.sig = True
            o.deps.append(d)
        for r in reads:
            st = self.res.setdefault(r, [None, []])
            st[1].append(o)
        for w in writes:
            self.res[w] = [o, []]
        self.ops[eng].append(o)
        self.nops += 1
        return o

    def mm(self, out, lhsT, rhs, start=True, stop=True, reads=(), writes=(), **kw):
        return self.op("pe", lambda e: e.matmul(out, lhsT, rhs, start=start, stop=stop, **kw), reads, writes)

    def tr(self, out, in_, ident, reads=(), writes=()):
        return self.op("pe", lambda e: e.transpose(out, in_, ident), reads, writes)

    def dma(self, eng, out, in_, reads=(), writes=(), **kw):
        return self.op(eng, lambda e: e.dma_start(out=out, in_=in_, **kw), reads, writes, dma=True)

    def emit(self, final_wait=True):
        nc = self.nc
        import contextlib
        with contextlib.ExitStack() as es:
            sems = {e: es.enter_context(nc.semaphore("s_" + e)) for e in ENGS}
            dsems = {}
            for q in ("sp", "pool", "act"):
                dsems[q] = [es.enter_context(nc.semaphore(f"d_{q}{k}")) for k in range(NDMA_SEM)]
            for e in ENGS:
                cnt = 0
                dcnt = 0
                duse = [0] * NDMA_SEM
                dlast = [None] * NDMA_SEM
                for o in self.ops[e]:
                    if o.is_dma:
                        k = dcnt % NDMA_SEM
                        dcnt += 1
                        duse[k] += 1
                        if dlast[k] is not None:
                            o.deps.append(dlast[k])
                        dlast[k] = o
                        o.token = (dsems[e][k], 16 * duse[k])
                        o.sig = True
                    elif o.sig:
                        cnt += 1
                        o.token = (sems[e], cnt)
                self_last_dma = dlast
                setattr(self, "_dlast_" + e, [d for d in dlast if d is not None])
            blk = es.enter_context(nc.Block())

            def run(engname):
                def body(eng):
                    known = {}
                    for o in self.ops[engname]:
                        for d in o.deps:
                            s, v = d.token
                            key = id(s)
                            if known.get(key, 0) < v:
                                eng.wait_ge(s, v)
                                known[key] = v
                        ins = o.fn(eng)
                        if o.sig:
                            s, v = o.token
                            ins.then_inc(s, 16 if o.is_dma else 1)
                    if final_wait:
                        for d in getattr(self, "_dlast_" + engname):
                            s, v = d.token
                            if known.get(id(s), 0) < v:
                                eng.wait_ge(s, v)
                                known[id(s)] = v
                return body

            blk.tensor(run("pe"))
            blk.scalar(run("act"))
            blk.vector(run("dve"))
            blk.gpsimd(run("pool"))
            blk.sync(run("sp"))

from concourse.bass_utils import run_bass_kernel_spmd
EVSTOP = float(os.environ.get('EVSTOP', '99'))
EVGROUPS = os.environ.get('EVGROUPS', '')
CP = 32

D = 1024
T = 2048
TS = 32
TT = T + TS
NSEQ = 5
DFF = 4096
EVEN_IN = 2688
BCOLS = 1664
NORM_EPS = 1e-6
SEGS = [(0, T, 0)] + [(T + 8 * s, 8, 1 + s) for s in range(4)]
TGROUPS = [(0, 512), (512, 512), (1024, 512), (1536, 512), (2048, 32)]


class Arena:
    def __init__(self, ap_f32, width):
        self.ap = ap_f32
        self.width = width
        self.off = 0

    def reset(self, off=0):
        self.off = off

    def alloc(self, shape, dt):
        n = int(np.prod(shape[1:]))
        nw = n if dt == F32 else (n + 1) // 2
        assert self.off + nw <= self.width, ("arena overflow", self.off, nw, self.width)
        a = self.ap[0:shape[0], self.off:self.off + nw]
        self.off += nw
        if dt != F32:
            a = a.bitcast(dt)[:, 0:n]
        if len(shape) == 3:
            a = a.rearrange("p (a b) -> p a b", a=shape[1])
        elif len(shape) == 4:
            a = a.rearrange("p (a b c) -> p a b c", a=shape[1], b=shape[2])
        return a


def build(nlayers=4, mix_even=True, mix_odd=True, dbg=False):
    nc = bass.Bass("TRN2", target_bir_lowering=False)
    din = lambda name, shape: nc.dram_tensor(name, shape, F32, kind="ExternalInput").ap()
    dout = lambda name, shape: nc.dram_tensor(name, shape, F32, kind="ExternalOutput").ap()
    xp = din("xp", [T, D]); xs = din("xs", [TS, D])
    swkv = din("swkv", [2, 4, 8, 64, 64]); sshift = din("sshift", [2, 4, BCOLS])
    ck = din("ck", [2, 4, 2048, D]); cv = din("cv", [2, 4, 2048, D])
    cc = din("cc", [NSEQ, D])
    ada_w = din("ada_w", [4, D, 6 * D]); ada_b = din("ada_b", [4, 6 * D])
    npar_d = [din(n, [4, D]) for n in ("norm_mix_pre", "norm_mix_post", "norm_ffn_pre", "norm_ffn_post")]
    w1 = din("ffn_w1", [4, D, DFF]); w2 = din("ffn_w2", [4, DFF, D])
    ev_w_in = din("ev_w_in", [2, D, EVEN_IN]); ev_w_out = din("ev_w_out", [2, D, D])
    gm_ln_g = din("gm_ln_g", [2, 512]); gm_ln_b = din("gm_ln_b", [2, 512])
    gm_ws = din("gm_ws", [2, 4, 128, 128]); gm_bs = din("gm_bs", [2, 4, 128])
    rw_mu = din("rw_mu", [2, BCOLS]); rw_w0 = din("rw_w0", [2, 512]); rw_w2 = din("rw_w2", [2, 32, 512])
    rw_a0 = din("rw_a0", [2, 512]); rw_a2 = din("rw_a2", [2, 32, 512]); rw_g2 = din("rw_g2", [2, 64, 512])
    rw_kk = din("rw_kk", [2, 512]); rw_ka = din("rw_ka", [2, 512]); rw_rk = din("rw_rk", [2, 512])
    rw_lnx_g = din("rw_lnx_g", [2, 512]); rw_lnx_b = din("rw_lnx_b", [2, 512])
    od_w_qkv = din("od_w_qkv", [2, D, 3 * D]); od_w_out = din("od_w_out", [2, D, D])
    kc_rope = din("kc_rope", [2, 128, TT]); kc_perm = din("kc_perm", [128, 128]); kc_cmask = din("kc_cmask", [128, 384])
    kc_ev = din("kc_ev", [128, 1664]); kc_idr = din("kc_idr", [64, 512]); kc_m8 = din("kc_m8", [8, 256])
    kc_m3 = din("kc_m3", [128, 2048]); kc_smask = din("kc_smask", [128, 256]); kc_nmask = din("kc_nmask", [8, 16])
    y_p = dout("y_p", [T, D]); y_s = dout("y_s", [TS, D])
    wkv_p = dout("wkv_p", [2, 8, 64, 64]); shift_p = dout("shift_p", [2, BCOLS])
    k_p = dout("k_p", [2, T, D]); v_p = dout("v_p", [2, T, D])
    wkv_s = dout("wkv_s", [2, 4, 8, 64, 64]); shift_s = dout("shift_s", [2, 4, BCOLS])
    k_s = dout("k_s", [2, TS, D]); v_s = dout("v_s", [2, TS, D])
    gv_s = dout("gv_s", [2, TS, 512])
    dbg_o = dout("dbg", [128, 8 * TT]) if dbg else None

    P = Prog(nc)
    with contextlib.ExitStack() as es:
        sb = lambda name, shape, dt: es.enter_context(nc.sbuf_tensor(name, shape, dt))
        xT = sb("xT", [128, 8, TT], F32)
        hT = sb("hT", [128, 8, TT], BF16)
        rstd_h = [None]
        identF = sb("identF", [128, 128], F32)
        identB = sb("identB", [128, 128], BF16)
        onesB = sb("onesB", [128, 128], BF16)
        zerB = sb("zerB", [128, 512], BF16)
        siluT = sb("siluT", [128, 8, NSEQ], BF16)
        npar = sb("npar", [128, 4, 32], F32)
        adab = sb("adab", [128, 192], F32)
        mod = sb("mod", [128, 48, NSEQ], F32)
        mods = sb("mods", [128, 6, 8, NSEQ], F32)
        epsb = sb("epsb", [128, 1], F32)
        epsb2 = sb("epsb2", [128, 1], F32)
        permT = sb("permT", [128, 128], BF16)
        cmask = sb("cmask", [128, 384], BF16)
        smask = sb("smask", [128, 256], BF16)
        nmask = sb("nmask", [8, 16], BF16)
        RING = 3
        wring = sb("wring", [128, RING, 4096], BF16)
        AW = 20300
        arena_t = sb("arena", [128, AW], F32)
        arena = Arena(arena_t, AW)
        psb = [es.enter_context(nc.psum_tensor(f"ps{i}", [128, 512], F32)) for i in range(8)]
        pscnt = [0]

        def nextps():
            i = pscnt[0] % 6
            pscnt[0] += 1
            return i

        acccnt = [0]

        def accps():
            i = 6 + acccnt[0] % 2
            acccnt[0] += 1
            return i

        wstate = {"n": 0}

        def wload(src3, shape):
            slot = wstate["n"] % RING
            wstate["n"] += 1
            a, b = shape
            dst = wring[:, slot, 0:a * b].rearrange("p (a b) -> p a b", a=a)
            if isinstance(src3, list):
                nt = len(src3)
                dv = wring[:, slot, 0:a * b].rearrange("p (a t f) -> p a t f", a=a, t=nt)
                for ti, sx in enumerate(src3):
                    P.dma("pool", dv[:, :, ti, :], sx, writes=[f"wr{slot}"])
            else:
                P.dma("pool", dst, src3, writes=[f"wr{slot}"])
            return dst, f"wr{slot}"

        class WStream:
            def __init__(self, pieces, depth=RING - 1):
                self.pieces = pieces
                self.loaded = []
                self.i = 0
                self.depth = depth

            def get(self):
                while len(self.loaded) < min(len(self.pieces), self.i + self.depth):
                    s, sh = self.pieces[len(self.loaded)]
                    self.loaded.append(wload(s, sh))
                r = self.loaded[self.i]
                self.i += 1
                return r

        pend_dma = []
        bar_t = sb("bar_t", [128, 8], F32)
        barcnt = [0]

        def dma(eng, out, in_, reads=(), writes=(), **kw):
            o = P.dma(eng, out, in_, reads=reads, writes=writes, **kw)
            return o

        def barrier():
            n = barcnt[0]
            barcnt[0] += 1
            allres = list(P.res.keys())
            P.op("dve", lambda e: e.memset(bar_t[:, 0:1], 0.0), reads=[], writes=allres + ["bar"])
            P.op("act", lambda e: e.memzero(bar_t[:, 1:2]), reads=["bar"], writes=["bar_act"])
            P.op("pool", lambda e: e.memset(bar_t[:, 2:3], 0.0), reads=["bar"], writes=["bar_pool"])
            P.mm(psb[7][0:1, 0:1], zerB[0:1, 0:1], zerB[0:1, 0:1], reads=["bar", "zerB"], writes=["bar_pe", "ps7"])
            P.op("sp", lambda e: e.nop(), reads=["bar"], writes=["bar_sp"])
            P.op("dve", lambda e: e.memset(bar_t[:, 3:4], 0.0), reads=["bar_act", "bar_pool", "bar_pe", "bar_sp"], writes=["bar2"])
            P.op("act", lambda e: e.memzero(bar_t[:, 4:5]), reads=["bar2"], writes=["bar3_act"])
            P.op("pool", lambda e: e.memset(bar_t[:, 5:6], 0.0), reads=["bar2"], writes=["bar3_pool"])
            P.mm(psb[7][0:1, 0:1], zerB[0:1, 0:1], zerB[0:1, 0:1], reads=["bar2", "zerB"], writes=["bar3_pe", "ps7"])
            P.op("sp", lambda e: e.nop(), reads=["bar2"], writes=["bar3_sp"])

        def phase(off=0):
            barrier()
            arena.reset(off)
            lrt_keep[0] = None

        lrt_keep = [None]
        P.op("pool", lambda e: e.memset(identF[:], 0.0), writes=["identF"])
        P.op("pool", lambda e: e.affine_select(out=identF[:], in_=identF[:], pattern=[[-1, 128]], compare_op=ALU.not_equal, fill=1.0, base=0, channel_multiplier=1), reads=["identF"], writes=["identF"])
        P.op("pool", lambda e: e.tensor_copy(identB[:], identF[:]), reads=["identF"], writes=["identB"])
        P.op("pool", lambda e: e.memset(onesB[:], 1.0), writes=["onesB"])
        P.op("pool", lambda e: e.memset(zerB[:], 0.0), writes=["zerB"])
        P.op("pool", lambda e: e.memset(epsb[:], NORM_EPS), writes=["epsb"])
        P.op("pool", lambda e: e.memset(epsb2[:], 1e-12), writes=["epsb"])
        dma("pool", permT[:], kc_perm, writes=["permT"])
        dma("pool", cmask[:], kc_cmask, writes=["cmask"])
        dma("pool", smask[:], kc_smask, writes=["smask"])
        dma("pool", nmask[:], kc_nmask, writes=["nmask"])

        def load_rows_T(src_rows, R, dst, dstres):
            if lrt_keep[0] is None:
                lrt_keep[0] = arena.alloc([128, 128], F32)
            st = lrt_keep[0]
            dma("sp", st[0:R, :], src_rows, writes=["lrt_st"])
            b = nextps()
            P.tr(psb[b][:, 0:R], st[0:R, :], identF[0:R, 0:R], reads=["lrt_st", "identF"], writes=[f"ps{b}"])
            P.op("dve", lambda e: e.tensor_copy(dst, psb[b][:, 0:R]), writes=[dstres, f"ps{b}"])

        for kind in range(4):
            load_rows_T(npar_d[kind].rearrange("l (c p) -> (l c) p", p=128), 32, npar[:, kind, :], "npar")
        ab = ada_b.rearrange("l (c p) -> (l c) p", p=128)
        load_rows_T(ab[0:96], 96, adab[:, 0:96], "adab")
        load_rows_T(ab[96:192], 96, adab[:, 96:192], "adab")
        cT = arena.alloc([128, 8, NSEQ], F32)
        st5 = arena.alloc([128, D], F32)
        dma("sp", st5[0:NSEQ, :], cc, writes=["st5"])
        for c in range(8):
            b = nextps()
            P.tr(psb[b][:, 0:NSEQ], st5[0:NSEQ, c * 128:(c + 1) * 128], identF[0:NSEQ, 0:NSEQ], reads=["st5", "identF"], writes=[f"ps{b}"])
            P.op("act", lambda e, b=b, c=c: e.activation(out=siluT[:, c, :], in_=psb[b][:, 0:NSEQ], func=AF.Silu), writes=["siluT", f"ps{b}"])
        barrier()
        arena.reset()

        def load_x(src, nrows, col0):
            xin2 = arena.alloc([128, 2, D], F32)
            ntile = (nrows + 127) // 128
            for tt in range(ntile):
                r = min(128, nrows - tt * 128)
                xin = xin2[:, tt % 2, :]
                xres = f"xin{tt % 2}"
                dma("sp", xin[0:r, :], src[tt * 128:tt * 128 + r, :], writes=[xres])
                for hb in range(2):
                    b = nextps()
                    for q in range(4):
                        c = hb * 4 + q
                        P.tr(psb[b][:, q * 128:q * 128 + r], xin[0:r, c * 128:(c + 1) * 128], identF[0:r, 0:r], reads=[xres, "identF"], writes=[f"ps{b}"])
                    src_ps = psb[b][:, :].rearrange("p (q t) -> p q t", q=4)[:, :, 0:r]
                    dst = xT[:, hb * 4:hb * 4 + 4, col0 + tt * 128:col0 + tt * 128 + r]
                    P.op("dve" if hb == 0 else "act",
                         (lambda e, dst=dst, src_ps=src_ps: e.tensor_copy(dst, src_ps)) if hb == 0 else
                         (lambda e, dst=dst, src_ps=src_ps: e.activation(out=dst, in_=src_ps, func=AF.Copy)),
                         writes=["xT", f"ps{b}"])
            barrier()
            arena.reset()

        load_x(xp, T, 0)
        load_x(xs, TS, T)

        def norm_stats(src, src_res, cols_list, col_off=0):
            ncols_tot = sum(n for _, n in cols_list)
            rstd_h[0] = (arena.alloc([128, ncols_tot], F32), col_off)
            rstd = rstd_h[0][0]
            sq = arena.alloc([128, 8, 512], BF16)
            for (c0, n) in cols_list:
                for c in range(8):
                    P.op("act", lambda e, c=c, c0=c0, n=n: e.activation(out=sq[:, c, 0:n], in_=src[:, c, c0 - col_off:c0 - col_off + n], func=AF.Square), reads=[src_res], writes=["sq"])
                b = nextps()
                for c in range(8):
                    P.mm(psb[b][:, 0:n], onesB[:, :], sq[:, c, 0:n], start=(c == 0), stop=(c == 7), reads=["sq", "onesB"], writes=[f"ps{b}"])
                P.op("act", lambda e, b=b, c0=c0, n=n: e.activation(out=rstd[:, c0 - col_off:c0 - col_off + n], in_=psb[b][:, 0:n], func=AF.Sqrt, scale=1.0 / D, bias=epsb[:, 0:1]), reads=["epsb"], writes=["rstd", f"ps{b}"])
                P.op("dve", lambda e, c0=c0, n=n: e.reciprocal(rstd[:, c0 - col_off:c0 - col_off + n], rstd[:, c0 - col_off:c0 - col_off + n]), reads=["rstd"], writes=["rstd"])

        def modulate(ia, ib):
            rstd, ro = rstd_h[0]
            assert ro == 0
            tmp = arena.alloc([128, T], F32)
            for c in range(8):
                for (c0, n, s) in SEGS:
                    P.op("dve", lambda e, c=c, c0=c0, n=n, s=s: e.scalar_tensor_tensor(out=tmp[:, 0:n], in0=xT[:, c, c0:c0 + n], scalar=mods[:, ia, c, s:s + 1], in1=rstd[:, c0:c0 + n], op0=ALU.mult, op1=ALU.mult), reads=["xT", "mods", "rstd"], writes=["tmp"])
                    P.op("act", lambda e, c=c, c0=c0, n=n, s=s: e.activation(out=hT[:, c, c0:c0 + n], in_=tmp[:, 0:n], func=AF.Identity, bias=mods[:, ib, c, s:s + 1]), reads=["tmp", "mods"], writes=["hT"])

        def residual(ysrc, yres, ig, segs, col_off=0):
            rstd, ro = rstd_h[0]
            assert ro == col_off
            tmp = arena.alloc([128, max(n for _, n, _ in segs)], F32)
            for c in range(8):
                for (c0, n, s) in segs:
                    P.op("dve", lambda e, c=c, c0=c0, n=n, s=s: e.scalar_tensor_tensor(out=tmp[:, 0:n], in0=ysrc[:, c, c0 - col_off:c0 - col_off + n], scalar=mods[:, ig, c, s:s + 1], in1=rstd[:, c0 - col_off:c0 - col_off + n], op0=ALU.mult, op1=ALU.mult), reads=[yres, "mods", "rstd"], writes=["tmp"])
                    P.op("pool", lambda e, c=c, c0=c0, n=n: e.tensor_tensor(out=xT[:, c, c0:c0 + n], in0=xT[:, c, c0:c0 + n], in1=tmp[:, 0:n], op=ALU.add), reads=["tmp", "xT"], writes=["xT"])

        def out_proj_and_residual(oT, wsrc, tag):
            pieces = [(wsrc.rearrange("(kc p) f -> p kc f", p=128)[:, :, pc * 512:(pc + 1) * 512], (8, 512)) for pc in range(2)]
            ws = WStream(pieces, depth=2)
            wo = [ws.get(), ws.get()]
            ygrp = arena.alloc([128, 8, 512], F32)
            for (c0, n) in TGROUPS:
                for dc in range(8):
                    wsl, wres = wo[dc // 4]
                    b = nextps()
                    for kc in range(8):
                        P.mm(psb[b][:, 0:n], wsl[:, kc, (dc % 4) * 128:(dc % 4 + 1) * 128], oT[:, kc, c0:c0 + n], start=(kc == 0), stop=(kc == 7), reads=[wres, tag], writes=[f"ps{b}"])
                    if dc % 2 == 0:
                        P.op("act", lambda e, b=b, n=n, dc=dc: e.activation(out=ygrp[:, dc, 0:n], in_=psb[b][:, 0:n], func=AF.Copy), writes=["ygrp", f"ps{b}"])
                    else:
                        P.op("dve", lambda e, b=b, n=n, dc=dc: e.tensor_copy(ygrp[:, dc, 0:n], psb[b][:, 0:n]), writes=["ygrp", f"ps{b}"])
                save = arena.off
                norm_stats(ygrp, "ygrp", [(c0, n)], col_off=c0)
                if c0 < T:
                    segs = [(c0, n, 0)]
                else:
                    segs = SEGS[1:]
                residual(ygrp, "ygrp", 2, segs, col_off=c0)
                arena.off = save

        def odd_mixer(o_):
            phase()
            norm_stats(xT, "xT", TGROUPS)
            modulate(0, 1)
            phase()
            oT = arena.alloc([128, 8, TT], BF16)
            keep_off = arena.off
            Ctab = arena.alloc([128, TT], BF16)
            Stab = arena.alloc([128, TT], BF16)
            m3 = arena.alloc([128, 4, 512], BF16)
            dma("pool", Ctab, kc_rope[0], writes=["Ctab"])
            dma("pool", Stab, kc_rope[1], writes=["Stab"])
            dma("pool", m3, kc_m3.rearrange("p (g c) -> p g c", g=4), writes=["m3"])
            QK = arena.alloc([128, 2, TT], BF16)
            qraw = arena.alloc([128, 2, 512], BF16)
            rt1 = arena.alloc([128, 512], BF16)
            rt2 = arena.alloc([128, 512], BF16)
            off_v3 = arena.off
            V3 = arena.alloc([128, 3, 16, 2 * 65], BF16)
            off_end = arena.off
            arena.off = off_v3
            Kc = arena.alloc([128, 16, 128], BF16)
            Vc = arena.alloc([128, 16, 2 * 65], BF16)
            KcT = arena.alloc([128, 2048], BF16)
            assert arena.off <= off_end
            arena.off = off_end
            kst = arena.alloc([128, 4, 128], F32)
            vst = arena.alloc([128, 4, 128], F32)
            PT = arena.alloc([128, 2, 512], BF16)
            rec = arena.alloc([128, 512], BF16)
            bcs = rec
            Vs = arena.alloc([128, 4, 2 * 65], BF16)
            vsst = vst
            PTs = arena.alloc([128, 256], BF16)
            PTn = arena.alloc([128, 16], BF16)
            ksst = kst[:, 0, :]
            V3v = V3.rearrange("p a t (h e) -> p (a t h) e", e=65)
            Vcv = Vc.rearrange("p t (h e) -> p (t h) e", e=65)
            Vsv = Vs.rearrange("p s (h e) -> p (s h) e", e=65)
            P.op("pool", lambda e: e.memset(Vsv[:, :, 64:65], 1.0), writes=["Vs"])
            wq = od_w_qkv[o_].rearrange("(kc p) (t f) -> p kc t f", p=128, t=3)
            pieces = [([wq[:, :, t3, 128 * j:128 * (j + 1)] for t3 in range(3)], (8, 384)) for j in range(8)]
            ws = WStream(pieces, depth=2)
            ptc = [0]
            for j in range(8):
                wsl_, wres = ws.get()
                wsl = wsl_.rearrange("p kc (t f) -> p kc t f", t=3)
                for which in range(2):
                    for gi, (c0, n) in enumerate(TGROUPS):
                        b = nextps()
                        for kc in range(8):
                            P.mm(psb[b][:, 0:n], wsl[:, kc, which, :], hT[:, kc, c0:c0 + n], start=(kc == 0), stop=(kc == 7), reads=[wres, "hT"], writes=[f"ps{b}"])
                        qi = ptc[0] % 2
                        ptc[0] += 1
                        P.op("act", lambda e, b=b, n=n, qi=qi: e.activation(out=qraw[:, qi, 0:n], in_=psb[b][:, 0:n], func=AF.Copy), writes=[f"qraw{qi}", f"ps{b}"])
                        b2 = nextps()
                        P.mm(psb[b2][:, 0:n], permT[:, :], qraw[:, qi, 0:n], reads=["permT", f"qraw{qi}"], writes=[f"ps{b2}"])
                        P.op("dve", lambda e, n=n, qi=qi, c0=c0: e.tensor_tensor(out=rt1[:, 0:n], in0=qraw[:, qi, 0:n], in1=Ctab[:, c0:c0 + n], op=ALU.mult), reads=[f"qraw{qi}", "Ctab"], writes=["rt1"])
                        P.op("dve", lambda e, n=n, b2=b2, c0=c0: e.tensor_tensor(out=rt2[:, 0:n], in0=psb[b2][:, 0:n], in1=Stab[:, c0:c0 + n], op=ALU.mult), reads=["Stab"], writes=["rt2", f"ps{b2}"])
                        P.op("pool", lambda e, n=n, c0=c0, which=which: e.tensor_tensor(out=QK[:, which, c0:c0 + n], in0=rt1[:, 0:n], in1=rt2[:, 0:n], op=ALU.add), reads=["rt1", "rt2"], writes=[f"QK{which}"])
                QT = QK[:, 0, :]
                KT = QK[:, 1, :]
                for t4 in range(4):
                    b = nextps()
                    pbf = psb[b][:, :].bitcast(BF16)
                    for q in range(4):
                        tt = t4 * 4 + q
                        P.tr(pbf[:, q * 128:(q + 1) * 128], KT[:, tt * 128:(tt + 1) * 128], identB[:, :], reads=["QK1", "identB"], writes=[f"ps{b}"])
                    P.op("act", lambda e, pbf=pbf: e.activation(out=kst[:, :, :], in_=pbf[:, 0:512].rearrange("p (q c) -> p q c", q=4), func=AF.Copy), writes=["kst", f"ps{b}"])
                    dma("sp", k_p[o_, t4 * 512:(t4 + 1) * 512, 128 * j:128 * (j + 1)].rearrange("(q p) c -> p q c", p=128), kst[:, :, :], reads=["kst"])
                b = nextps()
                pbf = psb[b][:, :].bitcast(BF16)
                P.tr(pbf[0:32, 0:128], KT[:, T:TT], identB[:, :], reads=["QK1", "identB"], writes=[f"ps{b}"])
                P.op("act", lambda e, pbf=pbf: e.activation(out=ksst[0:32, :], in_=pbf[0:32, 0:128], func=AF.Copy), writes=["kst", f"ps{b}"])
                dma("sp", k_s[o_, :, 128 * j:128 * (j + 1)], ksst[0:32, :], reads=["kst"])
                P.op("pool", lambda e: e.memset(V3v[:, :, 64:65], 1.0), writes=["V3"])
                for br, dil in enumerate((1, 4, 16)):
                    for t4 in range(4):
                        b = nextps()
                        for q in range(4):
                            ti = t4 * 4 + q
                            if br == 0:
                                start = 128 * ti
                            elif br == 1:
                                r, nblk = ti // 4, ti % 4
                                start = 512 * nblk + r
                            else:
                                start = ti
                            tok = hT[:, :, start:start + 128 * dil:dil] if dil > 1 else hT[:, :, start:start + 128]
                            for kc in range(8):
                                P.mm(psb[b][:, q * 128:(q + 1) * 128], tok[:, kc, :], wsl[:, kc, 2, :], start=(kc == 0), stop=(kc == 7), reads=[wres, "hT"], writes=[f"ps{b}"])
                        dstv = V3[:, br, t4 * 4:(t4 + 1) * 4, :].rearrange("p t (h e) -> p t h e", e=65)[:, :, :, 0:64]
                        srcv = psb[b][:, :].rearrange("p (t h e) -> p t h e", t=4, h=2)
                        P.op("act", lambda e, dstv=dstv, srcv=srcv: e.activation(out=dstv, in_=srcv, func=AF.Copy), writes=["V3", f"ps{b}"])
                        if br == 0:
                            P.op("dve", lambda e, b=b: e.tensor_copy(vst[:, :, :], psb[b][:, :].rearrange("p (q c) -> p q c", q=4)), writes=["vst", f"ps{b}"])
                            dma("sp", v_p[o_, t4 * 512:(t4 + 1) * 512, 128 * j:128 * (j + 1)].rearrange("(q p) c -> p q c", p=128), vst[:, :, :], reads=["vst"])
                b = nextps()
                for s4 in range(4):
                    for kc in range(8):
                        P.mm(psb[b][0:8, s4 * 128:(s4 + 1) * 128], hT[:, kc, T + 8 * s4:T + 8 * s4 + 8], wsl[:, kc, 2, :], start=(kc == 0), stop=(kc == 7), reads=[wres, "hT"], writes=[f"ps{b}"])
                dsts = Vs[0:8, :, :].rearrange("p s (h e) -> p s h e", e=65)[:, :, :, 0:64]
                P.op("act", lambda e, b=b, dsts=dsts: e.activation(out=dsts, in_=psb[b][0:8, :].rearrange("p (s h e) -> p s h e", s=4, h=2), func=AF.Copy), writes=["Vs", f"ps{b}"])
                P.op("dve", lambda e, b=b: e.tensor_copy(vsst[0:8, :, :], psb[b][0:8, :].rearrange("p (s c) -> p s c", s=4)), writes=["vst", f"ps{b}"])
                dma("sp", v_s[o_, :, 128 * j:128 * (j + 1)].rearrange("(s t) c -> t s c", t=8), vsst[0:8, :, :], reads=["vst"])

                def softmax_tile(bs, ncols, maskap, maskres):
                    pi = ptc[0] % 2
                    ptc[0] += 1
                    P.op("act", lambda e: e.activation(out=PT[:, pi, 0:ncols], in_=psb[bs][:, 0:ncols], func=AF.Exp, scale=0.125), writes=[f"PT{pi}", f"ps{bs}"])
                    P.op("pool", lambda e: e.tensor_tensor(out=PT[:, pi, 0:ncols], in0=PT[:, pi, 0:ncols], in1=maskap, op=ALU.mult), reads=[maskres, f"PT{pi}"], writes=[f"PT{pi}"])
                    return PT[:, pi, :], f"PT{pi}"

                for hh in range(2):
                    pb = 64 * hh
                    q_ = QT[pb:pb + 64, :]
                    k_ = KT[pb:pb + 64, :]
                    for G in range(4):
                        bo = accps()
                        P.mm(psb[bo][0:65, 0:512], zerB[:, 0:65], zerB[:, 0:512], start=True, stop=False, reads=["zerB"], writes=[f"ps{bo}"], skip_group_check=True)
                        for kb in range(max(4 * G - 1, 0), 4 * G + 4):
                            has_cur = kb >= 4 * G
                            has_prev = kb + 1 <= 4 * G + 3
                            q0 = 128 * kb if has_cur else 128 * (kb + 1)
                            ncol = 128 * (int(has_cur) + int(has_prev))
                            bs = nextps()
                            P.mm(psb[bs][:, 0:ncol], k_[:, 128 * kb:128 * kb + 128], q_[:, q0:q0 + ncol], reads=["QK0", "QK1"], writes=[f"ps{bs}"])
                            m0 = 0 if has_cur else 128
                            pt, ptres = softmax_tile(bs, ncol, cmask[:, m0:m0 + ncol], "cmask")
                            vl = V3[:, 0, kb, hh * 65:hh * 65 + 65]
                            for part in range(ncol // 128):
                                qq = q0 + 128 * part - 512 * G
                                P.mm(psb[bo][0:65, qq:qq + 128], vl, pt[:, 128 * part:128 * part + 128], start=False, stop=False, reads=["V3", ptres], writes=[f"ps{bo}"], skip_group_check=True)
                        for r in range(4):
                            blocks = [nb for nb in (G - 1, G) if nb >= 0]
                            bs = nextps()
                            qap = q_[:, 512 * G + r:512 * G + r + 512:4]
                            for bi, nb in enumerate(blocks):
                                P.mm(psb[bs][:, 128 * bi:128 * bi + 128], k_[:, 512 * nb + r:512 * nb + r + 512:4], qap, reads=["QK0", "QK1"], writes=[f"ps{bs}"])
                            ncol = 128 * len(blocks)
                            m0 = 128 if len(blocks) == 2 else 0
                            pt, ptres = softmax_tile(bs, ncol, cmask[:, m0:m0 + ncol], "cmask")
                            for bi, nb in enumerate(blocks):
                                vl = V3[:, 1, r * 4 + nb, hh * 65:hh * 65 + 65]
                                P.mm(psb[bo][0:65, r:512:4], vl, pt[:, 128 * bi:128 * bi + 128], start=False, stop=False, reads=["V3", ptres], writes=[f"ps{bo}"], skip_group_check=True)
                        bs = nextps()
                        for r in range(16):
                            P.mm(psb[bs][:, 32 * r:32 * r + 32], k_[:, r:2048:16], q_[:, 512 * G + r:512 * G + r + 512:16], reads=["QK0", "QK1"], writes=[f"ps{bs}"])
                        pt, ptres = softmax_tile(bs, 512, m3[:, G, :], "m3")
                        for r in range(16):
                            vl = V3[:, 2, r, hh * 65:hh * 65 + 65]
                            P.mm(psb[bo][0:65, r:512:16], vl, pt[:, 32 * r:32 * r + 32], start=False, stop=(r == 15), reads=["V3", ptres], writes=[f"ps{bo}"], skip_group_check=True)
                        P.op("dve", lambda e, bo=bo: e.reciprocal(rec[64:65, 0:512], psb[bo][64:65, 0:512]), writes=["rec", f"ps{bo}"])
                        bb = nextps()
                        P.mm(psb[bb][0:64, 0:512], onesB[64:65, 0:64], rec[64:65, 0:512], reads=["onesB", "rec"], writes=[f"ps{bb}"])
                        P.op("act", lambda e, bb=bb: e.activation(out=bcs[0:64, :], in_=psb[bb][0:64, 0:512], func=AF.Copy), writes=["bcs", f"ps{bb}"])
                        P.op("dve", lambda e, bo=bo, pb=pb, G=G, j=j: e.tensor_tensor(out=oT[pb:pb + 64, j, 512 * G:512 * G + 512], in0=psb[bo][0:64, 0:512], in1=bcs[0:64, :], op=ALU.mult), reads=["bcs"], writes=["oT", f"ps{bo}"])
                bso = accps()
                P.mm(psb[bso][0:65, 0:64], zerB[:, 0:65], zerB[:, 0:64], start=True, stop=False, reads=["zerB"], writes=[f"ps{bso}"], skip_group_check=True)
                for s4 in range(4):
                    dma("pool", Kc[:, :, :], ck[o_, s4, :, 128 * j:128 * (j + 1)].rearrange("(t p) c -> p t c", p=128), writes=["Kc", "V3"])
                    for hv in range(2):
                        dma("pool", Vc[:, :, hv * 65:hv * 65 + 64], cv[o_, s4, :, 128 * j + 64 * hv:128 * j + 64 * hv + 64].rearrange("(t p) e -> p t e", p=128), writes=["Vc", "V3"])
                    P.op("pool", lambda e: e.memset(Vcv[:, :, 64:65], 1.0), reads=["V3"], writes=["Vc1"])
                    for t8 in range(2):
                        b = nextps()
                        pbf = psb[b][:, :].bitcast(BF16)
                        for q in range(8):
                            ti = t8 * 8 + q
                            P.tr(pbf[:, q * 128:(q + 1) * 128], Kc[:, ti, :], identB[:, :], reads=["Kc", "V3", "identB"], writes=[f"ps{b}"])
                        P.op("dve", lambda e, pbf=pbf, t8=t8: e.tensor_copy(KcT[:, t8 * 1024:(t8 + 1) * 1024], pbf[:, 0:1024]), reads=["V3"], writes=["KcT", f"ps{b}"])
                    qs0 = T + 8 * s4
                    bs = nextps()
                    for hh in range(2):
                        pb = 64 * hh
                        for ti in range(16):
                            P.mm(psb[bs][:, hh * 128 + ti * 8:hh * 128 + ti * 8 + 8], KcT[pb:pb + 64, 128 * ti:128 * ti + 128], QT[pb:pb + 64, qs0:qs0 + 8], reads=["KcT", "V3", "QK0"], writes=[f"ps{bs}"])
                        P.mm(psb[bs][0:8, 256 + hh * 8:256 + hh * 8 + 8], KT[pb:pb + 64, qs0:qs0 + 8], QT[pb:pb + 64, qs0:qs0 + 8], reads=["QK1", "QK0"], writes=[f"ps{bs}"])
                    P.op("act", lambda e, bs=bs: e.activation(out=PTs[:, :], in_=psb[bs][:, 0:256], func=AF.Exp, scale=0.125), writes=["PTs", f"ps{bs}"])
                    P.op("act", lambda e, bs=bs: e.activation(out=PTn[0:8, :], in_=psb[bs][0:8, 256:272], func=AF.Exp, scale=0.125), writes=["PTn", f"ps{bs}"])
                    P.op("pool", lambda e: e.tensor_tensor(out=PTs[:, :], in0=PTs[:, :], in1=smask[:, :], op=ALU.mult), reads=["smask", "PTs"], writes=["PTs"])
                    P.op("pool", lambda e: e.tensor_tensor(out=PTn[0:8, :], in0=PTn[0:8, :], in1=nmask[0:8, :], op=ALU.mult), reads=["nmask", "PTn"], writes=["PTn"])
                    for hh in range(2):
                        oc = s4 * 16 + hh * 8
                        for ti in range(16):
                            P.mm(psb[bso][0:65, oc:oc + 8], Vc[:, ti, hh * 65:hh * 65 + 65], PTs[:, hh * 128 + ti * 8:hh * 128 + ti * 8 + 8], start=False, stop=False, reads=["Vc", "Vc1", "V3", "PTs"], writes=[f"ps{bso}"], skip_group_check=True)
                        P.mm(psb[bso][0:65, oc:oc + 8], Vs[0:8, s4, hh * 65:hh * 65 + 65], PTn[0:8, hh * 8:hh * 8 + 8], start=False, stop=(s4 == 3 and hh == 1), reads=["Vs", "PTn"], writes=[f"ps{bso}"], skip_group_check=True)
                P.op("dve", lambda e, bso=bso: e.reciprocal(rec[64:65, 0:64], psb[bso][64:65, 0:64]), writes=["rec", f"ps{bso}"])
                bb = nextps()
                P.mm(psb[bb][0:64, 0:64], onesB[64:65, 0:64], rec[64:65, 0:64], reads=["onesB", "rec"], writes=[f"ps{bb}"])
                P.op("act", lambda e, bb=bb: e.activation(out=bcs[0:64, 0:64], in_=psb[bb][0:64, 0:64], func=AF.Copy), writes=["bcs", f"ps{bb}"])
                for hh in range(2):
                    pb = 64 * hh
                    srco = psb[bso][0:64, 0:64].rearrange("p (s h q) -> p s h q", s=4, h=2)[:, :, hh, :]
                    srcb = bcs[0:64, 0:64].rearrange("p (s h q) -> p s h q", s=4, h=2)[:, :, hh, :]
                    dsto = oT[pb:pb + 64, j, T:TT].rearrange("p (s q) -> p s q", s=4)
                    P.op("dve", lambda e, srco=srco, srcb=srcb, dsto=dsto: e.tensor_tensor(out=dsto, in0=srco, in1=srcb, op=ALU.mult), reads=["bcs"], writes=["oT", f"ps{bso}"])
            phase(keep_off)
            if dbg:
                dma("pool", dbg_o.rearrange("p (c t) -> p c t", c=8), oT[:, :, :], reads=["oT"])
            out_proj_and_residual(oT, od_w_out[o_], "oT")

        def even_mixer(e_):
            phase()
            A2 = Arena(hT[:, :, :].rearrange("p c t -> p (c t)").bitcast(F32), (8 * TT) // 2)
            evc = arena.alloc([128, 1664], BF16)
            dma("pool", evc, kc_ev, writes=["evc"])
            su64 = evc[0:CP, 0:8 * CP]; ui64 = evc[0:CP, 512:512 + 8 * CP]; sl64 = evc[0:CP, 1024:1024 + 8 * CP]; blk2 = evc[:, 1536:1664]
            idr64 = arena.alloc([64, 512], BF16)
            dma("pool", idr64, kc_idr, writes=["idr64"])
            m8 = arena.alloc([8, 256], BF16)
            dma("pool", m8, kc_m8, writes=["m8"])
            wo_sb = A2.alloc([128, 8, 1024], BF16)
            dma("pool", wo_sb, ev_w_out[e_].rearrange("(kc p) f -> p kc f", p=128), writes=["wo_sb"])
            lw = A2.alloc([128, 512], BF16)
            dma("pool", lw[0:32, :], rw_w2[e_], writes=["lw"])
            dma("pool", lw[32:64, :], rw_a2[e_], writes=["lw"])
            dma("pool", lw[64:128, :], rw_g2[e_], writes=["lw"])
            evp = A2.alloc([128, 48], F32)
            plist = [(rw_mu[e_].rearrange("(c p) -> c p", p=128), 13, 0), (rw_w0[e_].rearrange("(c p) -> c p", p=128), 4, 13),
                     (rw_a0[e_].rearrange("(c p) -> c p", p=128), 4, 17), (rw_kk[e_].rearrange("(c p) -> c p", p=128), 4, 21),
                     (rw_ka[e_].rearrange("(c p) -> c p", p=128), 4, 25), (rw_rk[e_].rearrange("(c p) -> c p", p=128), 4, 29),
                     (rw_lnx_g[e_].rearrange("(c p) -> c p", p=128), 4, 33), (rw_lnx_b[e_].rearrange("(c p) -> c p", p=128), 4, 37)]
            for (src, R, o0) in plist:
                load_rows_T(src, R, evp[:, o0:o0 + R], "evp")
            MU, W0, A0, KKP, KAP, RKP, LG, LB = 0, 13, 17, 21, 25, 29, 33, 37
            lngb = A2.alloc([128, 2, 512], F32)
            dma("sp", lngb[:, 0, :], gm_ln_g[e_:e_ + 1, :].broadcast_to([128, 512]), writes=["lngb"])
            dma("sp", lngb[:, 1, :], gm_ln_b[e_:e_ + 1, :].broadcast_to([128, 512]), writes=["lngb"])
            bsb = A2.alloc([128, 4, 128], F32)
            dma("sp", bsb, gm_bs[e_:e_ + 1, :, :].broadcast_to([128, 4, 128]), writes=["bsb"])
            WcT = A2.alloc([128, 4, 128], BF16)
            wstg = arena.alloc([128, 4, 128], F32)
            dma("sp", wstg, gm_ws[e_].rearrange("g i j -> i g j"), writes=["wstg"])
            for g in range(4):
                b = nextps()
                P.tr(psb[b][:, 0:128], wstg[:, g, :], identF[:, :], reads=["wstg", "identF"], writes=[f"ps{b}"])
                P.op("dve", lambda e, b=b, g=g: e.tensor_tensor(out=WcT[:, g, :], in0=psb[b][:, 0:128], in1=cmask[:, 0:128], op=ALU.mult), reads=["cmask"], writes=["WcT", f"ps{b}"])
            Bd = A2.alloc([32, 4, 32], BF16)
            P.op("pool", lambda e: e.memset(Bd[:, :, :], 0.0), writes=["Bd"])
            for g in range(4):
                for s4 in range(4):
                    dma("sp", Bd[8 * s4:8 * s4 + 8, g, 8 * s4:8 * s4 + 8], WcT[0:8, g, 0:8], reads=["WcT"], writes=["Bd"])
            S = A2.alloc([64, 8, 64], F32)
            Sb = A2.alloc([64, 8, 64], BF16)
            pblast = A2.alloc([128, 13, 4], F32)
            P.op("pool", lambda e: e.memset(S[:, :, :], 0.0), writes=["S"])
            P.op("pool", lambda e: e.memset(Sb[:, :, :], 0.0), writes=["Sb"])
            P.op("pool", lambda e: e.memset(pblast[:, :, :], 0.0), writes=["pblast"])
            sst = A2.alloc([64, 8, 64], F32)
            keep_off = arena.off
            win = ev_w_in[e_].rearrange("(kc p) f -> p kc f", p=128)
            porder = [(0, 512), (512, 512), (2560, 128), (1536, 512), (1024, 512), (2048, 512)]
            EXPC = float(-np.exp(-0.5))
            groups = [(256 * i, 256, 1, 256, CP) for i in range(8)] + [(T, 32, 4, 8, 8)]
            def do_group(gi, c0, n, nseq, L, C):
                phase(keep_off)
                is_s = nseq > 1
                if is_s:
                    stg = arena.alloc([128, 128], F32)
                    for s4 in range(4):
                        dma("sp", stg[13 * s4:13 * s4 + 13, :], sshift[e_, s4, :].rearrange("(c p) -> c p", p=128), writes=["stg"])
                    b = nextps()
                    P.tr(psb[b][:, 0:52], stg[0:52, :], identF[0:52, 0:52], reads=["stg", "identF"], writes=[f"ps{b}"])
                    P.op("dve", lambda e, b=b: e.tensor_copy(pblast[:, :, :], psb[b][:, 0:52].rearrange("p (s c) -> p c s", s=4)), writes=["pblast", f"ps{b}"])
                rstd_g = arena.alloc([128, n], F32)
                E1 = arena.alloc([128, 4, n], F32)
                sqg = E1[:, :, :].rearrange("p c n -> p (c n)").bitcast(BF16).rearrange("p (c n) -> p c n", c=8)
                off_hg = arena.off
                hg = arena.alloc([128, 8, n], BF16)
                off_hg_end = arena.off
                tmpg = arena.alloc([128, n], F32)
                for c in range(8):
                    P.op("act", lambda e, c=c: e.activation(out=sqg[:, c, :], in_=xT[:, c, c0:c0 + n], func=AF.Square), reads=["xT"], writes=["sqg", "E1"])
                b = nextps()
                for c in range(8):
                    P.mm(psb[b][:, 0:n], onesB[:, :], sqg[:, c, :], start=(c == 0), stop=(c == 7), reads=["sqg", "onesB"], writes=[f"ps{b}"])
                P.op("act", lambda e, b=b: e.activation(out=rstd_g[:, :], in_=psb[b][:, 0:n], func=AF.Sqrt, scale=1.0 / D, bias=epsb[:, 0:1]), reads=["epsb"], writes=["rstd_g", f"ps{b}"])
                P.op("dve", lambda e: e.reciprocal(rstd_g[:, :], rstd_g[:, :]), reads=["rstd_g"], writes=["rstd_g"])
                gsegs = [(0, n, 0)] if not is_s else [(8 * s4, 8, 1 + s4) for s4 in range(4)]
                for c in range(8):
                    for (o0, nn, sq_) in gsegs:
                        P.op("dve", lambda e, c=c, o0=o0, nn=nn, sq_=sq_: e.scalar_tensor_tensor(out=tmpg[:, o0:o0 + nn], in0=xT[:, c, c0 + o0:c0 + o0 + nn], scalar=mods[:, 0, c, sq_:sq_ + 1], in1=rstd_g[:, o0:o0 + nn], op0=ALU.mult, op1=ALU.mult), reads=["xT", "mods", "rstd_g"], writes=["tmpg"])
                        P.op("act", lambda e, c=c, o0=o0, nn=nn, sq_=sq_: e.activation(out=hg[:, c, o0:o0 + nn], in_=tmpg[:, o0:o0 + nn], func=AF.Identity, bias=mods[:, 1, c, sq_:sq_ + 1]), reads=["tmpg", "mods"], writes=["hg"])
                if EVSTOP <= 2:
                    return
                mixg = arena.alloc([128, 8, n], BF16)
                uT = arena.alloc([128, 4, n], BF16)
                vn = arena.alloc([128, 2, 512], BF16)
                lnt = arena.alloc([128, 512], F32)
                lnt2 = arena.alloc([128, 512], F32)
                Of = lnt[0:64, :]
                Osq = lnt2[0:64, :]
                stat = arena.alloc([128, 16], F32)
                pbc = arena.alloc([128, nseq, L + 1], F32)
                xmc = arena.alloc([128, nseq, L], F32)
                lor = arena.alloc([128, n], BF16)
                off_ld = arena.off
                ld = arena.alloc([128, 4, n], F32)
                lp = arena.alloc([128, 4, n], F32)
                off_e2 = arena.off
                E2 = arena.alloc([128, 4, n], F32)
                ag = arena.alloc([128, 4, n], F32)
                off_e2_end = arena.off
                off_kp = arena.off
                kp = arena.alloc([128, 4, n], F32)
                off_kp_end = arena.off
                gT = arena.alloc([128, 4, n], BF16)
                At = arena.alloc([64, 8, n], BF16)
                Bt = arena.alloc([64, 8, n], BF16)
                Kt = arena.alloc([64, 8, n], BF16)
                Rt = arena.alloc([64, 8, n], BF16)
                vT = arena.alloc([64, 8, n], BF16)
                PCg = arena.alloc([64, 8, 8], F32)
                bn = arena.alloc([128, 4, n], BF16)
                t1 = arena.alloc([128, n], F32)
                t2 = arena.alloc([128, n], F32)
                t3 = arena.alloc([128, n], BF16)
                ws = WStream([(win[:, :, a:a + w], (8, w)) for (a, w) in porder], depth=2)

                def xm_chunk(b, cb):
                    P.op("act", lambda e: e.activation(out=pbc[:, :, 1:L + 1], in_=psb[b][:, 0:n].rearrange("p (s l) -> p s l", s=nseq), func=AF.Copy), writes=["pbc", f"ps{b}"])
                    P.op("dve", lambda e: e.tensor_copy(pbc[:, :, 0], pblast[:, cb, 0:nseq]), reads=["pblast"], writes=["pbc"])
                    P.op("dve", lambda e: e.tensor_tensor(out=xmc[:, :, :], in0=pbc[:, :, 0:L], in1=pbc[:, :, 1:L + 1], op=ALU.subtract), reads=["pbc"], writes=["xmc"])
                    P.op("dve", lambda e: e.scalar_tensor_tensor(out=xmc[:, :, :], in0=xmc[:, :, :], scalar=evp[:, MU + cb:MU + cb + 1], in1=pbc[:, :, 1:L + 1], op0=ALU.mult, op1=ALU.add), reads=["xmc", "pbc", "evp"], writes=["xmc"])
                    P.op("pool", lambda e: e.tensor_copy(pblast[:, cb, 0:nseq], pbc[:, :, L]), reads=["pbc"], writes=["pblast"])
                    return xmc[:, :, :].rearrange("p s l -> p (s l)")

                def proj_fm(wsl, wres, col, b):
                    for kc in range(8):
                        P.mm(psb[b][:, 0:n], wsl[:, kc, col * 128:(col + 1) * 128], hg[:, kc, :], start=(kc == 0), stop=(kc == 7), reads=[wres, "hg"], writes=[f"ps{b}"])

                wsl, wres = ws.get()
                for c in range(4):
                    b = nextps()
                    proj_fm(wsl, wres, c, b)
                    P.op("act", lambda e, b=b, c=c: e.activation(out=uT[:, c, :], in_=psb[b][:, 0:n], func=AF.Gelu), writes=["uT", f"ps{b}"])
                wsl, wres = ws.get()
                ntile = (n + 127) // 128
                for tt in range(ntile):
                    r = min(128, n - 128 * tt)
                    b = nextps()
                    for kc in range(8):
                        P.mm(psb[b][0:r, 0:512], hg[:, kc, 128 * tt:128 * tt + r], wsl[:, kc, :], start=(kc == 0), stop=(kc == 7), reads=[wres, "hg"], writes=[f"ps{b}"])
                    P.op("act", lambda e, b=b, r=r: e.activation(out=lnt[0:r, :], in_=psb[b][0:r, 0:512], func=AF.Gelu), writes=["lnt", f"ps{b}"])
                    P.op("dve", lambda e, r=r: e.reduce_sum(out=stat[0:r, 0:1], in_=lnt[0:r, :], axis=AX.X), reads=["lnt"], writes=["stat"])
                    P.op("pool", lambda e, r=r: e.tensor_tensor(out=lnt2[0:r, :], in0=lnt[0:r, :], in1=lnt[0:r, :], op=ALU.mult), reads=["lnt"], writes=["lnt2"])
                    P.op("dve", lambda e, r=r: e.reduce_sum(out=stat[0:r, 1:2], in_=lnt2[0:r, :], axis=AX.X), reads=["lnt2"], writes=["stat"])
                    P.op("dve", lambda e, r=r: e.tensor_scalar(out=stat[0:r, 2:3], in0=stat[0:r, 0:1], scalar1=1.0 / 512, scalar2=None, op0=ALU.mult), reads=["stat"], writes=["stat"])
                    P.op("dve", lambda e, r=r: e.tensor_tensor(out=stat[0:r, 3:4], in0=stat[0:r, 2:3], in1=stat[0:r, 2:3], op=ALU.mult), reads=["stat"], writes=["stat"])
                    P.op("dve", lambda e, r=r: e.scalar_tensor_tensor(out=stat[0:r, 4:5], in0=stat[0:r, 1:2], scalar=1.0 / 512, in1=stat[0:r, 3:4], op0=ALU.mult, op1=ALU.subtract), reads=["stat"], writes=["stat"])
                    P.op("dve", lambda e, r=r: e.tensor_scalar(out=stat[0:r, 4:5], in0=stat[0:r, 4:5], scalar1=1e-5, scalar2=None, op0=ALU.add), reads=["stat"], writes=["stat"])
                    P.op("act", lambda e, r=r: e.activation(out=stat[0:r, 5:6], in_=stat[0:r, 4:5], func=AF.Sqrt), reads=["stat"], writes=["stat"])
                    P.op("dve", lambda e, r=r: e.reciprocal(stat[0:r, 5:6], stat[0:r, 5:6]), reads=["stat"], writes=["stat"])
                    P.op("dve", lambda e, r=r: e.tensor_scalar(out=lnt[0:r, :], in0=lnt[0:r, :], scalar1=stat[0:r, 2:3], scalar2=stat[0:r, 5:6], op0=ALU.subtract, op1=ALU.mult), reads=["stat", "lnt"], writes=["lnt"])
                    P.op("pool", lambda e, r=r: e.tensor_tensor(out=lnt[0:r, :], in0=lnt[0:r, :], in1=lngb[0:r, 0, :], op=ALU.mult), reads=["lnt", "lngb"], writes=["lnt"])
                    P.op("pool", lambda e, r=r: e.tensor_tensor(out=lnt[0:r, :], in0=lnt[0:r, :], in1=lngb[0:r, 1, :], op=ALU.add), reads=["lnt", "lngb"], writes=["lnt"])
                    P.op("act", lambda e, r=r, tt=tt: e.activation(out=vn[0:r, tt, :], in_=lnt[0:r, :], func=AF.Copy), reads=["lnt"], writes=["vn"])
                    if is_s:
                        dma("sp", gv_s[e_, :, :], lnt[0:32, :], reads=["lnt"])
                for g in range(4):
                    b = nextps()
                    if is_s:
                        P.mm(psb[b][:, 0:32], vn[0:32, 0, g * 128:(g + 1) * 128], Bd[:, g, :], reads=["vn", "Bd"], writes=[f"ps{b}"])
                        for s4 in range(4):
                            P.op("dve", lambda e, b=b, s4=s4, g=g: e.tensor_tensor(out=t1[:, 8 * s4:8 * s4 + 8], in0=psb[b][:, 8 * s4:8 * s4 + 8], in1=bsb[:, g, 0:8], op=ALU.add), reads=["bsb"], writes=["t1", f"ps{b}"])
                    else:
                        for tt in range(ntile):
                            P.mm(psb[b][:, 128 * tt:128 * tt + 128], vn[:, tt, g * 128:(g + 1) * 128], WcT[:, g, :], reads=["vn", "WcT"], writes=[f"ps{b}"])
                            P.op("dve", lambda e, b=b, tt=tt, g=g: e.tensor_tensor(out=t1[:, 128 * tt:128 * tt + 128], in0=psb[b][:, 128 * tt:128 * tt + 128], in1=bsb[:, g, :], op=ALU.add), reads=["bsb"], writes=["t1", f"ps{b}"])
                    P.op("pool", lambda e, g=g: e.tensor_tensor(out=mixg[:, g, :], in0=t1[:, :], in1=uT[:, g, :], op=ALU.mult), reads=["t1", "uT"], writes=["mixg"])
                if EVSTOP <= 3:
                    return
                wsl, wres = ws.get()
                b = nextps()
                proj_fm(wsl, wres, 0, b)
                xm = xm_chunk(b, 12)
                P.op("act", lambda e: e.activation(out=lor[0:32, :], in_=xm[0:32, :], func=AF.Tanh), reads=["xmc"], writes=["lor"])
                P.op("act", lambda e: e.activation(out=lor[32:64, :], in_=xm[32:64, :], func=AF.Copy), reads=["xmc"], writes=["lor"])
                P.op("act", lambda e: e.activation(out=lor[64:128, :], in_=xm[64:128, :], func=AF.Sigmoid), reads=["xmc"], writes=["lor"])
                for c in range(4):
                    b = nextps()
                    P.mm(psb[b][:, 0:n], lw[0:32, c * 128:(c + 1) * 128], lor[0:32, :], reads=["lw", "lor"], writes=[f"ps{b}"])
                    P.op("act", lambda e, b=b, c=c: e.activation(out=ld[:, c, :], in_=psb[b][:, 0:n], func=AF.Sigmoid, bias=evp[:, W0 + c:W0 + c + 1]), reads=["evp"], writes=["ld", f"ps{b}"])
                    b = nextps()
                    P.mm(psb[b][:, 0:n], lw[32:64, c * 128:(c + 1) * 128], lor[32:64, :], reads=["lw", "lor"], writes=[f"ps{b}"])
                    P.op("act", lambda e, b=b, c=c: e.activation(out=ag[:, c, :], in_=psb[b][:, 0:n], func=AF.Sigmoid, bias=evp[:, A0 + c:A0 + c + 1]), reads=["evp"], writes=["ag", f"ps{b}"])
                    b = nextps()
                    P.mm(psb[b][:, 0:n], lw[64:128, c * 128:(c + 1) * 128], lor[64:128, :], reads=["lw", "lor"], writes=[f"ps{b}"])
                    P.op("act", lambda e, b=b, c=c: e.activation(out=gT[:, c, :], in_=psb[b][:, 0:n], func=AF.Copy), writes=["gT", f"ps{b}"])
                P.op("dve", lambda e: e.tensor_scalar(out=ld[:, :, :], in0=ld[:, :, :], scalar1=EXPC, scalar2=None, op0=ALU.mult), reads=["ld"], writes=["ld"])
                nch = (4 * n) // C
                ldv = ld[:, :, :].rearrange("p c (k l) -> p (c k) l", l=C)
                lpv = lp[:, :, :].rearrange("p c (k l) -> p (c k) l", l=C)
                e1v = E1[:, :, :].rearrange("p c (k l) -> p (c k) l", l=C)
                P.op("pool", lambda e: e.tensor_copy(lpv, ldv), reads=["ld"], writes=["lp"])
                src, dst, sres, dres = lpv, e1v, "lp", "E1"
                sh = 1
                while sh < C:
                    P.op("dve", lambda e, src=src, dst=dst, sh=sh: e.tensor_tensor(out=dst[:, :, sh:C], in0=src[:, :, sh:C], in1=src[:, :, 0:C - sh], op=ALU.add), reads=[sres], writes=[dres])
                    P.op("pool", lambda e, src=src, dst=dst, sh=sh: e.tensor_copy(dst[:, :, 0:sh], src[:, :, 0:sh]), reads=[sres], writes=[dres])
                    src, dst, sres, dres = dst, src, dres, sres
                    sh *= 2
                if src is not lpv:
                    P.op("pool", lambda e: e.tensor_copy(lpv, e1v), reads=["E1"], writes=["lp"])
                P.op("dve", lambda e: e.tensor_tensor(out=ld[:, :, :], in0=lp[:, :, :], in1=ld[:, :, :], op=ALU.subtract), reads=["lp", "ld"], writes=["ld"])
                P.op("act", lambda e: e.activation(out=E1[:, :, :], in_=lp[:, :, :], func=AF.Exp), reads=["lp"], writes=["E1"])
                P.op("act", lambda e: e.activation(out=E2[:, :, :], in_=lp[:, :, :], func=AF.Exp, scale=-1.0), reads=["lp"], writes=["E2"])
                P.op("act", lambda e: e.activation(out=ld[:, :, :], in_=ld[:, :, :], func=AF.Exp), reads=["ld"], writes=["ld"])
                E3 = ld
                nck = n // C
                for c in range(4):
                    for hh in range(2):
                        P.op("act", lambda e, c=c, hh=hh: e.activation(out=PCg[:, 2 * c + hh, 0:nck], in_=E1[64 * hh:64 * hh + 64, c, C - 1:n:C], func=AF.Copy), reads=["E1"], writes=["PCg"])
                if EVSTOP <= 4:
                    return
                wsl, wres = ws.get()
                for c in range(4):
                    b = nextps()
                    proj_fm(wsl, wres, c, b)
                    xm = xm_chunk(b, 4 + c)
                    P.op("dve", lambda e, c=c, xm=xm: e.tensor_scalar(out=t1[:, :], in0=xm, scalar1=evp[:, KKP + c:KKP + c + 1], scalar2=None, op0=ALU.mult), reads=["xmc", "evp"], writes=["t1"])
                    P.op("act", lambda e: e.activation(out=t3[:, :], in_=t1[:, :], func=AF.Square), reads=["t1"], writes=["t3"])
                    b2 = nextps()
                    P.mm(psb[b2][:, 0:n], blk2, t3[:, :], reads=["evc", "t3"], writes=[f"ps{b2}"])
                    P.op("act", lambda e, b2=b2: e.activation(out=t2[:, :], in_=psb[b2][:, 0:n], func=AF.Sqrt, bias=epsb2[:, 0:1]), reads=["epsb"], writes=["t2", f"ps{b2}"])
                    P.op("dve", lambda e: e.reciprocal(t2[:, :], t2[:, :]), reads=["t2"], writes=["t2"])
                    P.op("dve", lambda e: e.tensor_tensor(out=t1[:, :], in0=t1[:, :], in1=t2[:, :], op=ALU.mult), reads=["t1", "t2"], writes=["t1"])
                    P.op("dve", lambda e, c=c: e.tensor_scalar(out=t2[:, :], in0=ag[:, c, :], scalar1=-1.0, scalar2=evp[:, KAP + c:KAP + c + 1], op0=ALU.add, op1=ALU.mult), reads=["ag", "evp"], writes=["t2"])
                    P.op("dve", lambda e, c=c, xm=xm: e.scalar_tensor_tensor(out=kp[:, c, :], in0=t2[:, :], scalar=1.0, in1=xm, op0=ALU.add, op1=ALU.mult), reads=["t2", "xmc"], writes=["kp"])
                    P.op("pool", lambda e, c=c: e.tensor_tensor(out=t2[:, :], in0=t1[:, :], in1=ag[:, c, :], op=ALU.mult), reads=["t1", "ag"], writes=["t2"])
                    for hh in range(2):
                        ps_ = slice(64 * hh, 64 * hh + 64)
                        P.op("dve", lambda e, c=c, hh=hh, ps_=ps_: e.scalar_tensor_tensor(out=At[:, 2 * c + hh, :], in0=t1[ps_, :], scalar=-1.0, in1=E3[ps_, c, :], op0=ALU.mult, op1=ALU.mult), reads=["t1", "ld"], writes=["At"])
                        P.op("pool", lambda e, c=c, hh=hh, ps_=ps_: e.tensor_tensor(out=Bt[:, 2 * c + hh, :], in0=t2[ps_, :], in1=E2[ps_, c, :], op=ALU.mult), reads=["t2", "E2"], writes=["Bt"])
                        P.op("pool", lambda e, c=c, hh=hh, ps_=ps_: e.tensor_tensor(out=Kt[:, 2 * c + hh, :], in0=kp[ps_, c, :], in1=E2[ps_, c, :], op=ALU.mult), reads=["kp", "E2"], writes=["Kt"])
                wsl, wres = ws.get()
                for c in range(4):
                    b = nextps()
                    proj_fm(wsl, wres, c, b)
                    xm = xm_chunk(b, c)
                    for hh in range(2):
                        ps_ = slice(64 * hh, 64 * hh + 64)
                        P.op("pool", lambda e, c=c, xm=xm, hh=hh, ps_=ps_: e.tensor_tensor(out=Rt[:, 2 * c + hh, :], in0=xm[ps_, :], in1=E1[ps_, c, :], op=ALU.mult), reads=["xmc", "E1"], writes=["Rt"])
                    P.op("dve", lambda e, c=c, xm=xm: e.scalar_tensor_tensor(out=t3[:, :], in0=xm, scalar=evp[:, RKP + c:RKP + c + 1], in1=kp[:, c, :], op0=ALU.mult, op1=ALU.mult), reads=["xmc", "evp", "kp"], writes=["t3"])
                    b2 = nextps()
                    P.mm(psb[b2][:, 0:n], blk2, t3[:, :], reads=["evc", "t3"], writes=[f"ps{b2}"])
                    P.op("act", lambda e, b2=b2, c=c: e.activation(out=kp[:, c, :], in_=psb[b2][:, 0:n], func=AF.Copy), reads=["t3"], writes=["kp", f"ps{b2}"])
                wsl, wres = ws.get()
                for c in range(4):
                    b = nextps()
                    proj_fm(wsl, wres, c, b)
                    xm = xm_chunk(b, 8 + c)
                    for hh in range(2):
                        ps_ = slice(64 * hh, 64 * hh + 64)
                        P.op("act", lambda e, c=c, xm=xm, hh=hh, ps_=ps_: e.activation(out=vT[:, 2 * c + hh, :], in_=xm[ps_, :], func=AF.Copy), reads=["xmc"], writes=["vT"])
                    P.op("dve", lambda e, c=c, xm=xm: e.tensor_tensor(out=bn[:, c, :], in0=xm, in1=kp[:, c, :], op=ALU.mult), reads=["xmc", "kp"], writes=["bn"])
                if EVSTOP <= 5:
                    return
                if (not is_s and gi == 7) or is_s:
                    ncol = 13 * nseq
                    b = nextps()
                    P.tr(psb[b][0:ncol, 0:128], pblast[:, :, 0:nseq].rearrange("p c s -> p (c s)") if nseq == 4 else pblast[:, :, 0], identF[:, :], reads=["pblast", "identF"], writes=[f"ps{b}"])
                    sho = arena.alloc([128, 128], F32)
                    P.op("dve", lambda e, b=b, ncol=ncol: e.tensor_copy(sho[0:ncol, :], psb[b][0:ncol, 0:128]), writes=["sho", f"ps{b}"])
                    if not is_s:
                        dma("sp", shift_p[e_, :].rearrange("(c p) -> c p", p=128), sho[0:13, :], reads=["sho"])
                    else:
                        for cb in range(13):
                            dma("sp", shift_s[e_, :, cb * 128:(cb + 1) * 128], sho[4 * cb:4 * cb + 4, :], reads=["sho"])
                if EVSTOP <= 6:
                    return
                nlev = {64: 5, 32: 4, 16: 3, 8: 2}[C]
                HC = 8 * C
                if C == CP:
                    mSU, mUI, mSL, mID = su64, ui64, sl64, idr64[0:CP, 0:8 * CP]
                else:
                    mSU, mUI, mSL, mID = m8[:, 0:64], m8[:, 64:128], m8[:, 128:192], m8[:, 192:256]
                save_off = arena.off
                if n == 256:
                    arena.off = off_e2
                NM = arena.alloc([64, 2, 512], BF16)
                LM = arena.alloc([64, 2, 512], BF16)
                WM = arena.alloc([64, 2, 512], BF16)
                Xb = arena.alloc([64, 512], BF16)
                Ub = arena.alloc([64, 512], BF16)
                if n == 256:
                    assert arena.off <= off_e2_end
                    arena.off = off_hg
                tok3 = arena.alloc([64, 3, 512], BF16)
                if n == 256:
                    assert arena.off <= off_hg_end
                    arena.off = save_off
                save_off2 = arena.off
                if n == 256:
                    arena.off = off_kp
                AKm = arena.alloc([64, 512], BF16)
                RBm = arena.alloc([64, 512], BF16)
                RKm = arena.alloc([64, 512], BF16)
                Onb = arena.alloc([64, 512], BF16)
                if n == 256:
                    assert arena.off <= off_kp_end
                    arena.off = save_off2
                gst = arena.alloc([64, 48], F32)
                Stmp = uT[0:64, :, :].rearrange("p c n -> p (c n)").bitcast(F32)[:, 0:512].rearrange("p (h v) -> p h v", h=8) if n == 256 else arena.alloc([64, 8, 64], F32)
                def do_chunk(ck):
                    k0 = ck * C
                    if is_s:
                        dma("sp", sst[:, :, :], swkv[e_, ck].rearrange("h v k -> v h k"), writes=["sst"])
                        b = nextps()
                        for h in range(8):
                            P.tr(psb[b][0:64, h * 64:(h + 1) * 64], sst[:, h, :], identF[0:64, 0:64], reads=["sst", "identF"], writes=[f"ps{b}"])
                        P.op("dve", lambda e, b=b: e.tensor_copy(S[:, :, :], psb[b][0:64, 0:512].rearrange("p (h v) -> p h v", h=8)), writes=["S", f"ps{b}"])
                        P.op("act", lambda e: e.activation(out=Sb[:, :, :], in_=S[:, :, :], func=AF.Copy), reads=["S"], writes=["Sb"])

                    def hsl(arr, h):
                        return arr[:, h, k0:k0 + C]

                    def scores(lhs, rhs, lres, rres, mask, mres, dst, dres, add_id=False):
                        b = nextps()
                        for h in range(8):
                            P.mm(psb[b][0:C, h * C:(h + 1) * C], hsl(lhs, h), hsl(rhs, h), reads=[lres, rres], writes=[f"ps{b}"])
                        P.op("dve", lambda e, b=b: e.tensor_tensor(out=dst[0:C, 0:HC], in0=psb[b][0:C, 0:HC], in1=mask, op=ALU.mult), reads=[mres], writes=[dres, f"ps{b}"])
                    mres = "evc" if C == CP else "m8"
                    scores(Bt, At, "Bt", "At", mSU, mres, NM[:, 0, :], "NM0")
                    if EVSTOP <= 6.1:
                        return
                    scores(At, Bt, "At", "Bt", mSL, mres, LM[:, 0, :], "LM0")
                    scores(Kt, At, "Kt", "At", mSU, mres, AKm, "AKm")
                    scores(Bt, Rt, "Bt", "Rt", mUI, mres, RBm, "RBm")
                    scores(Kt, Rt, "Kt", "Rt", mUI, mres, RKm, "RKm")
                    if EVSTOP <= 6.2:
                        return
                    P.op("pool", lambda e: e.tensor_tensor(out=WM[0:C, 0, 0:HC], in0=NM[0:C, 0, 0:HC], in1=mID, op=ALU.add), reads=["NM0", "idr64", "m8"], writes=["WM0"])
                    if EVSTOP <= 6.3:
                        return
                    cur = 0
                    for lev in range(nlev):
                        nxt = 1 - cur
                        b = nextps()
                        for h in range(8):
                            P.mm(psb[b][0:C, h * C:(h + 1) * C], NM[0:C, cur, h * C:(h + 1) * C], LM[0:C, cur, h * C:(h + 1) * C], reads=[f"NM{cur}", f"LM{cur}"], writes=[f"ps{b}"])
                        P.op("act", lambda e, b=b, nxt=nxt: e.activation(out=LM[0:C, nxt, 0:HC], in_=psb[b][0:C, 0:HC], func=AF.Copy), writes=[f"LM{nxt}", f"ps{b}"])
                        if lev < nlev - 1:
                            b = nextps()
                            for h in range(8):
                                P.mm(psb[b][0:C, h * C:(h + 1) * C], LM[0:C, cur, h * C:(h + 1) * C], NM[0:C, cur, h * C:(h + 1) * C], reads=[f"NM{cur}", f"LM{cur}"], writes=[f"ps{b}"])
                            P.op("dve", lambda e, b=b, nxt=nxt: e.tensor_copy(NM[0:C, nxt, 0:HC], psb[b][0:C, 0:HC]), writes=[f"NM{nxt}", f"ps{b}"])
                        b = nextps()
                        for h in range(8):
                            P.mm(psb[b][0:C, h * C:(h + 1) * C], LM[0:C, nxt, h * C:(h + 1) * C], WM[0:C, cur, h * C:(h + 1) * C], reads=[f"LM{nxt}", f"WM{cur}"], writes=[f"ps{b}"])
                        P.op("dve", lambda e, b=b, nxt=nxt, cur=cur: e.tensor_tensor(out=WM[0:C, nxt, 0:HC], in0=psb[b][0:C, 0:HC], in1=WM[0:C, cur, 0:HC], op=ALU.add), reads=[f"WM{cur}"], writes=[f"WM{nxt}", f"ps{b}"])
                        cur = nxt
                    Wf = WM[:, cur, :]
                    wfres = f"WM{cur}"
                    if EVSTOP <= 7:
                        return
                    for ai, (arr, ares) in enumerate(((Bt, "Bt"), (Kt, "Kt"), (vT, "vT"))):
                        b = nextps()
                        pbf = psb[b][:, :].bitcast(BF16)
                        for h in range(8):
                            P.tr(pbf[0:C, h * 64:(h + 1) * 64], arr[:, h, k0:k0 + C], identB[0:64, 0:64], reads=[ares, "identB"], writes=[f"ps{b}"])
                        P.op("act" if ai != 1 else "dve", (lambda e, pbf=pbf, ai=ai: e.activation(out=tok3[0:C, ai, :], in_=pbf[0:C, 0:512], func=AF.Copy)) if ai != 1 else (lambda e, pbf=pbf, ai=ai: e.tensor_copy(tok3[0:C, ai, :], pbf[0:C, 0:512])), writes=[f"tok{ai}", f"ps{b}"])
                    Btok, Ktok, Vtok = tok3[:, 0, :], tok3[:, 1, :], tok3[:, 2, :]

                    def sbh(h):
                        return Sb[:, h, :]
                    b = nextps()
                    for h in range(8):
                        P.mm(psb[b][0:C, h * 64:(h + 1) * 64], hsl(At, h), sbh(h), start=True, stop=False, reads=["At", "Sb"], writes=[f"ps{b}"], skip_group_check=True)
                        P.mm(psb[b][0:C, h * 64:(h + 1) * 64], AKm[0:C, h * C:(h + 1) * C], Vtok[0:C, h * 64:(h + 1) * 64], start=False, stop=True, reads=["AKm", "tok2"], writes=[f"ps{b}"], skip_group_check=True)
                    P.op("act", lambda e, b=b: e.activation(out=Xb[0:C, :], in_=psb[b][0:C, 0:512], func=AF.Copy), writes=["Xb", f"ps{b}"])
                    b = nextps()
                    for h in range(8):
                        P.mm(psb[b][0:C, h * 64:(h + 1) * 64], Wf[0:C, h * C:(h + 1) * C], Xb[0:C, h * 64:(h + 1) * 64], reads=[wfres, "Xb"], writes=[f"ps{b}"])
                    P.op("act", lambda e, b=b: e.activation(out=Ub[0:C, :], in_=psb[b][0:C, 0:512], func=AF.Copy), writes=["Ub", f"ps{b}"])
                    b = nextps()
                    for h in range(8):
                        P.mm(psb[b][0:C, h * 64:(h + 1) * 64], hsl(Rt, h), sbh(h), start=True, stop=False, reads=["Rt", "Sb"], writes=[f"ps{b}"], skip_group_check=True)
                        P.mm(psb[b][0:C, h * 64:(h + 1) * 64], RBm[0:C, h * C:(h + 1) * C], Ub[0:C, h * 64:(h + 1) * 64], start=False, stop=False, reads=["RBm", "Ub"], writes=[f"ps{b}"], skip_group_check=True)
                        P.mm(psb[b][0:C, h * 64:(h + 1) * 64], RKm[0:C, h * C:(h + 1) * C], Vtok[0:C, h * 64:(h + 1) * 64], start=False, stop=True, reads=["RKm", "tok2"], writes=[f"ps{b}"], skip_group_check=True)
                    P.op("act", lambda e, b=b: e.activation(out=Of[0:C, :], in_=psb[b][0:C, 0:512], func=AF.Copy), writes=["lnt", f"ps{b}"])
                    b = nextps()
                    for h in range(8):
                        o_ap = psb[b][0:64, h * 64:(h + 1) * 64]
                        P.mm(o_ap, Btok[0:C, h * 64:(h + 1) * 64], Ub[0:C, h * 64:(h + 1) * 64], start=True, stop=False, reads=["tok0", "Ub"], writes=[f"ps{b}"], skip_group_check=True)
                        P.mm(o_ap, Ktok[0:C, h * 64:(h + 1) * 64], Vtok[0:C, h * 64:(h + 1) * 64], start=False, stop=True, reads=["tok1", "tok2"], writes=[f"ps{b}"], skip_group_check=True)
                    P.op("dve", lambda e, b=b: e.tensor_tensor(out=Stmp[:, :, :], in0=psb[b][0:64, 0:512].rearrange("p (h v) -> p h v", h=8), in1=S[:, :, :], op=ALU.add), reads=["S"], writes=["Stmp", f"ps{b}"])
                    for h in range(8):
                        P.op("dve" if h % 2 == 0 else "pool", lambda e, h=h: e.tensor_scalar(out=S[:, h, :], in0=Stmp[:, h, :], scalar1=PCg[:, h, ck:ck + 1], scalar2=None, op0=ALU.mult), reads=["Stmp", "PCg"], writes=["S"])
                    P.op("act", lambda e: e.activation(out=Sb[:, :, :], in_=S[:, :, :], func=AF.Copy), reads=["S"], writes=["Sb"])
                    if is_s or (gi == 7 and ck == n // C - 1):
                        b = nextps()
                        for h in range(8):
                            P.tr(psb[b][0:64, h * 64:(h + 1) * 64], S[:, h, :], identF[0:64, 0:64], reads=["S", "identF"], writes=[f"ps{b}"])
                        P.op("dve", lambda e, b=b: e.tensor_copy(sst[:, :, :].rearrange("v h k -> v (h k)"), psb[b][0:64, 0:512]), writes=["sst", f"ps{b}"])
                        dstw = wkv_s[e_, ck] if is_s else wkv_p[e_]
                        dma("sp", dstw.rearrange("h v k -> v h k"), sst[:, :, :], reads=["sst"])
                    Ov = Of[0:C, :].rearrange("t (h v) -> t h v", h=8)
                    P.op("dve", lambda e, Ov=Ov: e.reduce_sum(out=gst[0:C, 0:8], in_=Ov, axis=AX.X), reads=["lnt"], writes=["gst"])
                    P.op("pool", lambda e: e.tensor_tensor(out=Osq[0:C, :], in0=Of[0:C, :], in1=Of[0:C, :], op=ALU.mult), reads=["lnt"], writes=["lnt2"])
                    P.op("dve", lambda e: e.reduce_sum(out=gst[0:C, 8:16], in_=Osq[0:C, :].rearrange("t (h v) -> t h v", h=8), axis=AX.X), reads=["lnt2"], writes=["gst"])
                    P.op("dve", lambda e: e.tensor_scalar(out=gst[0:C, 16:24], in0=gst[0:C, 0:8], scalar1=1.0 / 64, scalar2=None, op0=ALU.mult), reads=["gst"], writes=["gst"])
                    P.op("dve", lambda e: e.tensor_tensor(out=gst[0:C, 24:32], in0=gst[0:C, 16:24], in1=gst[0:C, 16:24], op=ALU.mult), reads=["gst"], writes=["gst"])
                    P.op("dve", lambda e: e.scalar_tensor_tensor(out=gst[0:C, 32:40], in0=gst[0:C, 8:16], scalar=1.0 / 64, in1=gst[0:C, 24:32], op0=ALU.mult, op1=ALU.subtract), reads=["gst"], writes=["gst"])
                    P.op("dve", lambda e: e.tensor_scalar(out=gst[0:C, 32:40], in0=gst[0:C, 32:40], scalar1=64e-5, scalar2=None, op0=ALU.add), reads=["gst"], writes=["gst"])
                    P.op("act", lambda e: e.activation(out=gst[0:C, 40:48], in_=gst[0:C, 32:40], func=AF.Sqrt), reads=["gst"], writes=["gst"])
                    P.op("dve", lambda e: e.reciprocal(gst[0:C, 40:48], gst[0:C, 40:48]), reads=["gst"], writes=["gst"])
                    for h in range(8):
                        P.op("dve" if h % 2 == 0 else "pool", lambda e, h=h: e.tensor_scalar(out=Onb[0:C, h * 64:(h + 1) * 64], in0=Of[0:C, h * 64:(h + 1) * 64], scalar1=gst[0:C, 16 + h:17 + h], scalar2=gst[0:C, 40 + h:41 + h], op0=ALU.subtract, op1=ALU.mult), reads=["lnt", "gst"], writes=["Onb"])
                    b = nextps()
                    pbf = psb[b][:, :].bitcast(BF16)
                    for c in range(4):
                        P.tr(pbf[:, c * 64:c * 64 + C], Onb[0:C, c * 128:(c + 1) * 128], identB[0:C, 0:C], reads=["Onb", "identB"], writes=[f"ps{b}"])
                    for c in range(4):
                        P.op("act", lambda e, c=c, pbf=pbf: e.activation(out=t1[:, k0:k0 + C], in_=pbf[:, c * 64:c * 64 + C], func=AF.Identity, scale=evp[:, LG + c:LG + c + 1], bias=evp[:, LB + c:LB + c + 1]), reads=["evp"], writes=["t1c", f"ps{b}"])
                        P.op("dve", lambda e, c=c: e.tensor_tensor(out=t2[:, k0:k0 + C], in0=t1[:, k0:k0 + C], in1=bn[:, c, k0:k0 + C], op=ALU.add), reads=["t1c", "bn"], writes=["t2c"])
                        P.op("pool", lambda e, c=c: e.tensor_tensor(out=mixg[:, 4 + c, k0:k0 + C], in0=t2[:, k0:k0 + C], in1=gT[:, c, k0:k0 + C], op=ALU.mult), reads=["t2c", "gT"], writes=["mixg"])
                for ck_ in range(n // C):
                    do_chunk(ck_)
                if EVSTOP <= 8:
                    return
                save_off = arena.off
                arena.off = off_ld
                ygrp = arena.alloc([128, 8, n], F32)
                arena.off = save_off
                for dc in range(8):
                    b = nextps()
                    for kc in range(8):
                        P.mm(psb[b][:, 0:n], wo_sb[:, kc, dc * 128:(dc + 1) * 128], mixg[:, kc, :], start=(kc == 0), stop=(kc == 7), reads=["wo_sb", "mixg"], writes=[f"ps{b}"])
                    P.op("act" if dc % 2 == 0 else "dve", (lambda e, b=b, dc=dc: e.activation(out=ygrp[:, dc, :], in_=psb[b][:, 0:n], func=AF.Copy)) if dc % 2 == 0 else (lambda e, b=b, dc=dc: e.tensor_copy(ygrp[:, dc, :], psb[b][:, 0:n])), writes=["ygrp", "ld", "lp", f"ps{b}"])
                for c in range(8):
                    P.op("act", lambda e, c=c: e.activation(out=sqg[:, c, :], in_=ygrp[:, c, :], func=AF.Square), reads=["ygrp"], writes=["sqg", "E1"])
                b = nextps()
                for c in range(8):
                    P.mm(psb[b][:, 0:n], onesB[:, :], sqg[:, c, :], start=(c == 0), stop=(c == 7), reads=["sqg", "onesB"], writes=[f"ps{b}"])
                P.op("act", lambda e, b=b: e.activation(out=rstd_g[:, :], in_=psb[b][:, 0:n], func=AF.Sqrt, scale=1.0 / D, bias=epsb[:, 0:1]), reads=["epsb"], writes=["rstd_g", f"ps{b}"])
                P.op("dve", lambda e: e.reciprocal(rstd_g[:, :], rstd_g[:, :]), reads=["rstd_g"], writes=["rstd_g"])
                for c in range(8):
                    for (o0, nn, sq_) in gsegs:
                        P.op("dve", lambda e, c=c, o0=o0, nn=nn, sq_=sq_: e.scalar_tensor_tensor(out=tmpg[:, o0:o0 + nn], in0=ygrp[:, c, o0:o0 + nn], scalar=mods[:, 2, c, sq_:sq_ + 1], in1=rstd_g[:, o0:o0 + nn], op0=ALU.mult, op1=ALU.mult), reads=["ygrp", "mods", "rstd_g"], writes=["tmpg"])
                        P.op("pool", lambda e, c=c, o0=o0, nn=nn: e.tensor_tensor(out=xT[:, c, c0 + o0:c0 + o0 + nn], in0=xT[:, c, c0 + o0:c0 + o0 + nn], in1=tmpg[:, o0:o0 + nn], op=ALU.add), reads=["tmpg", "xT"], writes=["xT"])
            for gi_, gg in enumerate(groups):
                if EVGROUPS and str(gi_) not in EVGROUPS.split(","):
                    continue
                if EVSTOP <= 1:
                    continue
                do_group(gi_, *gg)
            phase()

        for l in range(nlayers):
            pieces = [(ada_w[l].rearrange("(kc p) f -> p kc f", p=128)[:, :, pc * 512:(pc + 1) * 512], (8, 512)) for pc in range(12)]
            ws = WStream(pieces)
            b = nextps()
            for pc in range(12):
                wsl, wres = ws.get()
                for j in range(4):
                    jj = pc * 4 + j
                    for kc in range(8):
                        P.mm(psb[b][:, jj * NSEQ:(jj + 1) * NSEQ], wsl[:, kc, j * 128:(j + 1) * 128], siluT[:, kc, :], start=(kc == 0), stop=(kc == 7), reads=[wres, "siluT"], writes=[f"ps{b}"])
            psv = psb[b][:, 0:48 * NSEQ].rearrange("p (j s) -> p j s", s=NSEQ)
            for s in range(NSEQ):
                P.op("dve", lambda e, s=s, psv=psv, l=l: e.tensor_tensor(out=mod[:, :, s], in0=psv[:, :, s], in1=adab[:, l * 48:(l + 1) * 48], op=ALU.add), reads=["adab"], writes=["mod", f"ps{b}"])
            for half, (kpre, kpost) in enumerate(((0, 1), (2, 3))):
                base = half * 24
                for s in range(NSEQ):
                    P.op("dve", lambda e, s=s, base=base, half=half, kpre=kpre, l=l: e.scalar_tensor_tensor(out=mods[:, half * 3 + 0, :, s], in0=mod[:, base + 8:base + 16, s], scalar=1.0, in1=npar[:, kpre, l * 8:(l + 1) * 8], op0=ALU.add, op1=ALU.mult), reads=["mod", "npar"], writes=["mods"])
                    P.op("dve", lambda e, s=s, base=base, half=half: e.tensor_copy(mods[:, half * 3 + 1, :, s], mod[:, base:base + 8, s]), reads=["mod"], writes=["mods"])
                    P.op("dve", lambda e, s=s, base=base, half=half, kpost=kpost, l=l: e.tensor_tensor(out=mods[:, half * 3 + 2, :, s], in0=mod[:, base + 16:base + 24, s], in1=npar[:, kpost, l * 8:(l + 1) * 8], op=ALU.mult), reads=["mod", "npar"], writes=["mods"])

            if l % 2 == 1 and mix_odd:
                odd_mixer(l // 2)
            if l % 2 == 0 and mix_even:
                even_mixer(l // 2)

            barrier()
            arena.reset()
            norm_stats(xT, "xT", TGROUPS)
            modulate(3, 4)
            halves = [[(0, 512), (512, 512)], [(1024, 512), (1536, 512), (2048, 32)]]
            for hi, groups in enumerate(halves):
                barrier()
                arena.reset()
                hc0 = groups[0][0]
                hn = sum(g[1] for g in groups)
                yacc = arena.alloc([128, 8, 1056], F32)
                uT = arena.alloc([128, 4, 512], BF16)
                rl2 = arena.alloc([128, 2, 512], BF16)
                rlc = [0]
                pieces = []
                for fg in range(8):
                    pieces.append((w1[l].rearrange("(kc p) f -> p kc f", p=128)[:, :, fg * 512:(fg + 1) * 512], (8, 512)))
                    pieces.append((w2[l][fg * 512:(fg + 1) * 512, :].rearrange("(j p) d -> p j d", p=128), (4, 1024)))
                ws = WStream(pieces)
                for fg in range(8):
                    w1g, w1res = ws.get()
                    w2g, w2res = ws.get()
                    for (c0, n) in groups:
                        for j in range(4):
                            b = nextps()
                            for kc in range(8):
                                P.mm(psb[b][:, 0:n], w1g[:, kc, j * 128:(j + 1) * 128], hT[:, kc, c0:c0 + n], start=(kc == 0), stop=(kc == 7), reads=[w1res, "hT"], writes=[f"ps{b}"])
                            ri = rlc[0] % 2
                            rlc[0] += 1
                            rl = rl2[:, ri, :]
                            P.op("act", lambda e, b=b, n=n, rl=rl: e.activation(out=rl[:, 0:n], in_=psb[b][:, 0:n], func=AF.Relu), writes=[f"rl{ri}", f"ps{b}"])
                            P.op("pool", lambda e, j=j, n=n, rl=rl: e.tensor_tensor(out=uT[:, j, 0:n], in0=rl[:, 0:n], in1=rl[:, 0:n], op=ALU.mult), reads=[f"rl{ri}"], writes=[f"uT{j}"])
                        for dc in range(8):
                            b = nextps()
                            for j in range(4):
                                P.mm(psb[b][:, 0:n], w2g[:, j, dc * 128:(dc + 1) * 128], uT[:, j, 0:n], start=(j == 0), stop=(j == 3), reads=[w2res, f"uT{j}"], writes=[f"ps{b}"])
                            ydst = yacc[:, dc, c0 - hc0:c0 - hc0 + n]
                            if fg == 0:
                                P.op("act", lambda e, b=b, n=n, ydst=ydst: e.activation(out=ydst, in_=psb[b][:, 0:n], func=AF.Copy), writes=["yacc", f"ps{b}"])
                            else:
                                P.op("dve", lambda e, b=b, n=n, ydst=ydst: e.tensor_tensor(out=ydst, in0=psb[b][:, 0:n], in1=ydst, op=ALU.add), reads=["yacc"], writes=["yacc", f"ps{b}"])
                norm_stats(yacc, "yacc", groups, col_off=hc0)
                segs = [sg for sg in SEGS if sg[0] >= hc0 and sg[0] < hc0 + hn]
                if hi == 0:
                    segs = [(0, 1024, 0)]
                else:
                    segs = [(1024, 1024, 0)] + SEGS[1:]
                residual(yacc, "yacc", 5, segs, col_off=hc0)

        barrier()
        arena.reset()

        def store_y(dst, nrows, col0):
            ntile = (nrows + 127) // 128
            for tt in range(ntile):
                r = min(128, nrows - tt * 128)
                if yo_keep[0] is None:
                    yo_keep[0] = arena.alloc([128, 2, D], F32)
                yo = yo_keep[0][:, tt % 2, :]
                yres = f"yo{tt % 2}"
                for hb in range(2):
                    b = nextps()
                    for q in range(4):
                        c = hb * 4 + q
                        P.tr(psb[b][0:r, q * 128:(q + 1) * 128], xT[:, c, col0 + tt * 128:col0 + tt * 128 + r], identF[:, :], reads=["xT", "identF"], writes=[f"ps{b}"])
                    P.op("dve" if hb == 0 else "act",
                         (lambda e, b=b, r=r, hb=hb, yo=yo: e.tensor_copy(yo[0:r, hb * 512:(hb + 1) * 512], psb[b][0:r, :])) if hb == 0 else
                         (lambda e, b=b, r=r, hb=hb, yo=yo: e.activation(out=yo[0:r, hb * 512:(hb + 1) * 512], in_=psb[b][0:r, :], func=AF.Copy)),
                         writes=[yres, f"ps{b}"])
                dma("sp", dst[tt * 128:tt * 128 + r, :], yo[0:r, :], reads=[yres])
        yo_keep = [None]
        store_y(y_p, T, 0)
        store_y(y_s, TS, T)
        with nc.allow_low_precision("bf16 matmul operands by design"):
            P.emit()
    return nc


def shard_inputs(inputs, b):
    f = lambda a: np.ascontiguousarray(a, dtype=np.float32)
    m = {
        "xp": f(inputs["x_prompt"][b]), "xs": f(inputs["x_sample"][4 * b:4 * b + 4].reshape(TS, D)),
        "swkv": f(inputs["state_wkv"][:, 4 * b:4 * b + 4]), "sshift": f(inputs["state_shift"][:, 4 * b:4 * b + 4]),
        "ck": f(inputs["cache_k"][:, 4 * b:4 * b + 4].reshape(2, 4, 2048, D)), "cv": f(inputs["cache_v"][:, 4 * b:4 * b + 4].reshape(2, 4, 2048, D)),
        "cc": f(np.concatenate([inputs["c_prompt"][b:b + 1], inputs["c_sample"][4 * b:4 * b + 4]], axis=0)),
    }
    for n in ("ada_w", "ada_b", "norm_mix_pre", "norm_mix_post", "norm_ffn_pre", "norm_ffn_post", "ffn_w1", "ffn_w2", "ev_w_in", "ev_w_out",
              "gm_ln_g", "gm_ln_b", "gm_ws", "gm_bs", "rw_mu", "rw_w0", "rw_w2", "rw_a0", "rw_a2", "rw_g2", "rw_kk", "rw_ka", "rw_lnx_g", "rw_lnx_b",
              "od_w_qkv", "od_w_out"):
        m[n] = f(inputs[n])
    m["rw_rk"] = f(inputs["rw_rk"].reshape(2, 512))
    m.update(host_consts())
    return m


def host_consts():
    import numpy as np
    f32 = np.float32
    half = 8
    inv = (np.float32(500000.0) ** (-np.arange(half, dtype=np.float32) * np.float32(2.0 / 16))).astype(f32)
    pos = np.concatenate([np.arange(T), np.tile(8192 + np.arange(8), 4)]).astype(f32)
    ang = pos[:, None] * inv[None, :]
    cos = np.cos(ang).astype(f32); sin = np.sin(ang).astype(f32)
    rope = np.zeros((2, 128, TT), f32)
    perm = np.zeros((128, 128), f32)
    for p in range(128):
        e = p % 64
        if e < 8:
            rope[0, p] = cos[:, e]; rope[1, p] = -sin[:, e]; perm[p + 8, p] = 1.0
        elif e < 16:
            rope[0, p] = cos[:, e - 8]; rope[1, p] = sin[:, e - 8]; perm[p - 8, p] = 1.0
        else:
            rope[0, p] = 1.0
    jj = np.arange(128)[:, None]; ii = np.arange(128)[None, :]
    cur = (jj <= ii).astype(f32); prev = (jj >= ii).astype(f32)
    cmask = np.concatenate([cur, prev, cur], axis=1)
    m3 = np.zeros((128, 4, 16, 32), f32)
    for G in range(4):
        m3[:, G, :, :] = (np.arange(128)[:, None, None] <= (32 * G + np.arange(32))[None, None, :])
    m3 = m3.reshape(128, 2048)
    c = (np.arange(16)[None, :, None] * 128 + np.arange(128)[:, None, None])
    t = np.arange(8)[None, None, :]
    dd = 2048 + t - c
    sm = ((dd <= 128).astype(f32) + ((dd % 4 == 0) & (dd <= 512)).astype(f32) + ((dd % 16 == 0) & (dd <= 2048)).astype(f32))
    smask = np.concatenate([sm.reshape(128, 128), sm.reshape(128, 128)], axis=1)
    tp = np.arange(8)[:, None]; tq = np.arange(8)[None, :]
    nm = (tp <= tq).astype(f32) + 2.0 * (tp == tq) + (tp == tq - 4)
    nmask = np.concatenate([nm, nm], axis=1).astype(f32)
    t_ = np.arange(64)
    su = (t_[:, None] < t_[None, :]).astype(f32); ui = (t_[:, None] <= t_[None, :]).astype(f32); sl = (t_[:, None] > t_[None, :]).astype(f32)
    ev = np.zeros((128, 1664), f32)
    ev[0:CP, 0:8 * CP] = np.tile(su[:CP, :CP], (1, 8)); ev[0:CP, 512:512 + 8 * CP] = np.tile(ui[:CP, :CP], (1, 8)); ev[0:CP, 1024:1024 + 8 * CP] = np.tile(sl[:CP, :CP], (1, 8))
    blk = np.zeros((128, 128), f32); blk[0:64, 0:64] = 1; blk[64:, 64:] = 1
    ev[:, 1536:1664] = blk
    idr = np.zeros((64, 512), f32); idr[0:CP, 0:8 * CP] = np.tile(np.eye(CP, dtype=f32), (1, 8))
    m8 = np.concatenate([np.tile(su[:8, :8], (1, 8)), np.tile(ui[:8, :8], (1, 8)), np.tile(sl[:8, :8], (1, 8)), np.tile(np.eye(8, dtype=f32), (1, 8))], axis=1)
    return {"kc_ev": ev, "kc_idr": idr, "kc_m8": m8.astype(f32), "kc_rope": rope, "kc_perm": perm, "kc_cmask": cmask.astype(f32), "kc_m3": m3, "kc_smask": smask.astype(f32), "kc_nmask": nmask}


_NC_CACHE = {}


def kernel(**inputs):
    n = 8
    if "nc" not in _NC_CACHE:
        _NC_CACHE["nc"] = build(nlayers=4, mix_even=True, mix_odd=True, dbg=False)
    nc = _NC_CACHE["nc"]
    inputs = {k: np.asarray(v) for k, v in inputs.items()}
    in_maps = [shard_inputs(inputs, b) for b in range(n)]
    res = run_bass_kernel_spmd(nc, in_maps, core_ids=list(range(n)))
    R = res.results
    f = np.float32
    y_prompt = np.stack([R[b]["y_p"] for b in range(n)], axis=0).astype(f)
    y_sample = np.concatenate([R[b]["y_s"].reshape(4, 8, D) for b in range(n)], axis=0).astype(f)
    wkv_prompt = np.stack([R[b]["wkv_p"] for b in range(n)], axis=1).astype(f)
    shift_prompt = np.stack([R[b]["shift_p"] for b in range(n)], axis=1).astype(f)
    k_prompt = np.stack([R[b]["k_p"].reshape(2, T, 16, 64) for b in range(n)], axis=1).astype(f)
    v_prompt = np.stack([R[b]["v_p"].reshape(2, T, 16, 64) for b in range(n)], axis=1).astype(f)
    wkv_sample = np.concatenate([R[b]["wkv_s"] for b in range(n)], axis=1).astype(f)
    shift_sample = np.concatenate([R[b]["shift_s"] for b in range(n)], axis=1).astype(f)
    k_sample = np.concatenate([R[b]["k_s"].reshape(2, 4, 8, 16, 64) for b in range(n)], axis=1).astype(f)
    v_sample = np.concatenate([R[b]["v_s"].reshape(2, 4, 8, 16, 64) for b in range(n)], axis=1).astype(f)
    gmlp_v_sample = np.concatenate([R[b]["gv_s"].reshape(2, 4, 8, 512) for b in range(n)], axis=1).astype(f)
    return (y_prompt, y_sample, wkv_prompt, shift_prompt, k_prompt, v_prompt, wkv_sample, shift_sample, k_sample, v_sample, gmlp_v_sample)
```

```python
import numpy as np
import contextlib, os
import concourse.bass as bass
import concourse.mybir as mybir

F32 = mybir.dt.float32
BF16 = mybir.dt.bfloat16
I32 = mybir.dt.int32
AF = mybir.ActivationFunctionType
ALU = mybir.AluOpType
AX = mybir.AxisListType

ENGS = ("pe", "act", "dve", "pool", "sp")
NDMA_SEM = 12


class Op:
    __slots__ = ("eng", "fn", "deps", "is_dma", "sig", "token", "idx")

    def __init__(self, eng, fn, is_dma):
        self.eng = eng
        self.fn = fn
        self.deps = []
        self.is_dma = is_dma
        self.sig = False
        self.token = None
        self.idx = -1


class Prog:
    def __init__(self, nc):
        self.nc = nc
        self.ops = {e: [] for e in ENGS}
        self.res = {}
        self.nops = 0

    def op(self, eng, fn, reads=(), writes=(), dma=False):
        o = Op(eng, fn, dma)
        o.idx = len(self.ops[eng])
        deps = {}
        for r in reads:
            st = self.res.get(r)
            if st is not None and st[0] is not None:
                deps[id(st[0])] = st[0]
        for w in writes:
            st = self.res.get(w)
            if st is not None:
                if st[0] is not None:
                    deps[id(st[0])] = st[0]
                for rd in st[1]:
                    deps[id(rd)] = rd
        for d in deps.values():
            if d is o:
                continue
            if (not d.is_dma) and d.eng == "pe" and eng == "pe" and not dma:
                continue
            d.sig = True
            o.deps.append(d)
        for r in reads:
            st = self.res.setdefault(r, [None, []])
            st[1].append(o)
        for w in writes:
            self.res[w] = [o, []]
        self.ops[eng].append(o)
        self.nops += 1
        return o

    def mm(self, out, lhsT, rhs, start=True, stop=True, reads=(), writes=(), **kw):
        return self.op("pe", lambda e: e.matmul(out, lhsT, rhs, start=start, stop=stop, **kw), reads, writes)

    def tr(self, out, in_, ident, reads=(), writes=()):
        return self.op("pe", lambda e: e.transpose(out, in_, ident), reads, writes)

    def dma(self, eng, out, in_, reads=(), writes=(), **kw):
        return self.op(eng, lambda e: e.dma_start(out=out, in_=in_, **kw), reads, writes, dma=True)

    def emit(self, final_wait=True):
        nc = self.nc
        import contextlib
        with contextlib.ExitStack() as es:
            sems = {e: es.enter_context(nc.semaphore("s_" + e)) for e in ENGS}
            dsems = {}
            for q in ("sp", "pool", "act"):
                dsems[q] = [es.enter_context(nc.semaphore(f"d_{q}{k}")) for k in range(NDMA_SEM)]
            for e in ENGS:
                cnt = 0
                dcnt = 0
                duse = [0] * NDMA_SEM
                dlast = [None] * NDMA_SEM
                for o in self.ops[e]:
                    if o.is_dma:
                        k = dcnt % NDMA_SEM
                        dcnt += 1
                        duse[k] += 1
                        if dlast[k] is not None:
                            o.deps.append(dlast[k])
                        dlast[k] = o
                        o.token = (dsems[e][k], 16 * duse[k])
                        o.sig = True
                    elif o.sig:
                        cnt += 1
                        o.token = (sems[e], cnt)
                self_last_dma = dlast
                setattr(self, "_dlast_" + e, [d for d in dlast if d is not None])
            blk = es.enter_context(nc.Block())

            def run(engname):
                def body(eng):
                    known = {}
                    for o in self.ops[engname]:
                        for d in o.deps:
                            s, v = d.token
                            key = id(s)
                            if known.get(key, 0) < v:
                                eng.wait_ge(s, v)
                                known[key] = v
                        ins = o.fn(eng)
                        if o.sig:
                            s, v = o.token
                            ins.then_inc(s, 16 if o.is_dma else 1)
                    if final_wait:
                        for d in getattr(self, "_dlast_" + engname):
                            s, v = d.token
                            if known.get(id(s), 0) < v:
                                eng.wait_ge(s, v)
                                known[id(s)] = v
                return body

            blk.tensor(run("pe"))
            blk.scalar(run("act"))
            blk.vector(run("dve"))
            blk.gpsimd(run("pool"))
            blk.sync(run("sp"))

from concourse.bass_utils import run_bass_kernel_spmd
EVSTOP = float(os.environ.get('EVSTOP', '99'))
EVGROUPS = os.environ.get('EVGROUPS', '')
CP = 32

D = 1024
T = 2048
TS = 32
TT = T + TS
NSEQ = 5
DFF = 4096
EVEN_IN = 2688
BCOLS = 1664
NORM_EPS = 1e-6
SEGS = [(0, T, 0)] + [(T + 8 * s, 8, 1 + s) for s in range(4)]
TGROUPS = [(0, 512), (512, 512), (1024, 512), (1536, 512), (2048, 32)]


class Arena:
    def __init__(self, ap_f32, width):
        self.ap = ap_f32
        self.width = width
        self.off = 0

    def reset(self, off=0):
        self.off = off

    def alloc(self, shape, dt):
        n = int(np.prod(shape[1:]))
        nw = n if dt == F32 else (n + 1) // 2
        assert self.off + nw <= self.width, ("arena overflow", self.off, nw, self.width)
        a = self.ap[0:shape[0], self.off:self.off + nw]
        self.off += nw
        if dt != F32:
            a = a.bitcast(dt)[:, 0:n]
        if len(shape) == 3:
            a = a.rearrange("p (a b) -> p a b", a=shape[1])
        elif len(shape) == 4:
            a = a.rearrange("p (a b c) -> p a b c", a=shape[1], b=shape[2])
        return a


def build(nlayers=4, mix_even=True, mix_odd=True, dbg=False):
    nc = bass.Bass("TRN2", target_bir_lowering=False)
    din = lambda name, shape: nc.dram_tensor(name, shape, F32, kind="ExternalInput").ap()
    dout = lambda name, shape: nc.dram_tensor(name, shape, F32, kind="ExternalOutput").ap()
    xp = din("xp", [T, D]); xs = din("xs", [TS, D])
    swkv = din("swkv", [2, 4, 8, 64, 64]); sshift = din("sshift", [2, 4, BCOLS])
    ck = din("ck", [2, 4, 2048, D]); cv = din("cv", [2, 4, 2048, D])
    cc = din("cc", [NSEQ, D])
    ada_w = din("ada_w", [4, D, 6 * D]); ada_b = din("ada_b", [4, 6 * D])
    npar_d = [din(n, [4, D]) for n in ("norm_mix_pre", "norm_mix_post", "norm_ffn_pre", "norm_ffn_post")]
    w1 = din("ffn_w1", [4, D, DFF]); w2 = din("ffn_w2", [4, DFF, D])
    ev_w_in = din("ev_w_in", [2, D, EVEN_IN]); ev_w_out = din("ev_w_out", [2, D, D])
    gm_ln_g = din("gm_ln_g", [2, 512]); gm_ln_b = din("gm_ln_b", [2, 512])
    gm_ws = din("gm_ws", [2, 4, 128, 128]); gm_bs = din("gm_bs", [2, 4, 128])
    rw_mu = din("rw_mu", [2, BCOLS]); rw_w0 = din("rw_w0", [2, 512]); rw_w2 = din("rw_w2", [2, 32, 512])
    rw_a0 = din("rw_a0", [2, 512]); rw_a2 = din("rw_a2", [2, 32, 512]); rw_g2 = din("rw_g2", [2, 64, 512])
    rw_kk = din("rw_kk", [2, 512]); rw_ka = din("rw_ka", [2, 512]); rw_rk = din("rw_rk", [2, 512])
    rw_lnx_g = din("rw_lnx_g", [2, 512]); rw_lnx_b = din("rw_lnx_b", [2, 512])
    od_w_qkv = din("od_w_qkv", [2, D, 3 * D]); od_w_out = din("od_w_out", [2, D, D])
    kc_rope = din("kc_rope", [2, 128, TT]); kc_perm = din("kc_perm", [128, 128]); kc_cmask = din("kc_cmask", [128, 384])
    kc_ev = din("kc_ev", [128, 1664]); kc_idr = din("kc_idr", [64, 512]); kc_m8 = din("kc_m8", [8, 256])
    kc_m3 = din("kc_m3", [128, 2048]); kc_smask = din("kc_smask", [128, 256]); kc_nmask = din("kc_nmask", [8, 16])
    y_p = dout("y_p", [T, D]); y_s = dout("y_s", [TS, D])
    wkv_p = dout("wkv_p", [2, 8, 64, 64]); shift_p = dout("shift_p", [2, BCOLS])
    k_p = dout("k_p", [2, T, D]); v_p = dout("v_p", [2, T, D])
    wkv_s = dout("wkv_s", [2, 4, 8, 64, 64]); shift_s = dout("shift_s", [2, 4, BCOLS])
    k_s = dout("k_s", [2, TS, D]); v_s = dout("v_s", [2, TS, D])
    gv_s = dout("gv_s", [2, TS, 512])
    dbg_o = dout("dbg", [128, 8 * TT]) if dbg else None

    P = Prog(nc)
    with contextlib.ExitStack() as es:
        sb = lambda name, shape, dt: es.enter_context(nc.sbuf_tensor(name, shape, dt))
        xT = sb("xT", [128, 8, TT], F32)
        hT = sb("hT", [128, 8, TT], BF16)
        rstd_h = [None]
        identF = sb("identF", [128, 128], F32)
        identB = sb("identB", [128, 128], BF16)
        onesB = sb("onesB", [128, 128], BF16)
        zerB = sb("zerB", [128, 512], BF16)
        siluT = sb("siluT", [128, 8, NSEQ], BF16)
        npar = sb("npar", [128, 4, 32], F32)
        adab = sb("adab", [128, 192], F32)
        mod = sb("mod", [128, 48, NSEQ], F32)
        mods = sb("mods", [128, 6, 8, NSEQ], F32)
        epsb = sb("epsb", [128, 1], F32)
        epsb2 = sb("epsb2", [128, 1], F32)
        permT = sb("permT", [128, 128], BF16)
        cmask = sb("cmask", [128, 384], BF16)
        smask = sb("smask", [128, 256], BF16)
        nmask = sb("nmask", [8, 16], BF16)
        RING = 3
        wring = sb("wring", [128, RING, 4096], BF16)
        AW = 20300
        arena_t = sb("arena", [128, AW], F32)
        arena = Arena(arena_t, AW)
        psb = [es.enter_context(nc.psum_tensor(f"ps{i}", [128, 512], F32)) for i in range(8)]
        pscnt = [0]

        def nextps():
            i = pscnt[0] % 6
            pscnt[0] += 1
            return i

        acccnt = [0]

        def accps():
            i = 6 + acccnt[0] % 2
            acccnt[0] += 1
            return i

        wstate = {"n": 0}

        def wload(src3, shape):
            slot = wstate["n"] % RING
            wstate["n"] += 1
            a, b = shape
            dst = wring[:, slot, 0:a * b].rearrange("p (a b) -> p a b", a=a)
            if isinstance(src3, list):
                nt = len(src3)
                dv = wring[:, slot, 0:a * b].rearrange("p (a t f) -> p a t f", a=a, t=nt)
                for ti, sx in enumerate(src3):
                    P.dma("pool", dv[:, :, ti, :], sx, writes=[f"wr{slot}"])
            else:
                P.dma("pool", dst, src3, writes=[f"wr{slot}"])
            return dst, f"wr{slot}"

        class WStream:
            def __init__(self, pieces, depth=RING - 1):
                self.pieces = pieces
                self.loaded = []
                self.i = 0
                self.depth = depth

            def get(self):
                while len(self.loaded) < min(len(self.pieces), self.i + self.depth):
                    s, sh = self.pieces[len(self.loaded)]
                    self.loaded.append(wload(s, sh))
                r = self.loaded[self.i]
                self.i += 1
                return r

        pend_dma = []
        bar_t = sb("bar_t", [128, 8], F32)
        barcnt = [0]

        def dma(eng, out, in_, reads=(), writes=(), **kw):
            o = P.dma(eng, out, in_, reads=reads, writes=writes, **kw)
            return o

        def barrier():
            n = barcnt[0]
            barcnt[0] += 1
            allres = list(P.res.keys())
            P.op("dve", lambda e: e.memset(bar_t[:, 0:1], 0.0), reads=[], writes=allres + ["bar"])
            P.op("act", lambda e: e.memzero(bar_t[:, 1:2]), reads=["bar"], writes=["bar_act"])
            P.op("pool", lambda e: e.memset(bar_t[:, 2:3], 0.0), reads=["bar"], writes=["bar_pool"])
            P.mm(psb[7][0:1, 0:1], zerB[0:1, 0:1], zerB[0:1, 0:1], reads=["bar", "zerB"], writes=["bar_pe", "ps7"])
            P.op("sp", lambda e: e.nop(), reads=["bar"], writes=["bar_sp"])
            P.op("dve", lambda e: e.memset(bar_t[:, 3:4], 0.0), reads=["bar_act", "bar_pool", "bar_pe", "bar_sp"], writes=["bar2"])
            P.op("act", lambda e: e.memzero(bar_t[:, 4:5]), reads=["bar2"], writes=["bar3_act"])
            P.op("pool", lambda e: e.memset(bar_t[:, 5:6], 0.0), reads=["bar2"], writes=["bar3_pool"])
            P.mm(psb[7][0:1, 0:1], zerB[0:1, 0:1], zerB[0:1, 0:1], reads=["bar2", "zerB"], writes=["bar3_pe", "ps7"])
            P.op("sp", lambda e: e.nop(), reads=["bar2"], writes=["bar3_sp"])

        def phase(off=0):
            barrier()
            arena.reset(off)
            lrt_keep[0] = None

        lrt_keep = [None]
        P.op("pool", lambda e: e.memset(identF[:], 0.0), writes=["identF"])
        P.op("pool", lambda e: e.affine_select(out=identF[:], in_=identF[:], pattern=[[-1, 128]], compare_op=ALU.not_equal, fill=1.0, base=0, channel_multiplier=1), reads=["identF"], writes=["identF"])
        P.op("pool", lambda e: e.tensor_copy(identB[:], identF[:]), reads=["identF"], writes=["identB"])
        P.op("pool", lambda e: e.memset(onesB[:], 1.0), writes=["onesB"])
        P.op("pool", lambda e: e.memset(zerB[:], 0.0), writes=["zerB"])
        P.op("pool", lambda e: e.memset(epsb[:], NORM_EPS), writes=["epsb"])
        P.op("pool", lambda e: e.memset(epsb2[:], 1e-12), writes=["epsb"])
        dma("pool", permT[:], kc_perm, writes=["permT"])
        dma("pool", cmask[:], kc_cmask, writes=["cmask"])
        dma("pool", smask[:], kc_smask, writes=["smask"])
        dma("pool", nmask[:], kc_nmask, writes=["nmask"])

        def load_rows_T(src_rows, R, dst, dstres):
            if lrt_keep[0] is None:
                lrt_keep[0] = arena.alloc([128, 128], F32)
            st = lrt_keep[0]
            dma("sp", st[0:R, :], src_rows, writes=["lrt_st"])
            b = nextps()
            P.tr(psb[b][:, 0:R], st[0:R, :], identF[0:R, 0:R], reads=["lrt_st", "identF"], writes=[f"ps{b}"])
            P.op("dve", lambda e: e.tensor_copy(dst, psb[b][:, 0:R]), writes=[dstres, f"ps{b}"])

        for kind in range(4):
            load_rows_T(npar_d[kind].rearrange("l (c p) -> (l c) p", p=128), 32, npar[:, kind, :], "npar")
        ab = ada_b.rearrange("l (c p) -> (l c) p", p=128)
        load_rows_T(ab[0:96], 96, adab[:, 0:96], "adab")
        load_rows_T(ab[96:192], 96, adab[:, 96:192], "adab")
        cT = arena.alloc([128, 8, NSEQ], F32)
        st5 = arena.alloc([128, D], F32)
        dma("sp", st5[0:NSEQ, :], cc, writes=["st5"])
        for c in range(8):
            b = nextps()
            P.tr(psb[b][:, 0:NSEQ], st5[0:NSEQ, c * 128:(c + 1) * 128], identF[0:NSEQ, 0:NSEQ], reads=["st5", "identF"], writes=[f"ps{b}"])
            P.op("act", lambda e, b=b, c=c: e.activation(out=siluT[:, c, :], in_=psb[b][:, 0:NSEQ], func=AF.Silu), writes=["siluT", f"ps{b}"])
        barrier()
        arena.reset()

        def load_x(src, nrows, col0):
            xin2 = arena.alloc([128, 2, D], F32)
            ntile = (nrows + 127) // 128
            for tt in range(ntile):
                r = min(128, nrows - tt * 128)
                xin = xin2[:, tt % 2, :]
                xres = f"xin{tt % 2}"
                dma("sp", xin[0:r, :], src[tt * 128:tt * 128 + r, :], writes=[xres])
                for hb in range(2):
                    b = nextps()
                    for q in range(4):
                        c = hb * 4 + q
                        P.tr(psb[b][:, q * 128:q * 128 + r], xin[0:r, c * 128:(c + 1) * 128], identF[0:r, 0:r], reads=[xres, "identF"], writes=[f"ps{b}"])
                    src_ps = psb[b][:, :].rearrange("p (q t) -> p q t", q=4)[:, :, 0:r]
                    dst = xT[:, hb * 4:hb * 4 + 4, col0 + tt * 128:col0 + tt * 128 + r]
                    P.op("dve" if hb == 0 else "act",
                         (lambda e, dst=dst, src_ps=src_ps: e.tensor_copy(dst, src_ps)) if hb == 0 else
                         (lambda e, dst=dst, src_ps=src_ps: e.activation(out=dst, in_=src_ps, func=AF.Copy)),
                         writes=["xT", f"ps{b}"])
            barrier()
            arena.reset()

        load_x(xp, T, 0)
        load_x(xs, TS, T)

        def norm_stats(src, src_res, cols_list, col_off=0, sqw=512):
            ncols_tot = sum(n for _, n in cols_list)
            rstd_h[0] = (arena.alloc([128, ncols_tot], F32), col_off)
            rstd = rstd_h[0][0]
            sq = arena.alloc([128, 8, sqw], BF16)
            for (c0, n) in cols_list:
                for c in range(8):
                    P.op("act", lambda e, c=c, c0=c0, n=n: e.activation(out=sq[:, c, 0:n], in_=src[:, c, c0 - col_off:c0 - col_off + n], func=AF.Square), reads=[src_res], writes=["sq"])
                b = nextps()
                for c in range(8):
                    P.mm(psb[b][:, 0:n], onesB[:, :], sq[:, c, 0:n], start=(c == 0), stop=(c == 7), reads=["sq", "onesB"], writes=[f"ps{b}"])
                P.op("act", lambda e, b=b, c0=c0, n=n: e.activation(out=rstd[:, c0 - col_off:c0 - col_off + n], in_=psb[b][:, 0:n], func=AF.Sqrt, scale=1.0 / D, bias=epsb[:, 0:1]), reads=["epsb"], writes=["rstd", f"ps{b}"])
                P.op("dve", lambda e, c0=c0, n=n: e.reciprocal(rstd[:, c0 - col_off:c0 - col_off + n], rstd[:, c0 - col_off:c0 - col_off + n]), reads=["rstd"], writes=["rstd"])

        def modulate(ia, ib):
            rstd, ro = rstd_h[0]
            assert ro == 0
            tmp = arena.alloc([128, T], F32)
            for c in range(8):
                for (c0, n, s) in SEGS:
                    P.op("dve", lambda e, c=c, c0=c0, n=n, s=s: e.scalar_tensor_tensor(out=tmp[:, 0:n], in0=xT[:, c, c0:c0 + n], scalar=mods[:, ia, c, s:s + 1], in1=rstd[:, c0:c0 + n], op0=ALU.mult, op1=ALU.mult), reads=["xT", "mods", "rstd"], writes=["tmp"])
                    P.op("act", lambda e, c=c, c0=c0, n=n, s=s: e.activation(out=hT[:, c, c0:c0 + n], in_=tmp[:, 0:n], func=AF.Identity, bias=mods[:, ib, c, s:s + 1]), reads=["tmp", "mods"], writes=["hT"])

        def residual(ysrc, yres, ig, segs, col_off=0):
            rstd, ro = rstd_h[0]
            assert ro == col_off
            tmp = arena.alloc([128, max(n for _, n, _ in segs)], F32)
            for c in range(8):
                for (c0, n, s) in segs:
                    P.op("dve", lambda e, c=c, c0=c0, n=n, s=s: e.scalar_tensor_tensor(out=tmp[:, 0:n], in0=ysrc[:, c, c0 - col_off:c0 - col_off + n], scalar=mods[:, ig, c, s:s + 1], in1=rstd[:, c0 - col_off:c0 - col_off + n], op0=ALU.mult, op1=ALU.mult), reads=[yres, "mods", "rstd"], writes=["tmp"])
                    P.op("pool", lambda e, c=c, c0=c0, n=n: e.tensor_tensor(out=xT[:, c, c0:c0 + n], in0=xT[:, c, c0:c0 + n], in1=tmp[:, 0:n], op=ALU.add), reads=["tmp", "xT"], writes=["xT"])

        def out_proj_and_residual(oT, wsrc, tag):
            pieces = [(wsrc.rearrange("(kc p) f -> p kc f", p=128)[:, :, pc * 512:(pc + 1) * 512], (8, 512)) for pc in range(2)]
            ws = WStream(pieces, depth=2)
            wo = [ws.get(), ws.get()]
            ygrp = arena.alloc([128, 8, 512], F32)
            for (c0, n) in TGROUPS:
                for dc in range(8):
                    wsl, wres = wo[dc // 4]
                    b = nextps()
                    for kc in range(8):
                        P.mm(psb[b][:, 0:n], wsl[:, kc, (dc % 4) * 128:(dc % 4 + 1) * 128], oT[:, kc, c0:c0 + n], start=(kc == 0), stop=(kc == 7), reads=[wres, tag], writes=[f"ps{b}"])
                    if dc % 2 == 0:
                        P.op("act", lambda e, b=b, n=n, dc=dc: e.activation(out=ygrp[:, dc, 0:n], in_=psb[b][:, 0:n], func=AF.Copy), writes=["ygrp", f"ps{b}"])
                    else:
                        P.op("dve", lambda e, b=b, n=n, dc=dc: e.tensor_copy(ygrp[:, dc, 0:n], psb[b][:, 0:n]), writes=["ygrp", f"ps{b}"])
                save = arena.off
                norm_stats(ygrp, "ygrp", [(c0, n)], col_off=c0)
                if c0 < T:
                    segs = [(c0, n, 0)]
                else:
                    segs = SEGS[1:]
                residual(ygrp, "ygrp", 2, segs, col_off=c0)
                arena.off = save

        def odd_mixer(o_):
            phase()
            norm_stats(xT, "xT", TGROUPS)
            modulate(0, 1)
            phase()
            oT = arena.alloc([128, 8, TT], BF16)
            keep_off = arena.off
            Ctab = arena.alloc([128, TT], BF16)
            Stab = arena.alloc([128, TT], BF16)
            m3 = arena.alloc([128, 4, 512], BF16)
            dma("pool", Ctab, kc_rope[0], writes=["Ctab"])
            dma("pool", Stab, kc_rope[1], writes=["Stab"])
            dma("pool", m3, kc_m3.rearrange("p (g c) -> p g c", g=4), writes=["m3"])
            QK = arena.alloc([128, 2, TT], BF16)
            qraw = arena.alloc([128, 2, 512], BF16)
            rt1 = arena.alloc([128, 512], BF16)
            rt2 = arena.alloc([128, 512], BF16)
            off_v3 = arena.off
            V3 = arena.alloc([128, 3, 16, 2 * 65], BF16)
            off_end = arena.off
            arena.off = off_v3
            Kc = arena.alloc([128, 16, 128], BF16)
            Vc = arena.alloc([128, 16, 2 * 65], BF16)
            KcT = arena.alloc([128, 2048], BF16)
            assert arena.off <= off_end
            arena.off = off_end
            kst = arena.alloc([128, 4, 128], F32)
            vst = arena.alloc([128, 4, 128], F32)
            PT = arena.alloc([128, 2, 512], BF16)
            rec = arena.alloc([128, 512], BF16)
            bcs = rec
            Vs = arena.alloc([128, 4, 2 * 65], BF16)
            vsst = vst
            PTs = arena.alloc([128, 256], BF16)
            PTn = arena.alloc([128, 16], BF16)
            ksst = kst[:, 0, :]
            V3v = V3.rearrange("p a t (h e) -> p (a t h) e", e=65)
            Vcv = Vc.rearrange("p t (h e) -> p (t h) e", e=65)
            Vsv = Vs.rearrange("p s (h e) -> p (s h) e", e=65)
            P.op("pool", lambda e: e.memset(Vsv[:, :, 64:65], 1.0), writes=["Vs"])
            wq = od_w_qkv[o_].rearrange("(kc p) (t f) -> p kc t f", p=128, t=3)
            pieces = [([wq[:, :, t3, 128 * j:128 * (j + 1)] for t3 in range(3)], (8, 384)) for j in range(8)]
            ws = WStream(pieces, depth=2)
            ptc = [0]
            for j in range(8):
                wsl_, wres = ws.get()
                wsl = wsl_.rearrange("p kc (t f) -> p kc t f", t=3)
                for which in range(2):
                    for gi, (c0, n) in enumerate(TGROUPS):
                        b = nextps()
                        for kc in range(8):
                            P.mm(psb[b][:, 0:n], wsl[:, kc, which, :], hT[:, kc, c0:c0 + n], start=(kc == 0), stop=(kc == 7), reads=[wres, "hT"], writes=[f"ps{b}"])
                        qi = ptc[0] % 2
                        ptc[0] += 1
                        P.op("act", lambda e, b=b, n=n, qi=qi: e.activation(out=qraw[:, qi, 0:n], in_=psb[b][:, 0:n], func=AF.Copy), writes=[f"qraw{qi}", f"ps{b}"])
                        b2 = nextps()
                        P.mm(psb[b2][:, 0:n], permT[:, :], qraw[:, qi, 0:n], reads=["permT", f"qraw{qi}"], writes=[f"ps{b2}"])
                        P.op("dve", lambda e, n=n, qi=qi, c0=c0: e.tensor_tensor(out=rt1[:, 0:n], in0=qraw[:, qi, 0:n], in1=Ctab[:, c0:c0 + n], op=ALU.mult), reads=[f"qraw{qi}", "Ctab"], writes=["rt1"])
                        P.op("dve", lambda e, n=n, b2=b2, c0=c0: e.tensor_tensor(out=rt2[:, 0:n], in0=psb[b2][:, 0:n], in1=Stab[:, c0:c0 + n], op=ALU.mult), reads=["Stab"], writes=["rt2", f"ps{b2}"])
                        P.op("pool", lambda e, n=n, c0=c0, which=which: e.tensor_tensor(out=QK[:, which, c0:c0 + n], in0=rt1[:, 0:n], in1=rt2[:, 0:n], op=ALU.add), reads=["rt1", "rt2"], writes=[f"QK{which}"])
                QT = QK[:, 0, :]
                KT = QK[:, 1, :]
                for t4 in range(4):
                    b = nextps()
                    pbf = psb[b][:, :].bitcast(BF16)
                    for q in range(4):
                        tt = t4 * 4 + q
                        P.tr(pbf[:, q * 128:(q + 1) * 128], KT[:, tt * 128:(tt + 1) * 128], identB[:, :], reads=["QK1", "identB"], writes=[f"ps{b}"])
                    P.op("act", lambda e, pbf=pbf: e.activation(out=kst[:, :, :], in_=pbf[:, 0:512].rearrange("p (q c) -> p q c", q=4), func=AF.Copy), writes=["kst", f"ps{b}"])
                    dma("sp", k_p[o_, t4 * 512:(t4 + 1) * 512, 128 * j:128 * (j + 1)].rearrange("(q p) c -> p q c", p=128), kst[:, :, :], reads=["kst"])
                b = nextps()
                pbf = psb[b][:, :].bitcast(BF16)
                P.tr(pbf[0:32, 0:128], KT[:, T:TT], identB[:, :], reads=["QK1", "identB"], writes=[f"ps{b}"])
                P.op("act", lambda e, pbf=pbf: e.activation(out=ksst[0:32, :], in_=pbf[0:32, 0:128], func=AF.Copy), writes=["kst", f"ps{b}"])
                dma("sp", k_s[o_, :, 128 * j:128 * (j + 1)], ksst[0:32, :], reads=["kst"])
                P.op("pool", lambda e: e.memset(V3v[:, :, 64:65], 1.0), writes=["V3"])
                for br, dil in enumerate((1, 4, 16)):
                    for t4 in range(4):
                        b = nextps()
                        for q in range(4):
                            ti = t4 * 4 + q
                            if br == 0:
                                start = 128 * ti
                            elif br == 1:
                                r, nblk = ti // 4, ti % 4
                                start = 512 * nblk + r
                            else:
                                start = ti
                            tok = hT[:, :, start:start + 128 * dil:dil] if dil > 1 else hT[:, :, start:start + 128]
                            for kc in range(8):
                                P.mm(psb[b][:, q * 128:(q + 1) * 128], tok[:, kc, :], wsl[:, kc, 2, :], start=(kc == 0), stop=(kc == 7), reads=[wres, "hT"], writes=[f"ps{b}"])
                        dstv = V3[:, br, t4 * 4:(t4 + 1) * 4, :].rearrange("p t (h e) -> p t h e", e=65)[:, :, :, 0:64]
                        srcv = psb[b][:, :].rearrange("p (t h e) -> p t h e", t=4, h=2)
                        P.op("act", lambda e, dstv=dstv, srcv=srcv: e.activation(out=dstv, in_=srcv, func=AF.Copy), writes=["V3", f"ps{b}"])
                        if br == 0:
                            P.op("dve", lambda e, b=b: e.tensor_copy(vst[:, :, :], psb[b][:, :].rearrange("p (q c) -> p q c", q=4)), writes=["vst", f"ps{b}"])
                            dma("sp", v_p[o_, t4 * 512:(t4 + 1) * 512, 128 * j:128 * (j + 1)].rearrange("(q p) c -> p q c", p=128), vst[:, :, :], reads=["vst"])
                b = nextps()
                for s4 in range(4):
                    for kc in range(8):
                        P.mm(psb[b][0:8, s4 * 128:(s4 + 1) * 128], hT[:, kc, T + 8 * s4:T + 8 * s4 + 8], wsl[:, kc, 2, :], start=(kc == 0), stop=(kc == 7), reads=[wres, "hT"], writes=[f"ps{b}"])
                dsts = Vs[0:8, :, :].rearrange("p s (h e) -> p s h e", e=65)[:, :, :, 0:64]
                P.op("act", lambda e, b=b, dsts=dsts: e.activation(out=dsts, in_=psb[b][0:8, :].rearrange("p (s h e) -> p s h e", s=4, h=2), func=AF.Copy), writes=["Vs", f"ps{b}"])
                P.op("dve", lambda e, b=b: e.tensor_copy(vsst[0:8, :, :], psb[b][0:8, :].rearrange("p (s c) -> p s c", s=4)), writes=["vst", f"ps{b}"])
                dma("sp", v_s[o_, :, 128 * j:128 * (j + 1)].rearrange("(s t) c -> t s c", t=8), vsst[0:8, :, :], reads=["vst"])

                def softmax_tile(bs, ncols, maskap, maskres):
                    pi = ptc[0] % 2
                    ptc[0] += 1
                    P.op("act", lambda e: e.activation(out=PT[:, pi, 0:ncols], in_=psb[bs][:, 0:ncols], func=AF.Exp, scale=0.125), writes=[f"PT{pi}", f"ps{bs}"])
                    P.op("pool", lambda e: e.tensor_tensor(out=PT[:, pi, 0:ncols], in0=PT[:, pi, 0:ncols], in1=maskap, op=ALU.mult), reads=[maskres, f"PT{pi}"], writes=[f"PT{pi}"])
                    return PT[:, pi, :], f"PT{pi}"

                for hh in range(2):
                    pb = 64 * hh
                    q_ = QT[pb:pb + 64, :]
                    k_ = KT[pb:pb + 64, :]
                    for G in range(4):
                        bo = accps()
                        P.mm(psb[bo][0:65, 0:512], zerB[:, 0:65], zerB[:, 0:512], start=True, stop=False, reads=["zerB"], writes=[f"ps{bo}"], skip_group_check=True)
                        for kb in range(max(4 * G - 1, 0), 4 * G + 4):
                            has_cur = kb >= 4 * G
                            has_prev = kb + 1 <= 4 * G + 3
                            q0 = 128 * kb if has_cur else 128 * (kb + 1)
                            ncol = 128 * (int(has_cur) + int(has_prev))
                            bs = nextps()
                            P.mm(psb[bs][:, 0:ncol], k_[:, 128 * kb:128 * kb + 128], q_[:, q0:q0 + ncol], reads=["QK0", "QK1"], writes=[f"ps{bs}"])
                            m0 = 0 if has_cur else 128
                            pt, ptres = softmax_tile(bs, ncol, cmask[:, m0:m0 + ncol], "cmask")
                            vl = V3[:, 0, kb, hh * 65:hh * 65 + 65]
                            for part in range(ncol // 128):
                                qq = q0 + 128 * part - 512 * G
                                P.mm(psb[bo][0:65, qq:qq + 128], vl, pt[:, 128 * part:128 * part + 128], start=False, stop=False, reads=["V3", ptres], writes=[f"ps{bo}"], skip_group_check=True)
                        for r in range(4):
                            blocks = [nb for nb in (G - 1, G) if nb >= 0]
                            bs = nextps()
                            qap = q_[:, 512 * G + r:512 * G + r + 512:4]
                            for bi, nb in enumerate(blocks):
                                P.mm(psb[bs][:, 128 * bi:128 * bi + 128], k_[:, 512 * nb + r:512 * nb + r + 512:4], qap, reads=["QK0", "QK1"], writes=[f"ps{bs}"])
                            ncol = 128 * len(blocks)
                            m0 = 128 if len(blocks) == 2 else 0
                            pt, ptres = softmax_tile(bs, ncol, cmask[:, m0:m0 + ncol], "cmask")
                            for bi, nb in enumerate(blocks):
                                vl = V3[:, 1, r * 4 + nb, hh * 65:hh * 65 + 65]
                                P.mm(psb[bo][0:65, r:512:4], vl, pt[:, 128 * bi:128 * bi + 128], start=False, stop=False, reads=["V3", ptres], writes=[f"ps{bo}"], skip_group_check=True)
                        bs = nextps()
                        for r in range(16):
                            P.mm(psb[bs][:, 32 * r:32 * r + 32], k_[:, r:2048:16], q_[:, 512 * G + r:512 * G + r + 512:16], reads=["QK0", "QK1"], writes=[f"ps{bs}"])
                        pt, ptres = softmax_tile(bs, 512, m3[:, G, :], "m3")
                        for r in range(16):
                            vl = V3[:, 2, r, hh * 65:hh * 65 + 65]
                            P.mm(psb[bo][0:65, r:512:16], vl, pt[:, 32 * r:32 * r + 32], start=False, stop=(r == 15), reads=["V3", ptres], writes=[f"ps{bo}"], skip_group_check=True)
                        P.op("dve", lambda e, bo=bo: e.reciprocal(rec[64:65, 0:512], psb[bo][64:65, 0:512]), writes=["rec", f"ps{bo}"])
                        bb = nextps()
                        P.mm(psb[bb][0:64, 0:512], onesB[64:65, 0:64], rec[64:65, 0:512], reads=["onesB", "rec"], writes=[f"ps{bb}"])
                        P.op("act", lambda e, bb=bb: e.activation(out=bcs[0:64, :], in_=psb[bb][0:64, 0:512], func=AF.Copy), writes=["bcs", f"ps{bb}"])
                        P.op("dve", lambda e, bo=bo, pb=pb, G=G, j=j: e.tensor_tensor(out=oT[pb:pb + 64, j, 512 * G:512 * G + 512], in0=psb[bo][0:64, 0:512], in1=bcs[0:64, :], op=ALU.mult), reads=["bcs"], writes=["oT", f"ps{bo}"])
                bso = accps()
                P.mm(psb[bso][0:65, 0:64], zerB[:, 0:65], zerB[:, 0:64], start=True, stop=False, reads=["zerB"], writes=[f"ps{bso}"], skip_group_check=True)
                for s4 in range(4):
                    dma("pool", Kc[:, :, :], ck[o_, s4, :, 128 * j:128 * (j + 1)].rearrange("(t p) c -> p t c", p=128), writes=["Kc", "V3"])
                    for hv in range(2):
                        dma("pool", Vc[:, :, hv * 65:hv * 65 + 64], cv[o_, s4, :, 128 * j + 64 * hv:128 * j + 64 * hv + 64].rearrange("(t p) e -> p t e", p=128), writes=["Vc", "V3"])
                    P.op("pool", lambda e: e.memset(Vcv[:, :, 64:65], 1.0), reads=["V3"], writes=["Vc1"])
                    for t8 in range(2):
                        b = nextps()
                        pbf = psb[b][:, :].bitcast(BF16)
                        for q in range(8):
                            ti = t8 * 8 + q
                            P.tr(pbf[:, q * 128:(q + 1) * 128], Kc[:, ti, :], identB[:, :], reads=["Kc", "V3", "identB"], writes=[f"ps{b}"])
                        P.op("dve", lambda e, pbf=pbf, t8=t8: e.tensor_copy(KcT[:, t8 * 1024:(t8 + 1) * 1024], pbf[:, 0:1024]), reads=["V3"], writes=["KcT", f"ps{b}"])
                    qs0 = T + 8 * s4
                    bs = nextps()
                    for hh in range(2):
                        pb = 64 * hh
                        for ti in range(16):
                            P.mm(psb[bs][:, hh * 128 + ti * 8:hh * 128 + ti * 8 + 8], KcT[pb:pb + 64, 128 * ti:128 * ti + 128], QT[pb:pb + 64, qs0:qs0 + 8], reads=["KcT", "V3", "QK0"], writes=[f"ps{bs}"])
                        P.mm(psb[bs][0:8, 256 + hh * 8:256 + hh * 8 + 8], KT[pb:pb + 64, qs0:qs0 + 8], QT[pb:pb + 64, qs0:qs0 + 8], reads=["QK1", "QK0"], writes=[f"ps{bs}"])
                    P.op("act", lambda e, bs=bs: e.activation(out=PTs[:, :], in_=psb[bs][:, 0:256], func=AF.Exp, scale=0.125), writes=["PTs", f"ps{bs}"])
                    P.op("act", lambda e, bs=bs: e.activation(out=PTn[0:8, :], in_=psb[bs][0:8, 256:272], func=AF.Exp, scale=0.125), writes=["PTn", f"ps{bs}"])
                    P.op("pool", lambda e: e.tensor_tensor(out=PTs[:, :], in0=PTs[:, :], in1=smask[:, :], op=ALU.mult), reads=["smask", "PTs"], writes=["PTs"])
                    P.op("pool", lambda e: e.tensor_tensor(out=PTn[0:8, :], in0=PTn[0:8, :], in1=nmask[0:8, :], op=ALU.mult), reads=["nmask", "PTn"], writes=["PTn"])
                    for hh in range(2):
                        oc = s4 * 16 + hh * 8
                        for ti in range(16):
                            P.mm(psb[bso][0:65, oc:oc + 8], Vc[:, ti, hh * 65:hh * 65 + 65], PTs[:, hh * 128 + ti * 8:hh * 128 + ti * 8 + 8], start=False, stop=False, reads=["Vc", "Vc1", "V3", "PTs"], writes=[f"ps{bso}"], skip_group_check=True)
                        P.mm(psb[bso][0:65, oc:oc + 8], Vs[0:8, s4, hh * 65:hh * 65 + 65], PTn[0:8, hh * 8:hh * 8 + 8], start=False, stop=(s4 == 3 and hh == 1), reads=["Vs", "PTn"], writes=[f"ps{bso}"], skip_group_check=True)
                P.op("dve", lambda e, bso=bso: e.reciprocal(rec[64:65, 0:64], psb[bso][64:65, 0:64]), writes=["rec", f"ps{bso}"])
                bb = nextps()
                P.mm(psb[bb][0:64, 0:64], onesB[64:65, 0:64], rec[64:65, 0:64], reads=["onesB", "rec"], writes=[f"ps{bb}"])
                P.op("act", lambda e, bb=bb: e.activation(out=bcs[0:64, 0:64], in_=psb[bb][0:64, 0:64], func=AF.Copy), writes=["bcs", f"ps{bb}"])
                for hh in range(2):
                    pb = 64 * hh
                    srco = psb[bso][0:64, 0:64].rearrange("p (s h q) -> p s h q", s=4, h=2)[:, :, hh, :]
                    srcb = bcs[0:64, 0:64].rearrange("p (s h q) -> p s h q", s=4, h=2)[:, :, hh, :]
                    dsto = oT[pb:pb + 64, j, T:TT].rearrange("p (s q) -> p s q", s=4)
                    P.op("dve", lambda e, srco=srco, srcb=srcb, dsto=dsto: e.tensor_tensor(out=dsto, in0=srco, in1=srcb, op=ALU.mult), reads=["bcs"], writes=["oT", f"ps{bso}"])
            phase(keep_off)
            if dbg:
                dma("pool", dbg_o.rearrange("p (c t) -> p c t", c=8), oT[:, :, :], reads=["oT"])
            out_proj_and_residual(oT, od_w_out[o_], "oT")

        def even_mixer(e_):
            phase()
            A2 = Arena(hT[:, :, :].rearrange("p c t -> p (c t)").bitcast(F32), (8 * TT) // 2)
            evc = arena.alloc([128, 1664], BF16)
            dma("pool", evc, kc_ev, writes=["evc"])
            su64 = evc[0:CP, 0:8 * CP]; ui64 = evc[0:CP, 512:512 + 8 * CP]; sl64 = evc[0:CP, 1024:1024 + 8 * CP]; blk2 = evc[:, 1536:1664]
            idr64 = arena.alloc([64, 512], BF16)
            dma("pool", idr64, kc_idr, writes=["idr64"])
            m8 = arena.alloc([8, 256], BF16)
            dma("pool", m8, kc_m8, writes=["m8"])
            wo_sb = A2.alloc([128, 8, 1024], BF16)
            dma("pool", wo_sb, ev_w_out[e_].rearrange("(kc p) f -> p kc f", p=128), writes=["wo_sb"])
            lw = A2.alloc([128, 512], BF16)
            dma("pool", lw[0:32, :], rw_w2[e_], writes=["lw"])
            dma("pool", lw[32:64, :], rw_a2[e_], writes=["lw"])
            dma("pool", lw[64:128, :], rw_g2[e_], writes=["lw"])
            evp = A2.alloc([128, 48], F32)
            plist = [(rw_mu[e_].rearrange("(c p) -> c p", p=128), 13, 0), (rw_w0[e_].rearrange("(c p) -> c p", p=128), 4, 13),
                     (rw_a0[e_].rearrange("(c p) -> c p", p=128), 4, 17), (rw_kk[e_].rearrange("(c p) -> c p", p=128), 4, 21),
                     (rw_ka[e_].rearrange("(c p) -> c p", p=128), 4, 25), (rw_rk[e_].rearrange("(c p) -> c p", p=128), 4, 29),
                     (rw_lnx_g[e_].rearrange("(c p) -> c p", p=128), 4, 33), (rw_lnx_b[e_].rearrange("(c p) -> c p", p=128), 4, 37)]
            for (src, R, o0) in plist:
                load_rows_T(src, R, evp[:, o0:o0 + R], "evp")
            MU, W0, A0, KKP, KAP, RKP, LG, LB = 0, 13, 17, 21, 25, 29, 33, 37
            lngb = A2.alloc([128, 2, 512], F32)
            dma("sp", lngb[:, 0, :], gm_ln_g[e_:e_ + 1, :].broadcast_to([128, 512]), writes=["lngb"])
            dma("sp", lngb[:, 1, :], gm_ln_b[e_:e_ + 1, :].broadcast_to([128, 512]), writes=["lngb"])
            bsb = A2.alloc([128, 4, 128], F32)
            dma("sp", bsb, gm_bs[e_:e_ + 1, :, :].broadcast_to([128, 4, 128]), writes=["bsb"])
            WcT = A2.alloc([128, 4, 128], BF16)
            wstg = arena.alloc([128, 4, 128], F32)
            dma("sp", wstg, gm_ws[e_].rearrange("g i j -> i g j"), writes=["wstg"])
            for g in range(4):
                b = nextps()
                P.tr(psb[b][:, 0:128], wstg[:, g, :], identF[:, :], reads=["wstg", "identF"], writes=[f"ps{b}"])
                P.op("dve", lambda e, b=b, g=g: e.tensor_tensor(out=WcT[:, g, :], in0=psb[b][:, 0:128], in1=cmask[:, 0:128], op=ALU.mult), reads=["cmask"], writes=["WcT", f"ps{b}"])
            Bd = A2.alloc([32, 4, 32], BF16)
            P.op("pool", lambda e: e.memset(Bd[:, :, :], 0.0), writes=["Bd"])
            for g in range(4):
                for s4 in range(4):
                    dma("sp", Bd[8 * s4:8 * s4 + 8, g, 8 * s4:8 * s4 + 8], WcT[0:8, g, 0:8], reads=["WcT"], writes=["Bd"])
            S = A2.alloc([64, 8, 64], F32)
            Sb = A2.alloc([64, 8, 64], BF16)
            pblast = A2.alloc([128, 13, 4], F32)
            P.op("pool", lambda e: e.memset(S[:, :, :], 0.0), writes=["S"])
            P.op("pool", lambda e: e.memset(Sb[:, :, :], 0.0), writes=["Sb"])
            P.op("pool", lambda e: e.memset(pblast[:, :, :], 0.0), writes=["pblast"])
            sst = A2.alloc([64, 8, 64], F32)
            keep_off = arena.off
            win = ev_w_in[e_].rearrange("(kc p) f -> p kc f", p=128)
            porder = [(0, 512), (512, 512), (2560, 128), (1536, 512), (1024, 512), (2048, 512)]
            EXPC = float(-np.exp(-0.5))
            groups = [(256 * i, 256, 1, 256, CP) for i in range(8)] + [(T, 32, 4, 8, 8)]
            def do_group(gi, c0, n, nseq, L, C):
                phase(keep_off)
                is_s = nseq > 1
                if is_s:
                    stg = arena.alloc([128, 128], F32)
                    for s4 in range(4):
                        dma("sp", stg[13 * s4:13 * s4 + 13, :], sshift[e_, s4, :].rearrange("(c p) -> c p", p=128), writes=["stg"])
                    b = nextps()
                    P.tr(psb[b][:, 0:52], stg[0:52, :], identF[0:52, 0:52], reads=["stg", "identF"], writes=[f"ps{b}"])
                    P.op("dve", lambda e, b=b: e.tensor_copy(pblast[:, :, :], psb[b][:, 0:52].rearrange("p (s c) -> p c s", s=4)), writes=["pblast", f"ps{b}"])
                rstd_g = arena.alloc([128, n], F32)
                E1 = arena.alloc([128, 4, n], F32)
                sqg = E1[:, :, :].rearrange("p c n -> p (c n)").bitcast(BF16).rearrange("p (c n) -> p c n", c=8)
                off_hg = arena.off
                hg = arena.alloc([128, 8, n], BF16)
                off_hg_end = arena.off
                tmpg = arena.alloc([128, n], F32)
                for c in range(8):
                    P.op("act", lambda e, c=c: e.activation(out=sqg[:, c, :], in_=xT[:, c, c0:c0 + n], func=AF.Square), reads=["xT"], writes=["sqg", "E1"])
                b = nextps()
                for c in range(8):
                    P.mm(psb[b][:, 0:n], onesB[:, :], sqg[:, c, :], start=(c == 0), stop=(c == 7), reads=["sqg", "onesB"], writes=[f"ps{b}"])
                P.op("act", lambda e, b=b: e.activation(out=rstd_g[:, :], in_=psb[b][:, 0:n], func=AF.Sqrt, scale=1.0 / D, bias=epsb[:, 0:1]), reads=["epsb"], writes=["rstd_g", f"ps{b}"])
                P.op("dve", lambda e: e.reciprocal(rstd_g[:, :], rstd_g[:, :]), reads=["rstd_g"], writes=["rstd_g"])
                gsegs = [(0, n, 0)] if not is_s else [(8 * s4, 8, 1 + s4) for s4 in range(4)]
                for c in range(8):
                    for (o0, nn, sq_) in gsegs:
                        P.op("dve", lambda e, c=c, o0=o0, nn=nn, sq_=sq_: e.scalar_tensor_tensor(out=tmpg[:, o0:o0 + nn], in0=xT[:, c, c0 + o0:c0 + o0 + nn], scalar=mods[:, 0, c, sq_:sq_ + 1], in1=rstd_g[:, o0:o0 + nn], op0=ALU.mult, op1=ALU.mult), reads=["xT", "mods", "rstd_g"], writes=["tmpg"])
                        P.op("act", lambda e, c=c, o0=o0, nn=nn, sq_=sq_: e.activation(out=hg[:, c, o0:o0 + nn], in_=tmpg[:, o0:o0 + nn], func=AF.Identity, bias=mods[:, 1, c, sq_:sq_ + 1]), reads=["tmpg", "mods"], writes=["hg"])
                if EVSTOP <= 2:
                    return
                mixg = arena.alloc([128, 8, n], BF16)
                uT = arena.alloc([128, 4, n], BF16)
                vn = arena.alloc([128, 2, 512], BF16)
                lnt = arena.alloc([128, 512], F32)
                lnt2 = arena.alloc([128, 512], F32)
                Of = lnt[0:64, :]
                Osq = lnt2[0:64, :]
                stat = arena.alloc([128, 16], F32)
                pbc = arena.alloc([128, nseq, L + 1], F32)
                xmc = arena.alloc([128, nseq, L], F32)
                lor = arena.alloc([128, n], BF16)
                off_ld = arena.off
                ld = arena.alloc([128, 4, n], F32)
                lp = arena.alloc([128, 4, n], F32)
                off_e2 = arena.off
                E2 = arena.alloc([128, 4, n], F32)
                ag = arena.alloc([128, 4, n], F32)
                off_e2_end = arena.off
                off_kp = arena.off
                kp = arena.alloc([128, 4, n], F32)
                off_kp_end = arena.off
                gT = arena.alloc([128, 4, n], BF16)
                At = arena.alloc([64, 8, n], BF16)
                Bt = arena.alloc([64, 8, n], BF16)
                Kt = arena.alloc([64, 8, n], BF16)
                Rt = arena.alloc([64, 8, n], BF16)
                vT = arena.alloc([64, 8, n], BF16)
                PCg = arena.alloc([64, 8, 8], F32)
                bn = arena.alloc([128, 4, n], BF16)
                t1 = arena.alloc([128, n], F32)
                t2 = arena.alloc([128, n], F32)
                t3 = arena.alloc([128, n], BF16)
                ws = WStream([(win[:, :, a:a + w], (8, w)) for (a, w) in porder], depth=2)

                def xm_chunk(b, cb):
                    P.op("act", lambda e: e.activation(out=pbc[:, :, 1:L + 1], in_=psb[b][:, 0:n].rearrange("p (s l) -> p s l", s=nseq), func=AF.Copy), writes=["pbc", f"ps{b}"])
                    P.op("dve", lambda e: e.tensor_copy(pbc[:, :, 0], pblast[:, cb, 0:nseq]), reads=["pblast"], writes=["pbc"])
                    P.op("dve", lambda e: e.tensor_tensor(out=xmc[:, :, :], in0=pbc[:, :, 0:L], in1=pbc[:, :, 1:L + 1], op=ALU.subtract), reads=["pbc"], writes=["xmc"])
                    P.op("dve", lambda e: e.scalar_tensor_tensor(out=xmc[:, :, :], in0=xmc[:, :, :], scalar=evp[:, MU + cb:MU + cb + 1], in1=pbc[:, :, 1:L + 1], op0=ALU.mult, op1=ALU.add), reads=["xmc", "pbc", "evp"], writes=["xmc"])
                    P.op("pool", lambda e: e.tensor_copy(pblast[:, cb, 0:nseq], pbc[:, :, L]), reads=["pbc"], writes=["pblast"])
                    return xmc[:, :, :].rearrange("p s l -> p (s l)")

                def proj_fm(wsl, wres, col, b):
                    for kc in range(8):
                        P.mm(psb[b][:, 0:n], wsl[:, kc, col * 128:(col + 1) * 128], hg[:, kc, :], start=(kc == 0), stop=(kc == 7), reads=[wres, "hg"], writes=[f"ps{b}"])

                wsl, wres = ws.get()
                for c in range(4):
                    b = nextps()
                    proj_fm(wsl, wres, c, b)
                    P.op("act", lambda e, b=b, c=c: e.activation(out=uT[:, c, :], in_=psb[b][:, 0:n], func=AF.Gelu), writes=["uT", f"ps{b}"])
                wsl, wres = ws.get()
                ntile = (n + 127) // 128
                for tt in range(ntile):
                    r = min(128, n - 128 * tt)
                    b = nextps()
                    for kc in range(8):
                        P.mm(psb[b][0:r, 0:512], hg[:, kc, 128 * tt:128 * tt + r], wsl[:, kc, :], start=(kc == 0), stop=(kc == 7), reads=[wres, "hg"], writes=[f"ps{b}"])
                    P.op("act", lambda e, b=b, r=r: e.activation(out=lnt[0:r, :], in_=psb[b][0:r, 0:512], func=AF.Gelu), writes=["lnt", f"ps{b}"])
                    P.op("dve", lambda e, r=r: e.reduce_sum(out=stat[0:r, 0:1], in_=lnt[0:r, :], axis=AX.X), reads=["lnt"], writes=["stat"])
                    P.op("pool", lambda e, r=r: e.tensor_tensor(out=lnt2[0:r, :], in0=lnt[0:r, :], in1=lnt[0:r, :], op=ALU.mult), reads=["lnt"], writes=["lnt2"])
                    P.op("dve", lambda e, r=r: e.reduce_sum(out=stat[0:r, 1:2], in_=lnt2[0:r, :], axis=AX.X), reads=["lnt2"], writes=["stat"])
                    P.op("dve", lambda e, r=r: e.tensor_scalar(out=stat[0:r, 2:3], in0=stat[0:r, 0:1], scalar1=1.0 / 512, scalar2=None, op0=ALU.mult), reads=["stat"], writes=["stat"])
                    P.op("dve", lambda e, r=r: e.tensor_tensor(out=stat[0:r, 3:4], in0=stat[0:r, 2:3], in1=stat[0:r, 2:3], op=ALU.mult), reads=["stat"], writes=["stat"])
                    P.op("dve", lambda e, r=r: e.scalar_tensor_tensor(out=stat[0:r, 4:5], in0=stat[0:r, 1:2], scalar=1.0 / 512, in1=stat[0:r, 3:4], op0=ALU.mult, op1=ALU.subtract), reads=["stat"], writes=["stat"])
                    P.op("dve", lambda e, r=r: e.tensor_scalar(out=stat[0:r, 4:5], in0=stat[0:r, 4:5], scalar1=1e-5, scalar2=None, op0=ALU.add), reads=["stat"], writes=["stat"])
                    P.op("act", lambda e, r=r: e.activation(out=stat[0:r, 5:6], in_=stat[0:r, 4:5], func=AF.Sqrt), reads=["stat"], writes=["stat"])
                    P.op("dve", lambda e, r=r: e.reciprocal(stat[0:r, 5:6], stat[0:r, 5:6]), reads=["stat"], writes=["stat"])
                    P.op("dve", lambda e, r=r: e.tensor_scalar(out=lnt[0:r, :], in0=lnt[0:r, :], scalar1=stat[0:r, 2:3], scalar2=stat[0:r, 5:6], op0=ALU.subtract, op1=ALU.mult), reads=["stat", "lnt"], writes=["lnt"])
                    P.op("pool", lambda e, r=r: e.tensor_tensor(out=lnt[0:r, :], in0=lnt[0:r, :], in1=lngb[0:r, 0, :], op=ALU.mult), reads=["lnt", "lngb"], writes=["lnt"])
                    P.op("pool", lambda e, r=r: e.tensor_tensor(out=lnt[0:r, :], in0=lnt[0:r, :], in1=lngb[0:r, 1, :], op=ALU.add), reads=["lnt", "lngb"], writes=["lnt"])
                    P.op("act", lambda e, r=r, tt=tt: e.activation(out=vn[0:r, tt, :], in_=lnt[0:r, :], func=AF.Copy), reads=["lnt"], writes=["vn"])
                    if is_s:
                        dma("sp", gv_s[e_, :, :], lnt[0:32, :], reads=["lnt"])
                for g in range(4):
                    b = nextps()
                    if is_s:
                        P.mm(psb[b][:, 0:32], vn[0:32, 0, g * 128:(g + 1) * 128], Bd[:, g, :], reads=["vn", "Bd"], writes=[f"ps{b}"])
                        for s4 in range(4):
                            P.op("dve", lambda e, b=b, s4=s4, g=g: e.tensor_tensor(out=t1[:, 8 * s4:8 * s4 + 8], in0=psb[b][:, 8 * s4:8 * s4 + 8], in1=bsb[:, g, 0:8], op=ALU.add), reads=["bsb"], writes=["t1", f"ps{b}"])
                    else:
                        for tt in range(ntile):
                            P.mm(psb[b][:, 128 * tt:128 * tt + 128], vn[:, tt, g * 128:(g + 1) * 128], WcT[:, g, :], reads=["vn", "WcT"], writes=[f"ps{b}"])
                            P.op("dve", lambda e, b=b, tt=tt, g=g: e.tensor_tensor(out=t1[:, 128 * tt:128 * tt + 128], in0=psb[b][:, 128 * tt:128 * tt + 128], in1=bsb[:, g, :], op=ALU.add), reads=["bsb"], writes=["t1", f"ps{b}"])
                    P.op("pool", lambda e, g=g: e.tensor_tensor(out=mixg[:, g, :], in0=t1[:, :], in1=uT[:, g, :], op=ALU.mult), reads=["t1", "uT"], writes=["mixg"])
                if EVSTOP <= 3:
                    return
                wsl, wres = ws.get()
                b = nextps()
                proj_fm(wsl, wres, 0, b)
                xm = xm_chunk(b, 12)
                P.op("act", lambda e: e.activation(out=lor[0:32, :], in_=xm[0:32, :], func=AF.Tanh), reads=["xmc"], writes=["lor"])
                P.op("act", lambda e: e.activation(out=lor[32:64, :], in_=xm[32:64, :], func=AF.Copy), reads=["xmc"], writes=["lor"])
                P.op("act", lambda e: e.activation(out=lor[64:128, :], in_=xm[64:128, :], func=AF.Sigmoid), reads=["xmc"], writes=["lor"])
                for c in range(4):
                    b = nextps()
                    P.mm(psb[b][:, 0:n], lw[0:32, c * 128:(c + 1) * 128], lor[0:32, :], reads=["lw", "lor"], writes=[f"ps{b}"])
                    P.op("act", lambda e, b=b, c=c: e.activation(out=ld[:, c, :], in_=psb[b][:, 0:n], func=AF.Sigmoid, bias=evp[:, W0 + c:W0 + c + 1]), reads=["evp"], writes=["ld", f"ps{b}"])
                    b = nextps()
                    P.mm(psb[b][:, 0:n], lw[32:64, c * 128:(c + 1) * 128], lor[32:64, :], reads=["lw", "lor"], writes=[f"ps{b}"])
                    P.op("act", lambda e, b=b, c=c: e.activation(out=ag[:, c, :], in_=psb[b][:, 0:n], func=AF.Sigmoid, bias=evp[:, A0 + c:A0 + c + 1]), reads=["evp"], writes=["ag", f"ps{b}"])
                    b = nextps()
                    P.mm(psb[b][:, 0:n], lw[64:128, c * 128:(c + 1) * 128], lor[64:128, :], reads=["lw", "lor"], writes=[f"ps{b}"])
                    P.op("act", lambda e, b=b, c=c: e.activation(out=gT[:, c, :], in_=psb[b][:, 0:n], func=AF.Copy), writes=["gT", f"ps{b}"])
                P.op("dve", lambda e: e.tensor_scalar(out=ld[:, :, :], in0=ld[:, :, :], scalar1=EXPC, scalar2=None, op0=ALU.mult), reads=["ld"], writes=["ld"])
                nch = (4 * n) // C
                ldv = ld[:, :, :].rearrange("p c (k l) -> p (c k) l", l=C)
                lpv = lp[:, :, :].rearrange("p c (k l) -> p (c k) l", l=C)
                e1v = E1[:, :, :].rearrange("p c (k l) -> p (c k) l", l=C)
                P.op("pool", lambda e: e.tensor_copy(lpv, ldv), reads=["ld"], writes=["lp"])
                src, dst, sres, dres = lpv, e1v, "lp", "E1"
                sh = 1
                while sh < C:
                    P.op("dve", lambda e, src=src, dst=dst, sh=sh: e.tensor_tensor(out=dst[:, :, sh:C], in0=src[:, :, sh:C], in1=src[:, :, 0:C - sh], op=ALU.add), reads=[sres], writes=[dres])
                    P.op("pool", lambda e, src=src, dst=dst, sh=sh: e.tensor_copy(dst[:, :, 0:sh], src[:, :, 0:sh]), reads=[sres], writes=[dres])
                    src, dst, sres, dres = dst, src, dres, sres
                    sh *= 2
                if src is not lpv:
                    P.op("pool", lambda e: e.tensor_copy(lpv, e1v), reads=["E1"], writes=["lp"])
                P.op("dve", lambda e: e.tensor_tensor(out=ld[:, :, :], in0=lp[:, :, :], in1=ld[:, :, :], op=ALU.subtract), reads=["lp", "ld"], writes=["ld"])
                P.op("act", lambda e: e.activation(out=E1[:, :, :], in_=lp[:, :, :], func=AF.Exp), reads=["lp"], writes=["E1"])
                P.op("act", lambda e: e.activation(out=E2[:, :, :], in_=lp[:, :, :], func=AF.Exp, scale=-1.0), reads=["lp"], writes=["E2"])
                P.op("act", lambda e: e.activation(out=ld[:, :, :], in_=ld[:, :, :], func=AF.Exp), reads=["ld"], writes=["ld"])
                E3 = ld
                nck = n // C
                for c in range(4):
                    for hh in range(2):
                        P.op("act", lambda e, c=c, hh=hh: e.activation(out=PCg[:, 2 * c + hh, 0:nck], in_=E1[64 * hh:64 * hh + 64, c, C - 1:n:C], func=AF.Copy), reads=["E1"], writes=["PCg"])
                if EVSTOP <= 4:
                    return
                wsl, wres = ws.get()
                for c in range(4):
                    b = nextps()
                    proj_fm(wsl, wres, c, b)
                    xm = xm_chunk(b, 4 + c)
                    P.op("dve", lambda e, c=c, xm=xm: e.tensor_scalar(out=t1[:, :], in0=xm, scalar1=evp[:, KKP + c:KKP + c + 1], scalar2=None, op0=ALU.mult), reads=["xmc", "evp"], writes=["t1"])
                    P.op("act", lambda e: e.activation(out=t3[:, :], in_=t1[:, :], func=AF.Square), reads=["t1"], writes=["t3"])
                    b2 = nextps()
                    P.mm(psb[b2][:, 0:n], blk2, t3[:, :], reads=["evc", "t3"], writes=[f"ps{b2}"])
                    P.op("act", lambda e, b2=b2: e.activation(out=t2[:, :], in_=psb[b2][:, 0:n], func=AF.Sqrt, bias=epsb2[:, 0:1]), reads=["epsb"], writes=["t2", f"ps{b2}"])
                    P.op("dve", lambda e: e.reciprocal(t2[:, :], t2[:, :]), reads=["t2"], writes=["t2"])
                    P.op("dve", lambda e: e.tensor_tensor(out=t1[:, :], in0=t1[:, :], in1=t2[:, :], op=ALU.mult), reads=["t1", "t2"], writes=["t1"])
                    P.op("dve", lambda e, c=c: e.tensor_scalar(out=t2[:, :], in0=ag[:, c, :], scalar1=-1.0, scalar2=evp[:, KAP + c:KAP + c + 1], op0=ALU.add, op1=ALU.mult), reads=["ag", "evp"], writes=["t2"])
                    P.op("dve", lambda e, c=c, xm=xm: e.scalar_tensor_tensor(out=kp[:, c, :], in0=t2[:, :], scalar=1.0, in1=xm, op0=ALU.add, op1=ALU.mult), reads=["t2", "xmc"], writes=["kp"])
                    P.op("pool", lambda e, c=c: e.tensor_tensor(out=t2[:, :], in0=t1[:, :], in1=ag[:, c, :], op=ALU.mult), reads=["t1", "ag"], writes=["t2"])
                    for hh in range(2):
                        ps_ = slice(64 * hh, 64 * hh + 64)
                        P.op("dve", lambda e, c=c, hh=hh, ps_=ps_: e.scalar_tensor_tensor(out=At[:, 2 * c + hh, :], in0=t1[ps_, :], scalar=-1.0, in1=E3[ps_, c, :], op0=ALU.mult, op1=ALU.mult), reads=["t1", "ld"], writes=["At"])
                        P.op("pool", lambda e, c=c, hh=hh, ps_=ps_: e.tensor_tensor(out=Bt[:, 2 * c + hh, :], in0=t2[ps_, :], in1=E2[ps_, c, :], op=ALU.mult), reads=["t2", "E2"], writes=["Bt"])
                        P.op("pool", lambda e, c=c, hh=hh, ps_=ps_: e.tensor_tensor(out=Kt[:, 2 * c + hh, :], in0=kp[ps_, c, :], in1=E2[ps_, c, :], op=ALU.mult), reads=["kp", "E2"], writes=["Kt"])
                wsl, wres = ws.get()
                for c in range(4):
                    b = nextps()
                    proj_fm(wsl, wres, c, b)
                    xm = xm_chunk(b, c)
                    for hh in range(2):
                        ps_ = slice(64 * hh, 64 * hh + 64)
                        P.op("pool", lambda e, c=c, xm=xm, hh=hh, ps_=ps_: e.tensor_tensor(out=Rt[:, 2 * c + hh, :], in0=xm[ps_, :], in1=E1[ps_, c, :], op=ALU.mult), reads=["xmc", "E1"], writes=["Rt"])
                    P.op("dve", lambda e, c=c, xm=xm: e.scalar_tensor_tensor(out=t3[:, :], in0=xm, scalar=evp[:, RKP + c:RKP + c + 1], in1=kp[:, c, :], op0=ALU.mult, op1=ALU.mult), reads=["xmc", "evp", "kp"], writes=["t3"])
                    b2 = nextps()
                    P.mm(psb[b2][:, 0:n], blk2, t3[:, :], reads=["evc", "t3"], writes=[f"ps{b2}"])
                    P.op("act", lambda e, b2=b2, c=c: e.activation(out=kp[:, c, :], in_=psb[b2][:, 0:n], func=AF.Copy), reads=["t3"], writes=["kp", f"ps{b2}"])
                wsl, wres = ws.get()
                for c in range(4):
                    b = nextps()
                    proj_fm(wsl, wres, c, b)
                    xm = xm_chunk(b, 8 + c)
                    for hh in range(2):
                        ps_ = slice(64 * hh, 64 * hh + 64)
                        P.op("act", lambda e, c=c, xm=xm, hh=hh, ps_=ps_: e.activation(out=vT[:, 2 * c + hh, :], in_=xm[ps_, :], func=AF.Copy), reads=["xmc"], writes=["vT"])
                    P.op("dve", lambda e, c=c, xm=xm: e.tensor_tensor(out=bn[:, c, :], in0=xm, in1=kp[:, c, :], op=ALU.mult), reads=["xmc", "kp"], writes=["bn"])
                if EVSTOP <= 5:
                    return
                if (not is_s and gi == 7) or is_s:
                    ncol = 13 * nseq
                    b = nextps()
                    P.tr(psb[b][0:ncol, 0:128], pblast[:, :, 0:nseq].rearrange("p c s -> p (c s)") if nseq == 4 else pblast[:, :, 0], identF[:, :], reads=["pblast", "identF"], writes=[f"ps{b}"])
                    sho = arena.alloc([128, 128], F32)
                    P.op("dve", lambda e, b=b, ncol=ncol: e.tensor_copy(sho[0:ncol, :], psb[b][0:ncol, 0:128]), writes=["sho", f"ps{b}"])
                    if not is_s:
                        dma("sp", shift_p[e_, :].rearrange("(c p) -> c p", p=128), sho[0:13, :], reads=["sho"])
                    else:
                        for cb in range(13):
                            dma("sp", shift_s[e_, :, cb * 128:(cb + 1) * 128], sho[4 * cb:4 * cb + 4, :], reads=["sho"])
                if EVSTOP <= 6:
                    return
                nlev = {64: 5, 32: 4, 16: 3, 8: 2}[C]
                HC = 8 * C
                if C == CP:
                    mSU, mUI, mSL, mID = su64, ui64, sl64, idr64[0:CP, 0:8 * CP]
                else:
                    mSU, mUI, mSL, mID = m8[:, 0:64], m8[:, 64:128], m8[:, 128:192], m8[:, 192:256]
                save_off = arena.off
                if n == 256:
                    arena.off = off_e2
                NM = arena.alloc([64, 2, 512], BF16)
                LM = arena.alloc([64, 2, 512], BF16)
                WM = arena.alloc([64, 2, 512], BF16)
                Xb = arena.alloc([64, 512], BF16)
                Ub = arena.alloc([64, 512], BF16)
                if n == 256:
                    assert arena.off <= off_e2_end
                    arena.off = off_hg
                tok3 = arena.alloc([64, 3, 512], BF16)
                if n == 256:
                    assert arena.off <= off_hg_end
                    arena.off = save_off
                save_off2 = arena.off
                if n == 256:
                    arena.off = off_kp
                AKm = arena.alloc([64, 512], BF16)
                RBm = arena.alloc([64, 512], BF16)
                RKm = arena.alloc([64, 512], BF16)
                Onb = arena.alloc([64, 512], BF16)
                if n == 256:
                    assert arena.off <= off_kp_end
                    arena.off = save_off2
                gst = arena.alloc([64, 48], F32)
                Stmp = uT[0:64, :, :].rearrange("p c n -> p (c n)").bitcast(F32)[:, 0:512].rearrange("p (h v) -> p h v", h=8) if n == 256 else arena.alloc([64, 8, 64], F32)
                curf = nlev % 2
                mres = "evc" if C == CP else "m8"

                def p1(ck, slot):
                    k0 = ck * C
                    co = slot * HC
                    NMs = NM[:, :, co:co + HC]; LMs = LM[:, :, co:co + HC]; WMs = WM[:, :, co:co + HC]
                    AKs = AKm[:, co:co + HC]; RBs = RBm[:, co:co + HC]; RKs = RKm[:, co:co + HC]
                    sx = f"s{slot}"

                    def hsl(arr, h):
                        return arr[:, h, k0:k0 + C]

                    def scores(lhs, rhs, lres, rres, mask, dst, dres):
                        b = nextps()
                        for h in range(8):
                            P.mm(psb[b][0:C, h * C:(h + 1) * C], hsl(lhs, h), hsl(rhs, h), reads=[lres, rres], writes=[f"ps{b}"])
                        P.op("dve", lambda e, b=b: e.tensor_tensor(out=dst, in0=psb[b][0:C, 0:HC], in1=mask, op=ALU.mult), reads=[mres], writes=[dres, f"ps{b}"])
                    scores(Bt, At, "Bt", "At", mSU, NMs[0:C, 0, :], f"NM0{sx}")
                    yield
                    scores(At, Bt, "At", "Bt", mSL, LMs[0:C, 0, :], f"LM0{sx}")
                    yield
                    scores(Kt, At, "Kt", "At", mSU, AKs[0:C, :], f"AKm{sx}")
                    yield
                    scores(Bt, Rt, "Bt", "Rt", mUI, RBs[0:C, :], f"RBm{sx}")
                    yield
                    scores(Kt, Rt, "Kt", "Rt", mUI, RKs[0:C, :], f"RKm{sx}")
                    yield
                    P.op("pool", lambda e: e.tensor_tensor(out=WMs[0:C, 0, :], in0=NMs[0:C, 0, :], in1=mID, op=ALU.add), reads=[f"NM0{sx}", "idr64", "m8"], writes=[f"WM0{sx}"])
                    cur = 0
                    for lev in range(nlev):
                        nxt = 1 - cur
                        b = nextps()
                        for h in range(8):
                            P.mm(psb[b][0:C, h * C:(h + 1) * C], NMs[0:C, cur, h * C:(h + 1) * C], LMs[0:C, cur, h * C:(h + 1) * C], reads=[f"NM{cur}{sx}", f"LM{cur}{sx}"], writes=[f"ps{b}"])
                        P.op("act", lambda e, b=b, nxt=nxt: e.activation(out=LMs[0:C, nxt, :], in_=psb[b][0:C, 0:HC], func=AF.Copy), writes=[f"LM{nxt}{sx}", f"ps{b}"])
                        yield
                        if lev < nlev - 1:
                            b = nextps()
                            for h in range(8):
                                P.mm(psb[b][0:C, h * C:(h + 1) * C], LMs[0:C, cur, h * C:(h + 1) * C], NMs[0:C, cur, h * C:(h + 1) * C], reads=[f"NM{cur}{sx}", f"LM{cur}{sx}"], writes=[f"ps{b}"])
                            P.op("dve", lambda e, b=b, nxt=nxt: e.tensor_copy(NMs[0:C, nxt, :], psb[b][0:C, 0:HC]), writes=[f"NM{nxt}{sx}", f"ps{b}"])
                            yield
                        b = nextps()
                        for h in range(8):
                            P.mm(psb[b][0:C, h * C:(h + 1) * C], LMs[0:C, nxt, h * C:(h + 1) * C], WMs[0:C, cur, h * C:(h + 1) * C], reads=[f"LM{nxt}{sx}", f"WM{cur}{sx}"], writes=[f"ps{b}"])
                        P.op("dve", lambda e, b=b, nxt=nxt, cur=cur: e.tensor_tensor(out=WMs[0:C, nxt, :], in0=psb[b][0:C, 0:HC], in1=WMs[0:C, cur, :], op=ALU.add), reads=[f"WM{cur}{sx}"], writes=[f"WM{nxt}{sx}", f"ps{b}"])
                        yield
                        cur = nxt
                    assert cur == curf

                def p2(ck, slot):
                    k0 = ck * C
                    co = slot * HC
                    sx = f"s{slot}"
                    WMs = WM[:, :, co:co + HC]
                    AKm_ = AKm[:, co:co + HC]; RBm_ = RBm[:, co:co + HC]; RKm_ = RKm[:, co:co + HC]
                    Wf = WMs[:, curf, :]
                    wfres = f"WM{curf}{sx}"

                    def hsl(arr, h):
                        return arr[:, h, k0:k0 + C]
                    if is_s:
                        dma("sp", sst[:, :, :], swkv[e_, ck].rearrange("h v k -> v h k"), writes=["sst"])
                        b = nextps()
                        for h in range(8):
                            P.tr(psb[b][0:64, h * 64:(h + 1) * 64], sst[:, h, :], identF[0:64, 0:64], reads=["sst", "identF"], writes=[f"ps{b}"])
                        P.op("dve", lambda e, b=b: e.tensor_copy(S[:, :, :], psb[b][0:64, 0:512].rearrange("p (h v) -> p h v", h=8)), writes=["S", f"ps{b}"])
                        P.op("act", lambda e: e.activation(out=Sb[:, :, :], in_=S[:, :, :], func=AF.Copy), reads=["S"], writes=["Sb"])
                    for ai, (arr, ares) in enumerate(((Bt, "Bt"), (Kt, "Kt"), (vT, "vT"))):
                        b = nextps()
                        pbf = psb[b][:, :].bitcast(BF16)
                        for h in range(8):
                            P.tr(pbf[0:C, h * 64:(h + 1) * 64], arr[:, h, k0:k0 + C], identB[0:64, 0:64], reads=[ares, "identB"], writes=[f"ps{b}"])
                        P.op("act" if ai != 1 else "dve", (lambda e, pbf=pbf, ai=ai: e.activation(out=tok3[0:C, ai, :], in_=pbf[0:C, 0:512], func=AF.Copy)) if ai != 1 else (lambda e, pbf=pbf, ai=ai: e.tensor_copy(tok3[0:C, ai, :], pbf[0:C, 0:512])), writes=[f"tok{ai}", f"ps{b}"])
                    Btok, Ktok, Vtok = tok3[:, 0, :], tok3[:, 1, :], tok3[:, 2, :]

                    def sbh(h):
                        return Sb[:, h, :]
                    b = nextps()
                    for h in range(8):
                        P.mm(psb[b][0:C, h * 64:(h + 1) * 64], hsl(At, h), sbh(h), start=True, stop=False, reads=["At", "Sb"], writes=[f"ps{b}"], skip_group_check=True)
                        P.mm(psb[b][0:C, h * 64:(h + 1) * 64], AKm_[0:C, h * C:(h + 1) * C], Vtok[0:C, h * 64:(h + 1) * 64], start=False, stop=True, reads=[f"AKm{sx}", "tok2"], writes=[f"ps{b}"], skip_group_check=True)
                    P.op("act", lambda e, b=b: e.activation(out=Xb[0:C, :], in_=psb[b][0:C, 0:512], func=AF.Copy), writes=["Xb", f"ps{b}"])
                    b = nextps()
                    for h in range(8):
                        P.mm(psb[b][0:C, h * 64:(h + 1) * 64], Wf[0:C, h * C:(h + 1) * C], Xb[0:C, h * 64:(h + 1) * 64], reads=[wfres, "Xb"], writes=[f"ps{b}"])
                    P.op("act", lambda e, b=b: e.activation(out=Ub[0:C, :], in_=psb[b][0:C, 0:512], func=AF.Copy), writes=["Ub", f"ps{b}"])
                    b = nextps()
                    for h in range(8):
                        P.mm(psb[b][0:C, h * 64:(h + 1) * 64], hsl(Rt, h), sbh(h), start=True, stop=False, reads=["Rt", "Sb"], writes=[f"ps{b}"], skip_group_check=True)
                        P.mm(psb[b][0:C, h * 64:(h + 1) * 64], RBm_[0:C, h * C:(h + 1) * C], Ub[0:C, h * 64:(h + 1) * 64], start=False, stop=False, reads=[f"RBm{sx}", "Ub"], writes=[f"ps{b}"], skip_group_check=True)
                        P.mm(psb[b][0:C, h * 64:(h + 1) * 64], RKm_[0:C, h * C:(h + 1) * C], Vtok[0:C, h * 64:(h + 1) * 64], start=False, stop=True, reads=[f"RKm{sx}", "tok2"], writes=[f"ps{b}"], skip_group_check=True)
                    P.op("act", lambda e, b=b: e.activation(out=Of[0:C, :], in_=psb[b][0:C, 0:512], func=AF.Copy), writes=["lnt", f"ps{b}"])
                    b = nextps()
                    for h in range(8):
                        o_ap = psb[b][0:64, h * 64:(h + 1) * 64]
                        P.mm(o_ap, Btok[0:C, h * 64:(h + 1) * 64], Ub[0:C, h * 64:(h + 1) * 64], start=True, stop=False, reads=["tok0", "Ub"], writes=[f"ps{b}"], skip_group_check=True)
                        P.mm(o_ap, Ktok[0:C, h * 64:(h + 1) * 64], Vtok[0:C, h * 64:(h + 1) * 64], start=False, stop=True, reads=["tok1", "tok2"], writes=[f"ps{b}"], skip_group_check=True)
                    P.op("dve", lambda e, b=b: e.tensor_tensor(out=Stmp[:, :, :], in0=psb[b][0:64, 0:512].rearrange("p (h v) -> p h v", h=8), in1=S[:, :, :], op=ALU.add), reads=["S"], writes=["Stmp", f"ps{b}"])
                    for h in range(8):
                        P.op("dve" if h % 2 == 0 else "pool", lambda e, h=h: e.tensor_scalar(out=S[:, h, :], in0=Stmp[:, h, :], scalar1=PCg[:, h, ck:ck + 1], scalar2=None, op0=ALU.mult), reads=["Stmp", "PCg"], writes=["S"])
                    P.op("act", lambda e: e.activation(out=Sb[:, :, :], in_=S[:, :, :], func=AF.Copy), reads=["S"], writes=["Sb"])
                    if is_s or (gi == 7 and ck == n // C - 1):
                        b = nextps()
                        for h in range(8):
                            P.tr(psb[b][0:64, h * 64:(h + 1) * 64], S[:, h, :], identF[0:64, 0:64], reads=["S", "identF"], writes=[f"ps{b}"])
                        P.op("dve", lambda e, b=b: e.tensor_copy(sst[:, :, :].rearrange("v h k -> v (h k)"), psb[b][0:64, 0:512]), writes=["sst", f"ps{b}"])
                        dstw = wkv_s[e_, ck] if is_s else wkv_p[e_]
                        dma("sp", dstw.rearrange("h v k -> v h k"), sst[:, :, :], reads=["sst"])
                    Ov = Of[0:C, :].rearrange("t (h v) -> t h v", h=8)
                    P.op("dve", lambda e, Ov=Ov: e.reduce_sum(out=gst[0:C, 0:8], in_=Ov, axis=AX.X), reads=["lnt"], writes=["gst"])
                    P.op("pool", lambda e: e.tensor_tensor(out=Osq[0:C, :], in0=Of[0:C, :], in1=Of[0:C, :], op=ALU.mult), reads=["lnt"], writes=["lnt2"])
                    P.op("dve", lambda e: e.reduce_sum(out=gst[0:C, 8:16], in_=Osq[0:C, :].rearrange("t (h v) -> t h v", h=8), axis=AX.X), reads=["lnt2"], writes=["gst"])
                    P.op("dve", lambda e: e.tensor_scalar(out=gst[0:C, 16:24], in0=gst[0:C, 0:8], scalar1=1.0 / 64, scalar2=None, op0=ALU.mult), reads=["gst"], writes=["gst"])
                    P.op("dve", lambda e: e.tensor_tensor(out=gst[0:C, 24:32], in0=gst[0:C, 16:24], in1=gst[0:C, 16:24], op=ALU.mult), reads=["gst"], writes=["gst"])
                    P.op("dve", lambda e: e.scalar_tensor_tensor(out=gst[0:C, 32:40], in0=gst[0:C, 8:16], scalar=1.0 / 64, in1=gst[0:C, 24:32], op0=ALU.mult, op1=ALU.subtract), reads=["gst"], writes=["gst"])
                    P.op("dve", lambda e: e.tensor_scalar(out=gst[0:C, 32:40], in0=gst[0:C, 32:40], scalar1=64e-5, scalar2=None, op0=ALU.add), reads=["gst"], writes=["gst"])
                    P.op("act", lambda e: e.activation(out=gst[0:C, 40:48], in_=gst[0:C, 32:40], func=AF.Sqrt), reads=["gst"], writes=["gst"])
                    P.op("dve", lambda e: e.reciprocal(gst[0:C, 40:48], gst[0:C, 40:48]), reads=["gst"], writes=["gst"])
                    for h in range(8):
                        P.op("dve" if h % 2 == 0 else "pool", lambda e, h=h: e.tensor_scalar(out=Onb[0:C, h * 64:(h + 1) * 64], in0=Of[0:C, h * 64:(h + 1) * 64], scalar1=gst[0:C, 16 + h:17 + h], scalar2=gst[0:C, 40 + h:41 + h], op0=ALU.subtract, op1=ALU.mult), reads=["lnt", "gst"], writes=["Onb"])
                    b = nextps()
                    pbf = psb[b][:, :].bitcast(BF16)
                    for c in range(4):
                        P.tr(pbf[:, c * 64:c * 64 + C], Onb[0:C, c * 128:(c + 1) * 128], identB[0:C, 0:C], reads=["Onb", "identB"], writes=[f"ps{b}"])
                    t1v = t1[:, 0:4 * C].rearrange("p (c t) -> p c t", c=4)
                    t2v = t2[:, 0:4 * C].rearrange("p (c t) -> p c t", c=4)
                    for c in range(4):
                        P.op("act", lambda e, c=c, pbf=pbf: e.activation(out=t1v[:, c, :], in_=pbf[:, c * 64:c * 64 + C], func=AF.Identity, scale=evp[:, LG + c:LG + c + 1], bias=evp[:, LB + c:LB + c + 1]), reads=["evp"], writes=["t1c", f"ps{b}"])
                    P.op("dve", lambda e: e.tensor_tensor(out=t2v, in0=t1v, in1=bn[:, :, k0:k0 + C], op=ALU.add), reads=["t1c", "bn"], writes=["t2c"])
                    P.op("pool", lambda e: e.tensor_tensor(out=mixg[:, 4:8, k0:k0 + C], in0=t2v, in1=gT[:, :, k0:k0 + C], op=ALU.mult), reads=["t2c", "gT"], writes=["mixg"])
                nchunks = n // C
                for i0_ in range(0, nchunks, 2):
                    gens = [p1(i0_ + s_, s_) for s_ in range(min(2, nchunks - i0_))]
                    alive = list(gens)
                    while alive:
                        for g_ in list(alive):
                            try:
                                next(g_)
                            except StopIteration:
                                alive.remove(g_)
                    for s_ in range(min(2, nchunks - i0_)):
                        p2(i0_ + s_, s_)
                if EVSTOP <= 8:
                    return
                save_off = arena.off
                arena.off = off_ld
                ygrp = arena.alloc([128, 8, n], F32)
                arena.off = save_off
                for dc in range(8):
                    b = nextps()
                    for kc in range(8):
                        P.mm(psb[b][:, 0:n], wo_sb[:, kc, dc * 128:(dc + 1) * 128], mixg[:, kc, :], start=(kc == 0), stop=(kc == 7), reads=["wo_sb", "mixg"], writes=[f"ps{b}"])
                    P.op("act" if dc % 2 == 0 else "dve", (lambda e, b=b, dc=dc: e.activation(out=ygrp[:, dc, :], in_=psb[b][:, 0:n], func=AF.Copy)) if dc % 2 == 0 else (lambda e, b=b, dc=dc: e.tensor_copy(ygrp[:, dc, :], psb[b][:, 0:n])), writes=["ygrp", "ld", "lp", f"ps{b}"])
                for c in range(8):
                    P.op("act", lambda e, c=c: e.activation(out=sqg[:, c, :], in_=ygrp[:, c, :], func=AF.Square), reads=["ygrp"], writes=["sqg", "E1"])
                b = nextps()
                for c in range(8):
                    P.mm(psb[b][:, 0:n], onesB[:, :], sqg[:, c, :], start=(c == 0), stop=(c == 7), reads=["sqg", "onesB"], writes=[f"ps{b}"])
                P.op("act", lambda e, b=b: e.activation(out=rstd_g[:, :], in_=psb[b][:, 0:n], func=AF.Sqrt, scale=1.0 / D, bias=epsb[:, 0:1]), reads=["epsb"], writes=["rstd_g", f"ps{b}"])
                P.op("dve", lambda e: e.reciprocal(rstd_g[:, :], rstd_g[:, :]), reads=["rstd_g"], writes=["rstd_g"])
                for c in range(8):
                    for (o0, nn, sq_) in gsegs:
                        P.op("dve", lambda e, c=c, o0=o0, nn=nn, sq_=sq_: e.scalar_tensor_tensor(out=tmpg[:, o0:o0 + nn], in0=ygrp[:, c, o0:o0 + nn], scalar=mods[:, 2, c, sq_:sq_ + 1], in1=rstd_g[:, o0:o0 + nn], op0=ALU.mult, op1=ALU.mult), reads=["ygrp", "mods", "rstd_g"], writes=["tmpg"])
                        P.op("pool", lambda e, c=c, o0=o0, nn=nn: e.tensor_tensor(out=xT[:, c, c0 + o0:c0 + o0 + nn], in0=xT[:, c, c0 + o0:c0 + o0 + nn], in1=tmpg[:, o0:o0 + nn], op=ALU.add), reads=["tmpg", "xT"], writes=["xT"])
            for gi_, gg in enumerate(groups):
                if EVGROUPS and str(gi_) not in EVGROUPS.split(","):
                    continue
                if EVSTOP <= 1:
                    continue
                do_group(gi_, *gg)
            phase()

        for l in range(nlayers):
            pieces = [(ada_w[l].rearrange("(kc p) f -> p kc f", p=128)[:, :, pc * 512:(pc + 1) * 512], (8, 512)) for pc in range(12)]
            ws = WStream(pieces)
            b = nextps()
            for pc in range(12):
                wsl, wres = ws.get()
                for j in range(4):
                    jj = pc * 4 + j
                    for kc in range(8):
                        P.mm(psb[b][:, jj * NSEQ:(jj + 1) * NSEQ], wsl[:, kc, j * 128:(j + 1) * 128], siluT[:, kc, :], start=(kc == 0), stop=(kc == 7), reads=[wres, "siluT"], writes=[f"ps{b}"])
            psv = psb[b][:, 0:48 * NSEQ].rearrange("p (j s) -> p j s", s=NSEQ)
            for s in range(NSEQ):
                P.op("dve", lambda e, s=s, psv=psv, l=l: e.tensor_tensor(out=mod[:, :, s], in0=psv[:, :, s], in1=adab[:, l * 48:(l + 1) * 48], op=ALU.add), reads=["adab"], writes=["mod", f"ps{b}"])
            for half, (kpre, kpost) in enumerate(((0, 1), (2, 3))):
                base = half * 24
                for s in range(NSEQ):
                    P.op("dve", lambda e, s=s, base=base, half=half, kpre=kpre, l=l: e.scalar_tensor_tensor(out=mods[:, half * 3 + 0, :, s], in0=mod[:, base + 8:base + 16, s], scalar=1.0, in1=npar[:, kpre, l * 8:(l + 1) * 8], op0=ALU.add, op1=ALU.mult), reads=["mod", "npar"], writes=["mods"])
                    P.op("dve", lambda e, s=s, base=base, half=half: e.tensor_copy(mods[:, half * 3 + 1, :, s], mod[:, base:base + 8, s]), reads=["mod"], writes=["mods"])
                    P.op("dve", lambda e, s=s, base=base, half=half, kpost=kpost, l=l: e.tensor_tensor(out=mods[:, half * 3 + 2, :, s], in0=mod[:, base + 16:base + 24, s], in1=npar[:, kpost, l * 8:(l + 1) * 8], op=ALU.mult), reads=["mod", "npar"], writes=["mods"])

            if l % 2 == 1 and mix_odd:
                odd_mixer(l // 2)
            if l % 2 == 0 and mix_even:
                even_mixer(l // 2)

            barrier()
            arena.reset()
            norm_stats(xT, "xT", TGROUPS)
            modulate(3, 4)
            halves = [[(0, 512), (512, 512)], [(1024, 512), (1536, 512), (2048, 32)]]
            for hi, groups in enumerate(halves):
                barrier()
                arena.reset()
                hc0 = groups[0][0]
                hn = sum(g[1] for g in groups)
                yacc = arena.alloc([128, 8, 1056], F32)
                uT = arena.alloc([128, 4, 512], BF16)
                rl2 = arena.alloc([128, 2, 512], BF16)
                rlc = [0]
                pieces = []
                for fg in range(8):
                    pieces.append((w1[l].rearrange("(kc p) f -> p kc f", p=128)[:, :, fg * 512:(fg + 1) * 512], (8, 512)))
                    pieces.append((w2[l][fg * 512:(fg + 1) * 512, :].rearrange("(j p) d -> p j d", p=128), (4, 1024)))
                ws = WStream(pieces)
                for fg in range(8):
                    w1g, w1res = ws.get()
                    w2g, w2res = ws.get()
                    for (c0, n) in groups:
                        for j in range(4):
                            b = nextps()
                            for kc in range(8):
                                P.mm(psb[b][:, 0:n], w1g[:, kc, j * 128:(j + 1) * 128], hT[:, kc, c0:c0 + n], start=(kc == 0), stop=(kc == 7), reads=[w1res, "hT"], writes=[f"ps{b}"])
                            ri = rlc[0] % 2
                            rlc[0] += 1
                            rl = rl2[:, ri, :]
                            P.op("act", lambda e, b=b, n=n, rl=rl: e.activation(out=rl[:, 0:n], in_=psb[b][:, 0:n], func=AF.Relu), writes=[f"rl{ri}", f"ps{b}"])
                            P.op("pool", lambda e, j=j, n=n, rl=rl: e.tensor_tensor(out=uT[:, j, 0:n], in0=rl[:, 0:n], in1=rl[:, 0:n], op=ALU.mult), reads=[f"rl{ri}"], writes=[f"uT{j}"])
                        for dc in range(8):
                            b = nextps()
                            for j in range(4):
                                P.mm(psb[b][:, 0:n], w2g[:, j, dc * 128:(dc + 1) * 128], uT[:, j, 0:n], start=(j == 0), stop=(j == 3), reads=[w2res, f"uT{j}"], writes=[f"ps{b}"])
                            ydst = yacc[:, dc, c0 - hc0:c0 - hc0 + n]
                            if fg == 0:
                                P.op("act", lambda e, b=b, n=n, ydst=ydst: e.activation(out=ydst, in_=psb[b][:, 0:n], func=AF.Copy), writes=["yacc", f"ps{b}"])
                            else:
                                P.op("dve", lambda e, b=b, n=n, ydst=ydst: e.tensor_tensor(out=ydst, in0=psb[b][:, 0:n], in1=ydst, op=ALU.add), reads=["yacc"], writes=["yacc", f"ps{b}"])
                norm_stats(yacc, "yacc", groups, col_off=hc0)
                segs = [sg for sg in SEGS if sg[0] >= hc0 and sg[0] < hc0 + hn]
                if hi == 0:
                    segs = [(0, 1024, 0)]
                else:
                    segs = [(1024, 1024, 0)] + SEGS[1:]
                residual(yacc, "yacc", 5, segs, col_off=hc0)

        barrier()
        arena.reset()

        def store_y(dst, nrows, col0):
            ntile = (nrows + 127) // 128
            for tt in range(ntile):
                r = min(128, nrows - tt * 128)
                if yo_keep[0] is None:
                    yo_keep[0] = arena.alloc([128, 2, D], F32)
                yo = yo_keep[0][:, tt % 2, :]
                yres = f"yo{tt % 2}"
                for hb in range(2):
                    b = nextps()
                    for q in range(4):
                        c = hb * 4 + q
                        P.tr(psb[b][0:r, q * 128:(q + 1) * 128], xT[:, c, col0 + tt * 128:col0 + tt * 128 + r], identF[:, :], reads=["xT", "identF"], writes=[f"ps{b}"])
                    P.op("dve" if hb == 0 else "act",
                         (lambda e, b=b, r=r, hb=hb, yo=yo: e.tensor_copy(yo[0:r, hb * 512:(hb + 1) * 512], psb[b][0:r, :])) if hb == 0 else
                         (lambda e, b=b, r=r, hb=hb, yo=yo: e.activation(out=yo[0:r, hb * 512:(hb + 1) * 512], in_=psb[b][0:r, :], func=AF.Copy)),
                         writes=[yres, f"ps{b}"])
                dma("sp", dst[tt * 128:tt * 128 + r, :], yo[0:r, :], reads=[yres])
        yo_keep = [None]
        store_y(y_p, T, 0)
        store_y(y_s, TS, T)
        with nc.allow_low_precision("bf16 matmul operands by design"):
            P.emit()
    return nc


def shard_inputs(inputs, b):
    f = lambda a: np.ascontiguousarray(a, dtype=np.float32)
    m = {
        "xp": f(inputs["x_prompt"][b]), "xs": f(inputs["x_sample"][4 * b:4 * b + 4].reshape(TS, D)),
        "swkv": f(inputs["state_wkv"][:, 4 * b:4 * b + 4]), "sshift": f(inputs["state_shift"][:, 4 * b:4 * b + 4]),
        "ck": f(inputs["cache_k"][:, 4 * b:4 * b + 4].reshape(2, 4, 2048, D)), "cv": f(inputs["cache_v"][:, 4 * b:4 * b + 4].reshape(2, 4, 2048, D)),
        "cc": f(np.concatenate([inputs["c_prompt"][b:b + 1], inputs["c_sample"][4 * b:4 * b + 4]], axis=0)),
    }
    for n in ("ada_w", "ada_b", "norm_mix_pre", "norm_mix_post", "norm_ffn_pre", "norm_ffn_post", "ffn_w1", "ffn_w2", "ev_w_in", "ev_w_out",
              "gm_ln_g", "gm_ln_b", "gm_ws", "gm_bs", "rw_mu", "rw_w0", "rw_w2", "rw_a0", "rw_a2", "rw_g2", "rw_kk", "rw_ka", "rw_lnx_g", "rw_lnx_b",
              "od_w_qkv", "od_w_out"):
        m[n] = f(inputs[n])
    m["rw_rk"] = f(inputs["rw_rk"].reshape(2, 512))
    m.update(host_consts())
    return m


def host_consts():
    import numpy as np
    f32 = np.float32
    half = 8
    inv = (np.float32(500000.0) ** (-np.arange(half, dtype=np.float32) * np.float32(2.0 / 16))).astype(f32)
    pos = np.concatenate([np.arange(T), np.tile(8192 + np.arange(8), 4)]).astype(f32)
    ang = pos[:, None] * inv[None, :]
    cos = np.cos(ang).astype(f32); sin = np.sin(ang).astype(f32)
    rope = np.zeros((2, 128, TT), f32)
    perm = np.zeros((128, 128), f32)
    for p in range(128):
        e = p % 64
        if e < 8:
            rope[0, p] = cos[:, e]; rope[1, p] = -sin[:, e]; perm[p + 8, p] = 1.0
        elif e < 16:
            rope[0, p] = cos[:, e - 8]; rope[1, p] = sin[:, e - 8]; perm[p - 8, p] = 1.0
        else:
            rope[0, p] = 1.0
    jj = np.arange(128)[:, None]; ii = np.arange(128)[None, :]
    cur = (jj <= ii).astype(f32); prev = (jj >= ii).astype(f32)
    cmask = np.concatenate([cur, prev, cur], axis=1)
    m3 = np.zeros((128, 4, 16, 32), f32)
    for G in range(4):
        m3[:, G, :, :] = (np.arange(128)[:, None, None] <= (32 * G + np.arange(32))[None, None, :])
    m3 = m3.reshape(128, 2048)
    c = (np.arange(16)[None, :, None] * 128 + np.arange(128)[:, None, None])
    t = np.arange(8)[None, None, :]
    dd = 2048 + t - c
    sm = ((dd <= 128).astype(f32) + ((dd % 4 == 0) & (dd <= 512)).astype(f32) + ((dd % 16 == 0) & (dd <= 2048)).astype(f32))
    smask = np.concatenate([sm.reshape(128, 128), sm.reshape(128, 128)], axis=1)
    tp = np.arange(8)[:, None]; tq = np.arange(8)[None, :]
    nm = (tp <= tq).astype(f32) + 2.0 * (tp == tq) + (tp == tq - 4)
    nmask = np.concatenate([nm, nm], axis=1).astype(f32)
    t_ = np.arange(64)
    su = (t_[:, None] < t_[None, :]).astype(f32); ui = (t_[:, None] <= t_[None, :]).astype(f32); sl = (t_[:, None] > t_[None, :]).astype(f32)
    ev = np.zeros((128, 1664), f32)
    ev[0:CP, 0:8 * CP] = np.tile(su[:CP, :CP], (1, 8)); ev[0:CP, 512:512 + 8 * CP] = np.tile(ui[:CP, :CP], (1, 8)); ev[0:CP, 1024:1024 + 8 * CP] = np.tile(sl[:CP, :CP], (1, 8))
    blk = np.zeros((128, 128), f32); blk[0:64, 0:64] = 1; blk[64:, 64:] = 1
    ev[:, 1536:1664] = blk
    idr = np.zeros((64, 512), f32); idr[0:CP, 0:8 * CP] = np.tile(np.eye(CP, dtype=f32), (1, 8))
    m8 = np.concatenate([np.tile(su[:8, :8], (1, 8)), np.tile(ui[:8, :8], (1, 8)), np.tile(sl[:8, :8], (1, 8)), np.tile(np.eye(8, dtype=f32), (1, 8))], axis=1)
    return {"kc_ev": ev, "kc_idr": idr, "kc_m8": m8.astype(f32), "kc_rope": rope, "kc_perm": perm, "kc_cmask": cmask.astype(f32), "kc_m3": m3, "kc_smask": smask.astype(f32), "kc_nmask": nmask}


_NC_CACHE = {}


def kernel(**inputs):
    n = 8
    if "nc" not in _NC_CACHE:
        _NC_CACHE["nc"] = build(nlayers=4, mix_even=True, mix_odd=True, dbg=False)
    nc = _NC_CACHE["nc"]
    inputs = {k: np.asarray(v) for k, v in inputs.items()}
    in_maps = [shard_inputs(inputs, b) for b in range(n)]
    res = run_bass_kernel_spmd(nc, in_maps, core_ids=list(range(n)))
    R = res.results
    f = np.float32
    y_prompt = np.stack([R[b]["y_p"] for b in range(n)], axis=0).astype(f)
    y_sample = np.concatenate([R[b]["y_s"].reshape(4, 8, D) for b in range(n)], axis=0).astype(f)
    wkv_prompt = np.stack([R[b]["wkv_p"] for b in range(n)], axis=1).astype(f)
    shift_prompt = np.stack([R[b]["shift_p"] for b in range(n)], axis=1).astype(f)
    k_prompt = np.stack([R[b]["k_p"].reshape(2, T, 16, 64) for b in range(n)], axis=1).astype(f)
    v_prompt = np.stack([R[b]["v_p"].reshape(2, T, 16, 64) for b in range(n)], axis=1).astype(f)
    wkv_sample = np.concatenate([R[b]["wkv_s"] for b in range(n)], axis=1).astype(f)
    shift_sample = np.concatenate([R[b]["shift_s"] for b in range(n)], axis=1).astype(f)
    k_sample = np.concatenate([R[b]["k_s"].reshape(2, 4, 8, 16, 64) for b in range(n)], axis=1).astype(f)
    v_sample = np.concatenate([R[b]["v_s"].reshape(2, 4, 8, 16, 64) for b in range(n)], axis=1).astype(f)
    gmlp_v_sample = np.concatenate([R[b]["gv_s"].reshape(2, 4, 8, 512) for b in range(n)], axis=1).astype(f)
    return (y_prompt, y_sample, wkv_prompt, shift_prompt, k_prompt, v_prompt, wkv_sample, shift_sample, k_sample, v_sample, gmlp_v_sample)
```
